# Optimizing a Trainium2 kernel written in Bass

```python
import math
import jax, jax.numpy as jnp
from jax import lax
import numpy as np

D_MODEL = 1024
BATCH = 16
SEQ = 2048
DEPTH = 1

A_HEADS = 8
A_HEAD_DIM = 64
IDX_HEADS = 8
IDX_DIM = 64
TOPK_MAX = 256
Q_BLOCK = 128
ROPE_THETA = 10000.0
B_HEADS = 4
B_KEY_DIM = 128
B_VAL_DIM = 128
CONV_WIDTH = 4
CHUNK = 64
D_FF = 2816
MACARON_WEIGHT = 0.5
DEEPNORM_ALPHA = (2.0 * DEPTH) ** 0.25
DEEPNORM_BETA = (8.0 * DEPTH) ** -0.25
LN_EPS = 1e-5
RMS_EPS = 1e-6
N_MOD = 9

MIX_WIDTH = A_HEADS * A_HEAD_DIM + B_HEADS * B_VAL_DIM
IN_SPLITS = (A_HEADS * A_HEAD_DIM, A_HEAD_DIM, A_HEAD_DIM,
             IDX_HEADS * IDX_DIM, IDX_DIM, IDX_HEADS,
             B_HEADS * B_KEY_DIM, B_HEADS * B_KEY_DIM, B_HEADS * B_VAL_DIM,
             B_HEADS * B_VAL_DIM, B_HEADS, B_HEADS)
IN_WIDTH = sum(IN_SPLITS)
CONV_CH = 2 * B_HEADS * B_KEY_DIM + B_HEADS * B_VAL_DIM

kernel_name = "hymba_dsa_gdn_macaron_deepnorm_adaln"


def layer_norm(x, g, b):
    xf = x.astype(jnp.float32)
    mu = jnp.mean(xf, axis=-1, keepdims=True)
    var = jnp.mean(jnp.square(xf - mu), axis=-1, keepdims=True)
    return ((xf - mu) * lax.rsqrt(var + LN_EPS) * g + b).astype(x.dtype)


def rms_norm(x, g):
    xf = x.astype(jnp.float32)
    return xf * lax.rsqrt(jnp.mean(jnp.square(xf), axis=-1, keepdims=True) + RMS_EPS) * g


def l2norm(t):
    tf = t.astype(jnp.float32)
    return tf * lax.rsqrt(jnp.sum(jnp.square(tf), axis=-1, keepdims=True) + RMS_EPS)


def modulate(x, shift, scale):
    return x * (1.0 + scale) + shift


def swiglu_ffn(u, w1, w3, w2):
    return (jax.nn.silu(u @ w1) * (u @ w3)) @ w2


def rope(t, positions):
    d = t.shape[-1]
    inv_freq = ROPE_THETA ** (-jnp.arange(0, d, 2, dtype=jnp.float32) / d)
    ang = positions.astype(jnp.float32)[..., None] * inv_freq
    cos = jnp.cos(ang)[:, :, None, :]
    sin = jnp.sin(ang)[:, :, None, :]
    tf = t.astype(jnp.float32)
    t1, t2 = tf[..., : d // 2], tf[..., d // 2:]
    return jnp.concatenate([t1 * cos - t2 * sin, t2 * cos + t1 * sin], axis=-1).astype(t.dtype)


def dsa_sparse_attention(q, k, v, q_idx, k_idx, w_idx):
    bsz, seq = q.shape[0], q.shape[1]
    n_sel = min(TOPK_MAX, seq // 4)
    nblk = seq // Q_BLOCK
    key_pos = jnp.arange(seq)
    b_idx = jnp.arange(bsz)[:, None, None]

    def to_blocks(t):
        return jnp.moveaxis(t.reshape((bsz, nblk, Q_BLOCK) + t.shape[2:]), 1, 0)

    def block(args):
        blk, qb, qib, wib = args
        q_pos = blk * Q_BLOCK + jnp.arange(Q_BLOCK)
        causal = key_pos[None, :] <= q_pos[:, None]
        rel = jax.nn.relu(jnp.einsum('bqhd,bsd->bqhs', qib, k_idx))
        score = jnp.einsum('bqh,bqhs->bqs', wib, rel).astype(jnp.float32)
        score = jnp.where(causal[None], score, -jnp.inf)
        _, sel = lax.top_k(score, n_sel)
        valid = sel <= q_pos[None, :, None]
        k_sel = k[b_idx, sel]
        v_sel = v[b_idx, sel]
        logits = jnp.einsum('bqhd,bqnd->bqhn', qb, k_sel).astype(jnp.float32) * (A_HEAD_DIM ** -0.5)
        logits = jnp.where(valid[:, :, None, :], logits, -jnp.inf)
        p = jax.nn.softmax(logits, axis=-1).astype(v.dtype)
        return jnp.einsum('bqhn,bqnd->bqhd', p, v_sel)

    out = lax.map(block, (jnp.arange(nblk), to_blocks(q), to_blocks(q_idx), to_blocks(w_idx)))
    return jnp.moveaxis(out, 0, 1).reshape(bsz, seq, A_HEADS * A_HEAD_DIM)


def causal_short_conv(u, w):
    n_ch = u.shape[-1]
    return lax.conv_general_dilated(u, w[:, None, :].astype(u.dtype), window_strides=(1,),
                                    padding=[(CONV_WIDTH - 1, 0)],
                                    dimension_numbers=('NWC', 'WIO', 'NWC'),
                                    feature_group_count=n_ch)


def gated_delta_rule_chunked(q, k, v, log_a, beta):
    bsz, seq, nh, dk = q.shape
    dv = v.shape[-1]
    nc = seq // CHUNK

    def chunks(t):
        t = t.astype(jnp.float32).reshape((bsz, nc, CHUNK, nh) + t.shape[3:])
        return jnp.moveaxis(t, 3, 1)

    q, k, v, log_a, beta = (chunks(t) for t in (q, k, v, log_a, beta))
    q = q * (dk ** -0.5)
    gam = jnp.cumsum(log_a, axis=-1)
    diff = gam[..., :, None] - gam[..., None, :]
    strict = jnp.tril(jnp.ones((CHUNK, CHUNK), dtype=bool), -1)
    incl = jnp.tril(jnp.ones((CHUNK, CHUNK), dtype=bool))
    dec_strict = jnp.where(strict, jnp.exp(jnp.where(strict, diff, 0.0)), 0.0)
    dec_incl = jnp.where(incl, jnp.exp(jnp.where(incl, diff, 0.0)), 0.0)
    g_cum = jnp.exp(gam)
    lower = beta[..., :, None] * jnp.einsum('bhnid,bhnjd->bhnij', k, k) * dec_strict
    t_mat = lower + jnp.eye(CHUNK, dtype=jnp.float32)
    w = lax.linalg.triangular_solve(t_mat, (beta * g_cum)[..., None] * k,
                                    left_side=True, lower=True, unit_diagonal=True)
    u = lax.linalg.triangular_solve(t_mat, beta[..., None] * v,
                                    left_side=True, lower=True, unit_diagonal=True)
    a_qk = jnp.einsum('bhnid,bhnjd->bhnij', q, k) * dec_incl
    q_dec = q * g_cum[..., None]
    k_dec = k * jnp.exp(gam[..., -1:] - gam)[..., None]
    g_last = g_cum[..., -1]

    def step(state, xs):
        w_c, u_c, aqk_c, qd_c, kd_c, gl_c = xs
        delta = u_c - w_c @ state
        out = qd_c @ state + aqk_c @ delta
        state = gl_c[..., None, None] * state + jnp.swapaxes(kd_c, -1, -2) @ delta
        return state, out

    xs = tuple(jnp.moveaxis(t, 2, 0) for t in (w, u, a_qk, q_dec, k_dec, g_last))
    state0 = jnp.zeros((bsz, nh, dk, dv), jnp.float32)
    _, out = lax.scan(step, state0, xs)
    out = jnp.moveaxis(out, 0, 2)
    return jnp.moveaxis(out, 1, 3).reshape(bsz, seq, nh, dv)


def hybrid_mixer(u, positions, w_in, conv_w, a_log, dt_bias, dn_norm_g, w_out):
    bsz, seq, _ = u.shape
    split_points = tuple(int(s) for s in np.cumsum(IN_SPLITS)[:-1])
    proj = u @ w_in
    a_q, a_k, a_v, i_q, i_k, i_w, b_q, b_k, b_v, b_z, b_a, b_b = jnp.split(proj, split_points, axis=-1)

    q = rope(a_q.reshape(bsz, seq, A_HEADS, A_HEAD_DIM), positions)
    k = rope(a_k[:, :, None, :], positions)[:, :, 0]
    qi = rope(i_q.reshape(bsz, seq, IDX_HEADS, IDX_DIM), positions)
    ki = rope(i_k[:, :, None, :], positions)[:, :, 0]
    wi = i_w * (IDX_HEADS ** -0.5 * IDX_DIM ** -0.5)
    attn_out = dsa_sparse_attention(q, k, a_v, qi, ki, wi)

    qkv = jax.nn.silu(causal_short_conv(jnp.concatenate([b_q, b_k, b_v], axis=-1), conv_w))
    dq, dk_, dv_ = jnp.split(qkv, (B_HEADS * B_KEY_DIM, 2 * B_HEADS * B_KEY_DIM), axis=-1)
    dq = l2norm(dq.reshape(bsz, seq, B_HEADS, B_KEY_DIM))
    dk_ = l2norm(dk_.reshape(bsz, seq, B_HEADS, B_KEY_DIM))
    dv_ = dv_.reshape(bsz, seq, B_HEADS, B_VAL_DIM)
    log_a = -jnp.exp(a_log.astype(jnp.float32)) * jax.nn.softplus((b_a + dt_bias).astype(jnp.float32))
    beta = jax.nn.sigmoid(b_b.astype(jnp.float32))
    dn = gated_delta_rule_chunked(dq, dk_, dv_, log_a, beta)
    gate = jax.nn.silu(b_z.reshape(bsz, seq, B_HEADS, B_VAL_DIM).astype(jnp.float32))
    dn_out = (rms_norm(dn, dn_norm_g) * gate).astype(u.dtype).reshape(bsz, seq, B_HEADS * B_VAL_DIM)

    return jnp.concatenate([attn_out, dn_out], axis=-1) @ w_out


def setup_inputs(seed: int = 0) -> dict:
    key = jax.random.key(seed)
    ks = jax.random.split(key, 24)
    D = D_MODEL

    def nrm(k, shape, scale):
        return jax.random.normal(k, shape, jnp.float32) * scale

    x = nrm(ks[0], (BATCH, SEQ, D), 1.0)
    c = nrm(ks[1], (BATCH, D), 1.0)
    positions = (jnp.arange(SEQ, dtype=jnp.int32)[None, :]
                 + jax.random.randint(ks[2], (BATCH, 1), 0, 64, dtype=jnp.int32))
    w_ada = nrm(ks[3], (DEPTH, D, N_MOD * D), 0.5 * D ** -0.5)
    b_ada = nrm(ks[4], (DEPTH, N_MOD * D), 0.02)
    ffn1_w1 = nrm(ks[5], (DEPTH, D, D_FF), D ** -0.5)
    ffn1_w3 = nrm(ks[6], (DEPTH, D, D_FF), D ** -0.5)
    ffn1_w2 = nrm(ks[7], (DEPTH, D_FF, D), DEEPNORM_BETA * D_FF ** -0.5)
    ln1_g = 1.0 + nrm(ks[8], (DEPTH, D), 0.02)
    ln1_b = nrm(ks[9], (DEPTH, D), 0.02)
    w_in = nrm(ks[10], (DEPTH, D, IN_WIDTH), D ** -0.5)
    conv_w = nrm(ks[11], (DEPTH, CONV_WIDTH, CONV_CH), CONV_WIDTH ** -0.5)
    a_log = jnp.log(jax.random.uniform(ks[12], (DEPTH, B_HEADS), jnp.float32, 1.0, 16.0))
    dt = jnp.exp(jax.random.uniform(ks[13], (DEPTH, B_HEADS), jnp.float32,
                                    math.log(1e-3), math.log(1e-1)))
    dt_bias = dt + jnp.log(-jnp.expm1(-dt))
    dn_norm_g = 1.0 + nrm(ks[14], (DEPTH, B_VAL_DIM), 0.02)
    w_out = nrm(ks[15], (DEPTH, MIX_WIDTH, D), DEEPNORM_BETA * MIX_WIDTH ** -0.5)
    ln2_g = 1.0 + nrm(ks[16], (DEPTH, D), 0.02)
    ln2_b = nrm(ks[17], (DEPTH, D), 0.02)
    ffn2_w1 = nrm(ks[18], (DEPTH, D, D_FF), D ** -0.5)
    ffn2_w3 = nrm(ks[19], (DEPTH, D, D_FF), D ** -0.5)
    ffn2_w2 = nrm(ks[20], (DEPTH, D_FF, D), DEEPNORM_BETA * D_FF ** -0.5)
    ln3_g = 1.0 + nrm(ks[21], (DEPTH, D), 0.02)
    ln3_b = nrm(ks[22], (DEPTH, D), 0.02)
    return {'x': x, 'c': c, 'positions': positions, 'w_ada': w_ada, 'b_ada': b_ada,
            'ffn1_w1': ffn1_w1, 'ffn1_w3': ffn1_w3, 'ffn1_w2': ffn1_w2, 'ln1_g': ln1_g, 'ln1_b': ln1_b,
            'w_in': w_in, 'conv_w': conv_w, 'a_log': a_log, 'dt_bias': dt_bias, 'dn_norm_g': dn_norm_g,
            'w_out': w_out, 'ln2_g': ln2_g, 'ln2_b': ln2_b,
            'ffn2_w1': ffn2_w1, 'ffn2_w3': ffn2_w3, 'ffn2_w2': ffn2_w2, 'ln3_g': ln3_g, 'ln3_b': ln3_b}


def reference(x, c, positions, w_ada, b_ada, ffn1_w1, ffn1_w3, ffn1_w2, ln1_g, ln1_b,
              w_in, conv_w, a_log, dt_bias, dn_norm_g, w_out, ln2_g, ln2_b,
              ffn2_w1, ffn2_w3, ffn2_w2, ln3_g, ln3_b):
    for layer in range(DEPTH):
        mod = (jax.nn.silu(c) @ w_ada[layer] + b_ada[layer])[:, None, :]
        sh1, sc1, g1, sh2, sc2, g2, sh3, sc3, g3 = jnp.split(mod, N_MOD, axis=-1)
        h = swiglu_ffn(modulate(x, sh1, sc1), ffn1_w1[layer], ffn1_w3[layer], ffn1_w2[layer])
        x = layer_norm(DEEPNORM_ALPHA * x + MACARON_WEIGHT * g1 * h, ln1_g[layer], ln1_b[layer])
        h = hybrid_mixer(modulate(x, sh2, sc2), positions, w_in[layer], conv_w[layer], a_log[layer],
                         dt_bias[layer], dn_norm_g[layer], w_out[layer])
        x = layer_norm(DEEPNORM_ALPHA * x + g2 * h, ln2_g[layer], ln2_b[layer])
        h = swiglu_ffn(modulate(x, sh3, sc3), ffn2_w1[layer], ffn2_w3[layer], ffn2_w2[layer])
        x = layer_norm(DEEPNORM_ALPHA * x + MACARON_WEIGHT * g3 * h, ln3_g[layer], ln3_b[layer])
    return x
```

```python
import numpy as np
from contextlib import ExitStack
import concourse.bass as bass
import concourse.mybir as mybir
from concourse.bass_utils import run_bass_kernel_spmd

F32 = mybir.dt.float32
BF16 = mybir.dt.bfloat16
I32 = mybir.dt.int32
AF = mybir.ActivationFunctionType
ALU = mybir.AluOpType
AX = mybir.AxisListType

D = 1024
SEQ = 2048
NT = 16
DFF = 2816
NCH = 22
GROUPS = [4, 4, 4, 5, 5]
ALPHA = 2.0 ** 0.25
NCOL = 3280
NTM = 1744
NDS = 16
BIG = 1.0e5
NEGFILL = -1.0e30
MASKV = -30000.0
NOINTER = False
STAGE = 99


class Bf:
    __slots__ = ("w", "r", "x")

    def __init__(self, x=False):
        self.w = None
        self.r = {}
        self.x = x


def Bs(n):
    return [Bf() for _ in range(n)]


class K:
    def __init__(self, nc, es):
        self.nc = nc
        self.eng = {"pe": nc.tensor, "dve": nc.vector, "act": nc.scalar, "pool": nc.gpsimd, "sp": nc.sync}
        self.sem = {e: es.enter_context(nc.semaphore("s_" + e)) for e in self.eng}
        self.cnt = {e: 0 for e in self.eng}
        self.seen = {e: {} for e in self.eng}
        self.dsem = [es.enter_context(nc.semaphore("dq%d" % i)) for i in range(NDS)]
        self.dval = [0] * NDS
        self.dnext = 0

    def _deps(self, e, r, w):
        deps = {}

        def add(ev):
            if ev is None:
                return
            key, sem, val = ev
            if key == "pe" and e == "pe":
                return
            if key not in deps or deps[key][1] < val:
                deps[key] = (sem, val)

        for b in r:
            add(b.w)
            if b.x:
                for ek, ev in b.r.items():
                    if ek != e:
                        add(ev)
        for b in w:
            add(b.w)
            for ev in b.r.values():
                add(ev)
        for key, (sem, val) in deps.items():
            if self.seen[e].get(key, 0) < val:
                self.eng[e].wait_ge(sem, val)
                self.seen[e][key] = val

    def _mark(self, e, ev, r, w):
        for b in r:
            b.r[ev[0]] = ev
        for b in w:
            b.w = ev
            b.r = {}

    def op(self, e, fn, r=(), w=()):
        self._deps(e, r, w)
        inst = fn(self.eng[e])
        self.cnt[e] += 1
        inst.then_inc(self.sem[e], 1)
        ev = (e, self.sem[e], self.cnt[e])
        self._mark(e, ev, r, w)
        return ev

    def dma(self, out, in_, r=(), w=(), e="sp"):
        i = self.dnext
        self.dnext = (i + 1) % NDS
        self._deps(e, r, w)
        key = ("d", i)
        if self.dval[i] > self.seen[e].get(key, 0):
            self.eng[e].wait_ge(self.dsem[i], self.dval[i])
            self.seen[e][key] = self.dval[i]
        inst = self.eng[e].dma_start(out=out, in_=in_)
        self.dval[i] += 16
        inst.then_inc(self.dsem[i], 16)
        ev = (key, self.dsem[i], self.dval[i])
        self._mark(e, ev, r, w)
        return ev

    def barrier(self):
        for e in self.eng:
            for o in self.eng:
                if o != e and self.cnt[o] > self.seen[e].get(o, 0):
                    self.eng[e].wait_ge(self.sem[o], self.cnt[o])
                    self.seen[e][o] = self.cnt[o]
            for i in range(NDS):
                key = ("d", i)
                if self.dval[i] > self.seen[e].get(key, 0):
                    self.eng[e].wait_ge(self.dsem[i], self.dval[i])
                    self.seen[e][key] = self.dval[i]


C_ID, C_ONES, C_TRI, C_OBD, C_SEL0, C_SEL1, C_MB1, C_MB2, C_INVF, C_I8, C_POW, C_END = (
    0, 128, 256, 384, 512, 640, 768, 896, 1024, 1056, 1568, 1592)
NBIS = 22


def make_consts():
    c = np.zeros((128, C_END), np.float32)
    idx = np.arange(128)
    same = (idx[:, None] // 64) == (idx[None, :] // 64)
    c[:, C_ID:C_ID + 128] = np.eye(128)
    c[:, C_ONES:C_ONES + 128] = 1.0
    c[:, C_TRI:C_TRI + 128] = (same & (idx[:, None] <= idx[None, :]))
    c[:, C_OBD:C_OBD + 128] = same
    c[:, C_SEL0:C_SEL0 + 128] = (idx[:, None] < 64)
    c[:, C_SEL1:C_SEL1 + 128] = (idx[:, None] >= 64)
    c[:, C_MB1:C_MB1 + 128] = np.where(same & (idx[:, None] > idx[None, :]), 0.0, -BIG)
    c[:, C_MB2:C_MB2 + 128] = np.where(same & (idx[None, :] >= idx[:, None]), 0.0, -BIG)
    inv = (10000.0 ** (-np.arange(0, 64, 2, dtype=np.float32) / np.float32(64))).astype(np.float32)
    c[:, C_INVF:C_INVF + 32] = inv[None, :]
    c[:, C_I8:C_I8 + 512] = np.tile(np.eye(128, dtype=np.float32), (1, 4))
    c[:, C_POW:C_POW + 24] = (0.5 ** np.arange(1, 25, dtype=np.float64))[None, :]
    return c


def build(stage=99):
    nc = bass.Bass("TRN2", target_bir_lowering=False)
    dt = lambda n, s, d=F32, kind="ExternalInput": nc.dram_tensor(n, s, d, kind=kind).ap()
    x_d = dt("x", [2, SEQ, D])
    cT_d = dt("cT", [128, 8, 2])
    pos_d = dt("pos", [128, 2, NT], I32)
    wada_d = dt("wada", [72, 128, 8, 128])
    bada_d = dt("badaT", [128, 72])
    fw = []
    for i in (1, 2):
        fw.append((dt("w1_%d" % i, [NCH, 128, 1024]), dt("w3_%d" % i, [NCH, 128, 1024]), dt("w2_%d" % i, [NCH, 128, 1024])))
    win_d = dt("win", [128, 8, NCOL])
    wout_d = dt("wout", [128, 8, D])
    lnp_d = dt("lnp", [6, D])
    conv_d = dt("convT", [128, 12, 4])
    gdnp_d = dt("gdnp", [8])
    dng_d = dt("dng", [128])
    const_d = dt("consts", [128, C_END])
    out_d = dt("out", [2, SEQ, D], kind="ExternalOutput")
    xs_d = dt("xs", [SEQ, D], kind="Internal")

    with ExitStack() as es:
        k = K(nc, es)
        uid = [0]

        def _alloc(stack, n, s, d):
            uid[0] += 1
            return stack.enter_context(nc.sbuf_tensor("%s_%d" % (n, uid[0]), s, d))
        sb = lambda n, s, d=F32: _alloc(es, n, s, d)
        ps = es.enter_context(nc.psum_tensor("ps", [128, 8, 512], F32))
        cst = sb("cst", [128, C_END])
        cstB = Bf()
        idb = sb("idb", [128, 128], BF16)
        i8b = sb("i8b", [128, 512], BF16)
        modT = sb("modT", [128, 72, 2])
        modB = Bf()
        xsB = Bs(NT)
        outB = Bs(2 * NT)
        psB = [Bf(True) for _ in range(8)]
        ident = cst[:, C_ID:C_ID + 128]

        k.dma(cst[:], const_d, w=[cstB])
        k.op("pool", lambda e: e.tensor_copy(out=idb[:], in_=cst[:, C_ID:C_ID + 128]), r=[cstB], w=[cstB])
        k.op("pool", lambda e: e.tensor_copy(out=i8b[:], in_=cst[:, C_I8:C_I8 + 512]), r=[cstB], w=[cstB])

        with ExitStack() as es2:
            sb2 = lambda n, s, d=F32: _alloc(es2, n, s, d)
            wsl = [sb2("wsl%d" % i, [128, 8, 1024]) for i in range(2)]
            wslB = Bs(2)
            scT = sb2("scT", [128, 8, 2])
            bad = sb2("bad", [128, 72])
            scB = Bf()
            k.dma(scT[:], cT_d, w=[scB])
            k.dma(bad[:], bada_d, w=[scB])
            k.op("act", lambda e: e.activation(out=scT[:], in_=scT[:], func=AF.Silu), r=[scB], w=[scB])
            for slab in range(9):
                s = slab % 2
                k.dma(wsl[s][:], wada_d[slab * 8:(slab + 1) * 8].rearrange("j p k f -> p j (k f)"), w=[wslB[s]])
                for jj in range(8):
                    j = slab * 8 + jj
                    for kc in range(8):
                        k.op("pe", lambda e: e.matmul(ps[:, 0, 2 * j:2 * j + 2], lhsT=wsl[s][:, jj, kc * 128:(kc + 1) * 128],
                                                      rhs=scT[:, kc, :], start=(kc == 0), stop=(kc == 7)),
                             r=[wslB[s], scB], w=[psB[0]])
            for b in range(2):
                k.op("dve", lambda e: e.tensor_tensor(out=modT[:, :, b], in0=ps[:, 0, b:144:2], in1=bad[:], op=ALU.add),
                     r=[psB[0], scB], w=[modB])
            for v in (1, 4, 7):
                k.op("dve", lambda e: e.tensor_scalar(out=modT[:, v * 8:v * 8 + 8, :], in0=modT[:, v * 8:v * 8 + 8, :],
                                                      scalar1=1.0, scalar2=None, op0=ALU.add), r=[modB], w=[modB])
            for v in (2, 8):
                k.op("dve", lambda e: e.tensor_scalar(out=modT[:, v * 8:v * 8 + 8, :], in0=modT[:, v * 8:v * 8 + 8, :],
                                                      scalar1=0.5, scalar2=None, op0=ALU.mult), r=[modB], w=[modB])
            k.barrier()

        def gen_gate_row(v, b, gbc, gbcB, colbc, colB):
            for kc in range(8):
                k.op("dve", lambda e: e.tensor_scalar(out=colbc[:], in0=cst[:, C_ONES:C_ONES + 128],
                                                      scalar1=modT[:, v * 8 + kc, b:b + 1], scalar2=None, op0=ALU.mult),
                     r=[modB, cstB], w=[colB])
                bank = 6 + kc // 4
                k.op("pe", lambda e: e.matmul(ps[:, bank, (kc % 4) * 128:(kc % 4 + 1) * 128], lhsT=colbc[:], rhs=ident,
                                              start=True, stop=True), r=[colB, cstB], w=[psB[bank]])
            for hb in range(2):
                k.op("act", lambda e: e.activation(out=gbc[:, hb * 512:(hb + 1) * 512], in_=ps[:, 6 + hb, :], func=AF.Identity),
                     r=[psB[6 + hb]], w=[gbcB])

        def layer_norm_tile(xt_ap, xB, lnbc, lnB, st6, mv, tmpc, smB):
            for hh in range(2):
                k.op("dve", lambda e: e.bn_stats(out=st6[:, hh, :], in_=xt_ap[:, hh * 512:(hh + 1) * 512]), r=[xB], w=[smB])
            k.op("dve", lambda e: e.bn_aggr(out=mv[:], in_=st6[:].rearrange("p a b -> p (a b)")), r=[smB], w=[smB])
            k.op("dve", lambda e: e.tensor_scalar(out=tmpc[:], in0=mv[:, 1:2], scalar1=1e-5, scalar2=None, op0=ALU.add),
                 r=[smB], w=[smB])
            k.op("act", lambda e: e.activation(out=tmpc[:], in_=tmpc[:], func=AF.Sqrt), r=[smB], w=[smB])
            k.op("dve", lambda e: e.reciprocal(out=tmpc[:], in_=tmpc[:]), r=[smB], w=[smB])
            k.op("dve", lambda e: e.tensor_scalar(out=xt_ap, in0=xt_ap, scalar1=mv[:, 0:1], scalar2=tmpc[:, 0:1],
                                                  op0=ALU.subtract, op1=ALU.mult), r=[smB, xB], w=[xB])
            k.op("dve", lambda e: e.tensor_tensor(out=xt_ap, in0=xt_ap, in1=lnbc[:, 0, :], op=ALU.mult), r=[xB, lnB], w=[xB])
            k.op("pool", lambda e: e.tensor_tensor(out=xt_ap, in0=xt_ap, in1=lnbc[:, 1, :], op=ALU.add), r=[xB, lnB], w=[xB])

        def ffn_phase(b, which):
            v0 = 0 if which == 0 else 6
            w1d, w3d, w2d = fw[which]
            lnrow = 0 if which == 0 else 4
            with ExitStack() as es2:
                sb2 = lambda n, s, d=F32: _alloc(es2, n, s, d)
                xres = sb2("xres", [128, NT, D])
                xB = Bs(NT)
                uT = sb2("uT", [128, 8, SEQ], BF16)
                uB = Bs(NT)
                wb = [[sb2("wg%d_%d" % (s, m), [128, 5, 1024], BF16) for m in range(3)] for s in range(2)]
                wB = [[Bs(3) for _ in range(5)] for _ in range(2)]
                stg = [sb2("stg%d" % i, [128, 1024]) for i in range(3)]
                stgB = Bs(3)
                gbc = sb2("gbc", [128, 1024])
                gbcB = Bf()
                colbc = sb2("colbc", [128, 128])
                colB = Bf()
                lnbc = sb2("lnbc", [128, 2, D])
                lnB = Bf()
                sT = [sb2("sT%d" % i, [128, 256], BF16) for i in range(3)]
                sTB = Bs(3)
                gT = [sb2("gT%d" % i, [128, 256], BF16) for i in range(3)]
                gTB = Bs(3)
                HBK = (0, 1, 6)
                st6 = sb2("st6", [128, 2, 6])
                mv = sb2("mv", [128, 2])
                tmpc = sb2("tmpc", [128, 1])
                smB = Bf()
                poB = Bs(2)

                gen_gate_row(v0 + 2, b, gbc, gbcB, colbc, colB)
                for i in range(2):
                    k.dma(lnbc[:, i, :], lnp_d[lnrow + i, :].partition_broadcast(128), w=[lnB])
                nstg = [0]

                def load_chunk(cg, slot, ci):
                    for m, src in enumerate((w1d, w3d, w2d)):
                        s = nstg[0] % 3
                        nstg[0] += 1
                        k.dma(stg[s][:], src[cg], w=[stgB[s]])
                        if m < 2:
                            k.op("act", lambda e: e.activation(out=wb[slot][m][:, ci, :], in_=stg[s][:], func=AF.Identity),
                                 r=[stgB[s]], w=[wB[slot][ci][m]])
                        else:
                            k.op("pool", lambda e: e.tensor_tensor(out=wb[slot][m][:, ci, :], in0=stg[s][:], in1=gbc[:], op=ALU.mult),
                                 r=[stgB[s], gbcB], w=[wB[slot][ci][m]])

                gstart = [sum(GROUPS[:g]) for g in range(len(GROUPS))]
                def load_tile(t):
                    if which == 0:
                        k.dma(xres[:, t, :], x_d[b, t * 128:(t + 1) * 128, :], w=[xB[t]])
                    else:
                        k.dma(xres[:, t, :], xs_d[t * 128:(t + 1) * 128, :], r=[xsB[t]], w=[xB[t]])

                for ci in range(GROUPS[0]):
                    load_chunk(ci, 0, ci)
                    load_tile(2 * ci)
                    load_tile(2 * ci + 1)
                for t in range(2 * GROUPS[0], NT):
                    load_tile(t)

                def prep_round(t, rnd):
                    bank = 7
                    if True:
                        for kc in range(4 * rnd, 4 * rnd + 4):
                            k.op("pe", lambda e: e.transpose(ps[:, bank, (kc % 4) * 128:(kc % 4 + 1) * 128],
                                                             xres[:, t, kc * 128:(kc + 1) * 128], ident),
                                 r=[xB[t], cstB], w=[psB[bank]])
                        for kc in range(4 * rnd, 4 * rnd + 4):
                            k.op("act", lambda e: e.activation(out=uT[:, kc, t * 128:(t + 1) * 128],
                                                               in_=ps[:, bank, (kc % 4) * 128:(kc % 4 + 1) * 128], func=AF.Identity,
                                                               scale=modT[:, (v0 + 1) * 8 + kc, b:b + 1], bias=modT[:, v0 * 8 + kc, b:b + 1]),
                                 r=[psB[bank], modB], w=[uB[t]])

                def finish_tile(t):
                    layer_norm_tile(xres[:, t, :], xB[t], lnbc, lnB, st6, mv, tmpc, smB)
                    if which == 0:
                        k.dma(xs_d[t * 128:(t + 1) * 128, :], xres[:, t, :], r=[xB[t]], w=[xsB[t]])
                    else:
                        k.dma(out_d[b, t * 128:(t + 1) * 128, :], xres[:, t, :], r=[xB[t]], w=[outB[b * NT + t]])

                pending = []
                step = 0
                for tt_ in range(2):
                    for rnd_ in range(2):
                        prep_round(tt_, rnd_)
                for g, gs in enumerate(GROUPS):
                    slot = g % 2
                    for blk in range(8):
                        if g + 1 < len(GROUPS) and blk < GROUPS[g + 1]:
                            load_chunk(gstart[g + 1] + blk, 1 - slot, blk)
                        for ci in range(gs):
                            if g == 0 and ci < 4 and blk + 1 < 8:
                                prep_round(2 * blk + 2 + ci // 2, ci % 2)
                            hi = step % 3
                            hb = HBK[hi]
                            step += 1
                            for m in range(2):
                                for kc in range(8):
                                    k.op("pe", lambda e: e.matmul(ps[:, hb, m * 256:(m + 1) * 256],
                                                                  lhsT=wb[slot][m][:, ci, kc * 128:(kc + 1) * 128],
                                                                  rhs=uT[:, kc, blk * 256:(blk + 1) * 256], start=(kc == 0), stop=(kc == 7)),
                                         r=[uB[2 * blk], uB[2 * blk + 1], wB[slot][ci][m]], w=[psB[hb]])
                            k.op("act", lambda e: e.activation(out=sT[hi][:], in_=ps[:, hb, 0:256], func=AF.Silu),
                                 r=[psB[hb]], w=[sTB[hi]])
                            k.op("dve", lambda e: e.tensor_tensor(out=gT[hi][:], in0=sT[hi][:], in1=ps[:, hb, 256:512], op=ALU.mult),
                                 r=[sTB[hi], psB[hb]], w=[gTB[hi]])
                            while len(pending) >= 2:
                                pending.pop(0)()

                            def w2_step(hb=hi, ci=ci, slot=slot, gs=gs, blk=blk, g=g):
                                for tt in range(2):
                                    for hf in range(2):
                                        k.op("pe", lambda e: e.matmul(ps[:, 2 + 2 * tt + hf, :], lhsT=gT[hb][:, tt * 128:(tt + 1) * 128],
                                                                      rhs=wb[slot][2][:, ci, hf * 512:(hf + 1) * 512],
                                                                      start=(ci == 0), stop=(ci == gs - 1)),
                                             r=[gTB[hb], wB[slot][ci][2]], w=[poB[tt]])
                                if ci == gs - 1:
                                    for tt in range(2):
                                        t = 2 * blk + tt
                                        xv = xres[:, t, :].rearrange("p (a c) -> p a c", a=2)
                                        if g == 0:
                                            k.op("dve", lambda e: e.scalar_tensor_tensor(out=xv, in0=xv, scalar=ALPHA, in1=ps[:, 2 + 2 * tt:4 + 2 * tt, :],
                                                                                         op0=ALU.mult, op1=ALU.add), r=[poB[tt], xB[t]], w=[xB[t]])
                                        else:
                                            k.op("dve", lambda e: e.tensor_tensor(out=xv, in0=xv, in1=ps[:, 2 + 2 * tt:4 + 2 * tt, :], op=ALU.add),
                                                 r=[poB[tt], xB[t]], w=[xB[t]])
                                        if g == len(GROUPS) - 1:
                                            finish_tile(t)
                            pending.append(w2_step)
                for fn in pending:
                    fn()
                k.barrier()

        def mixer_phase(b):
            with ExitStack() as es2:
                sb2 = lambda n, s, d=F32: _alloc(es2, n, s, d)
                win = sb2("win", [128, 8, NCOL], BF16)
                wout = sb2("wout", [128, 8, D], BF16)
                wB_ = Bf()
                with ExitStack() as es3:
                    sb3 = lambda n, s, d=F32: _alloc(es3, n, s, d)
                    stg = [sb3("mstg%d" % i, [128, 1024]) for i in range(3)]
                    stgB = Bs(3)
                    gbc = sb3("mgbc", [128, 1024])
                    gbcB = Bf()
                    colbc = sb3("mcolbc", [128, 128])
                    colB = Bf()
                    gen_gate_row(5, b, gbc, gbcB, colbc, colB)
                    n = 0
                    for kc in range(8):
                        for c0 in range(0, NCOL, 1024):
                            cw = min(1024, NCOL - c0)
                            s = n % 3
                            n += 1
                            k.dma(stg[s][:, 0:cw], win_d[:, kc, c0:c0 + cw], w=[stgB[s]])
                            k.op("pool", lambda e: e.tensor_copy(out=win[:, kc, c0:c0 + cw], in_=stg[s][:, 0:cw]), r=[stgB[s]], w=[wB_])
                        s = n % 3
                        n += 1
                        k.dma(stg[s][:], wout_d[:, kc, :], w=[stgB[s]])
                        k.op("pool", lambda e: e.tensor_tensor(out=wout[:, kc, :], in0=stg[s][:], in1=gbc[:], op=ALU.mult),
                             r=[stgB[s], gbcB], w=[wB_])
                    k.barrier()
                mixer_tiles(b, win, wout, wB_, sb2)
                k.barrier()

        def mixer_tiles(b, win, wout, wB_, sb2):
            cosT = sb2("cosT", [128, NT, 32])
            sinT = sb2("sinT", [128, NT, 32])
            csB = Bf()
            with ExitStack() as es4:
                sb4 = lambda n, s, d=F32: _alloc(es4, n, s, d)
                posi = sb4("posi", [128, NT], I32)
                posf = sb4("posf", [128, NT])
                ang = sb4("ang", [128, NT, 32])
                angi = sb4("angi", [128, NT, 32], I32)
                angf = sb4("angf", [128, NT, 32])
                angm = sb4("angm", [128, NT, 32])
                k.dma(posi[:], pos_d[:, b, :], w=[csB])
                k.op("dve", lambda e: e.tensor_copy(out=posf[:], in_=posi[:]), r=[csB], w=[csB])
                k.op("dve", lambda e: e.tensor_tensor(out=ang[:], in0=posf[:].unsqueeze(2).to_broadcast([128, NT, 32]),
                                                      in1=cst[:, C_INVF:C_INVF + 32].unsqueeze(1).to_broadcast([128, NT, 32]), op=ALU.mult),
                     r=[csB, cstB], w=[csB])
                TWO_PI = 2.0 * np.pi
                C1 = 6.28125
                C2 = float(TWO_PI - C1)

                def reduce_sin(dst, shift):
                    k.op("dve", lambda e: e.tensor_scalar(out=angf[:], in0=ang[:], scalar1=float(shift), scalar2=float(1.0 / TWO_PI),
                                                          op0=ALU.add, op1=ALU.mult), r=[csB], w=[csB])
                    k.op("dve", lambda e: e.tensor_copy(out=angi[:], in_=angf[:]), r=[csB], w=[csB])
                    k.op("dve", lambda e: e.tensor_copy(out=angf[:], in_=angi[:]), r=[csB], w=[csB])
                    k.op("dve", lambda e: e.scalar_tensor_tensor(out=angm[:], in0=angf[:], scalar=-C1, in1=ang[:], op0=ALU.mult, op1=ALU.add),
                         r=[csB], w=[csB])
                    k.op("dve", lambda e: e.scalar_tensor_tensor(out=angm[:], in0=angf[:], scalar=-C2, in1=angm[:], op0=ALU.mult, op1=ALU.add),
                         r=[csB], w=[csB])
                    if shift != 0.0:
                        k.op("dve", lambda e: e.tensor_scalar(out=angm[:], in0=angm[:], scalar1=float(shift), scalar2=None, op0=ALU.add),
                             r=[csB], w=[csB])
                    k.op("dve", lambda e: e.tensor_scalar(out=angf[:], in0=angm[:], scalar1=float(np.pi), scalar2=-TWO_PI,
                                                          op0=ALU.is_gt, op1=ALU.mult), r=[csB], w=[csB])
                    k.op("dve", lambda e: e.tensor_tensor(out=angm[:], in0=angm[:], in1=angf[:], op=ALU.add), r=[csB], w=[csB])
                    k.op("dve", lambda e: e.tensor_scalar(out=angf[:], in0=angm[:], scalar1=float(-np.pi), scalar2=TWO_PI,
                                                          op0=ALU.is_lt, op1=ALU.mult), r=[csB], w=[csB])
                    k.op("dve", lambda e: e.tensor_tensor(out=angm[:], in0=angm[:], in1=angf[:], op=ALU.add), r=[csB], w=[csB])
                    k.op("dve", lambda e: e.tensor_scalar(out=angm[:], in0=angm[:], scalar1=float(-np.pi), scalar2=float(np.pi),
                                                          op0=ALU.max, op1=ALU.min), r=[csB], w=[csB])
                    k.op("act", lambda e: e.activation(out=dst[:], in_=angm[:], func=AF.Sin), r=[csB], w=[csB])

                reduce_sin(sinT, 0.0)
                reduce_sin(cosT, float(np.pi / 2))
                k.barrier()

            lnbc = sb2("mlnbc", [128, 2, D])
            lnB = Bf()
            kT = sb2("kT", [64, SEQ], BF16)
            kiT = sb2("kiT", [64, SEQ], BF16)
            vaug = sb2("vaug", [128, NT, 65], BF16)
            kvB = Bf()
            xts = [sb2("xt%d" % i, [128, D]) for i in range(2)]
            xtB = Bs(2)
            uTt = sb2("uTt", [128, 8, 128], BF16)
            uB = Bf()
            roped = sb2("roped", [128, 18, 64], BF16)
            ropB = Bf()
            qT = sb2("qT", [64, 8, 128], BF16)
            qiT = sb2("qiT", [64, 8, 128], BF16)
            qB = Bf()
            absw = sb2("absw", [128, 8])
            sgn = sb2("sgn", [128, 8])
            awB = Bf()
            dsg = sb2("dsg", [128, 8, 128])
            dsB = Bf()
            score = sb2("score", [128, SEQ])
            scB = Bf()
            work = sb2("work", [128, SEQ])
            wkB = Bf()
            tok = work[:, 0:NTM]
            tokB = wkB
            mbias = sb2("mbias", [128, SEQ], BF16)
            mbB = Bf()
            bs = sb2("bs", [128, 8])
            wkt = sb2("wkt", [128, 24])
            m8B = Bf()
            rel = [sb2("rel%d" % i, [128, 512]) for i in range(3)]
            relB = Bs(3)
            PT = [sb2("PT%d" % i, [128, 512], BF16) for i in range(3)]
            PTB = Bs(3)
            rec = sb2("rec", [128, 8])
            attn = sb2("attn", [128, 8, 64], BF16)
            atB = Bf()
            catT = sb2("catT", [128, 8, 128], BF16)
            catB = Bf()
            st6 = sb2("mst6", [128, 2, 6])
            mv = sb2("mmv", [128, 2])
            tmpc = sb2("mtmpc", [128, 1])
            smB = Bf()
            xc = sb2("xc", [128, 12, 131])
            xcB = Bf()
            tm = sb2("tm", [128, 1536])
            tmB = Bf()
            junk = sb2("junk", [128, 128])
            jkB = Bf()
            ss = sb2("ss", [128, 8])
            rs = sb2("rs", [128, 8])
            sc4 = sb2("sc4", [128, 16, 4])
            s4B = Bf()
            gg = sb2("gg", [128, 16])
            gdnp = sb2("gdnp", [128, 8])
            negA = sb2("negA", [128, 4])
            dng = sb2("dng", [128, 128])
            convw = sb2("convw", [128, 12, 4])
            gpB = Bf()
            hd = [sb2("hd%d" % h, [128, 6, 128]) for h in range(4)]
            hdB = [Bs(6) for _ in range(4)]
            ycv = lambda cc: hd[2 + cc // 6][:, cc % 6, :]
            ycB = lambda cc: hdB[2 + cc // 6][cc % 6]
            rA = hd[0][:].rearrange("p a c -> p (a c)")[:, 0:576].rearrange("p (h d) -> p h d", d=32)
            rBt = hd[1][:].rearrange("p a c -> p (a c)")[:, 0:576].rearrange("p (h d) -> p h d", d=32)
            rpB = hdB[0] + hdB[1]
            kd = [sb2("kd%d" % h, [128, 128], BF16) for h in range(4)]
            kdB = Bs(4)
            T3 = [sb2("T3%d" % h, [128, 3, 128], BF16) for h in range(4)]
            T3B = Bs(4)
            AN = [[sb2("AN%d_%d" % (h, i), [128, 2, 128]) for i in range(2)] for h in range(4)]
            ANB = [Bs(2) for _ in range(4)]
            Mm = [[sb2("Mm%d_%d" % (h, i), [128, 128]) for i in range(2)] for h in range(4)]
            MB = [Bs(2) for _ in range(4)]
            aqk = [sb2("aqk%d" % h, [128, 128], BF16) for h in range(4)]
            aqB = Bs(4)
            U = [sb2("U%d" % h, [128, 128]) for h in range(4)]
            UB = Bs(4)
            WT = [sb2("WT%d" % h, [128, 128], BF16) for h in range(4)]
            WTB = Bs(4)
            dl = [sb2("dl%d" % h, [128, 128], BF16) for h in range(4)]
            dlB = Bs(4)
            S = sb2("S", [128, 4, 128])
            Sb = sb2("Sb", [128, 4, 128], BF16)
            SB_ = Bs(4)
            SbB = Bs(4)
            otm = sb2("otm", [128, 4, 128])
            otB = Bs(4)
            sz = sb2("sz", [128, 512])
            szB = Bf()
            dn = sb2("dn", [128, 512], BF16)
            dnB = Bf()

            dbank = [0]

            def dbl():
                i = dbank[0] % 2
                dbank[0] += 1
                b0 = (0, 2)[i]
                return b0, [psB[b0], psB[b0 + 1]]

            for i in range(2):
                k.dma(lnbc[:, i, :], lnp_d[2 + i, :].partition_broadcast(128), w=[lnB])
            k.dma(gdnp[:], gdnp_d.partition_broadcast(128), w=[gpB])
            k.dma(dng[:], dng_d.partition_broadcast(128), w=[gpB])
            k.dma(convw[:], conv_d, w=[gpB])
            k.op("act", lambda e: e.activation(out=negA[:], in_=gdnp[:, 0:4], func=AF.Exp), r=[gpB], w=[gpB])
            k.op("dve", lambda e: e.tensor_scalar(out=negA[:], in0=negA[:], scalar1=-1.0, scalar2=None, op0=ALU.mult), r=[gpB], w=[gpB])
            k.op("pool", lambda e: e.memset(S[:], 0.0), w=SB_)
            k.op("pool", lambda e: e.memset(Sb[:], 0.0), w=SbB)
            k.op("pool", lambda e: e.memset(xc[:], 0.0), w=[xcB])
            k.op("pool", lambda e: e.memset(vaug[:], 1.0), w=[kvB])
            WC = float(8 ** -0.5 * 64 ** -0.5)

            def prologue(t):
                xa, xB1 = xts[t % 2], xtB[t % 2]
                k.dma(xa[:], xs_d[t * 128:(t + 1) * 128, :], r=[xsB[t]], w=[xB1])
                b0, dB = dbl()
                for kc in range(8):
                    bank = b0 + kc // 4
                    k.op("pe", lambda e: e.transpose(ps[:, bank, (kc % 4) * 128:(kc % 4 + 1) * 128], xa[:, kc * 128:(kc + 1) * 128], ident),
                         r=[xB1, cstB], w=dB)
                for kc in range(8):
                    bank = b0 + kc // 4
                    k.op("act", lambda e: e.activation(out=uTt[:, kc, :], in_=ps[:, bank, (kc % 4) * 128:(kc % 4 + 1) * 128],
                                                       func=AF.Identity, scale=modT[:, 32 + kc, b:b + 1], bias=modT[:, 24 + kc, b:b + 1]),
                         r=dB + [modB], w=[uB])
                for (c0, c1) in ((0, 1024), (1024, NTM)):
                    b0, dB = dbl()
                    for s0 in range(c0, c1, 512):
                        s1 = min(s0 + 512, c1)
                        bank = b0 + (s0 - c0) // 512
                        for kc in range(8):
                            k.op("pe", lambda e: e.matmul(ps[:, bank, 0:s1 - s0], lhsT=uTt[:, kc, :], rhs=win[:, kc, s0:s1],
                                                          start=(kc == 0), stop=(kc == 7)), r=[uB, wB_], w=dB)
                        k.op("act", lambda e: e.activation(out=tok[:, s0:s1], in_=ps[:, bank, 0:s1 - s0], func=AF.Identity),
                             r=dB, w=[tokB])
                tk = tok[:, 0:1152].rearrange("p (h d) -> p h d", d=64)
                cb = cosT[:, t, :].unsqueeze(1).to_broadcast([128, 18, 32])
                sbb = sinT[:, t, :].unsqueeze(1).to_broadcast([128, 18, 32])
                k.op("dve", lambda e: e.tensor_tensor(out=rA, in0=tk[:, :, 0:32], in1=cb, op=ALU.mult), r=[tokB, csB], w=rpB)
                k.op("dve", lambda e: e.tensor_tensor(out=rBt, in0=tk[:, :, 32:64], in1=sbb, op=ALU.mult), r=[tokB, csB], w=rpB)
                k.op("dve", lambda e: e.tensor_tensor(out=roped[:, :, 0:32], in0=rA, in1=rBt, op=ALU.subtract), r=rpB, w=[ropB])
                k.op("dve", lambda e: e.tensor_tensor(out=rA, in0=tk[:, :, 32:64], in1=cb, op=ALU.mult), r=[tokB, csB, ropB], w=rpB)
                k.op("dve", lambda e: e.tensor_tensor(out=rBt, in0=tk[:, :, 0:32], in1=sbb, op=ALU.mult), r=[tokB, csB], w=rpB)
                k.op("dve", lambda e: e.tensor_tensor(out=roped[:, :, 32:64], in0=rA, in1=rBt, op=ALU.add), r=rpB, w=[ropB])
                k.op("act", lambda e: e.activation(out=vaug[:, t, 0:64], in_=tok[:, 1152:1216], func=AF.Identity), r=[tokB], w=[kvB])
                k.op("act", lambda e: e.activation(out=sgn[:], in_=tok[:, 1216:1224], func=AF.Sign), r=[tokB], w=[awB])
                k.op("dve", lambda e: e.scalar_tensor_tensor(out=absw[:], in0=tok[:, 1216:1224], scalar=WC, in1=sgn[:],
                                                             op0=ALU.mult, op1=ALU.mult), r=[tokB, awB], w=[awB])
                b0, dB = dbl()
                pbf = ps[:, b0:b0 + 2, :].bitcast(BF16)
                for h in range(16):
                    k.op("pe", lambda e: e.transpose(pbf[0:64, h // 8, (h % 8) * 128:(h % 8 + 1) * 128], roped[:, h, :], idb[:]),
                         r=[ropB, cstB], w=dB)
                k.op("act", lambda e: e.activation(out=qT[:].rearrange("p h t -> p (h t)"), in_=pbf[0:64, 0, :], func=AF.Identity),
                     r=dB, w=[qB])
                k.op("act", lambda e: e.activation(out=qiT[:].rearrange("p h t -> p (h t)"), in_=pbf[0:64, 1, :], func=AF.Identity),
                     r=dB, w=[qB])
                b0, dB = dbl()
                pbf2 = ps[:, b0, :].bitcast(BF16)
                for h in range(2):
                    k.op("pe", lambda e: e.transpose(pbf2[0:64, h * 128:(h + 1) * 128], roped[:, 16 + h, :], idb[:]),
                         r=[ropB, cstB], w=dB)
                k.op("act", lambda e: e.activation(out=kT[:, t * 128:(t + 1) * 128], in_=pbf2[0:64, 0:128], func=AF.Identity), r=dB, w=[kvB])
                k.op("act", lambda e: e.activation(out=kiT[:, t * 128:(t + 1) * 128], in_=pbf2[0:64, 128:256], func=AF.Identity), r=dB, w=[kvB])
                k.op("dve", lambda e: e.tensor_tensor(out=sc4[:, 0, :], in0=tok[:, 1224:1228], in1=gdnp[:, 4:8], op=ALU.add),
                     r=[tokB, gpB], w=[s4B])
                k.op("act", lambda e: e.activation(out=sc4[:, 0, :], in_=sc4[:, 0, :], func=AF.Exp), r=[s4B], w=[s4B])
                k.op("act", lambda e: e.activation(out=sc4[:, 0, :], in_=sc4[:, 0, :], func=AF.Ln, bias=1.0), r=[s4B], w=[s4B])
                k.op("dve", lambda e: e.tensor_tensor(out=sc4[:, 0, :], in0=sc4[:, 0, :], in1=negA[:], op=ALU.mult), r=[s4B, gpB], w=[s4B])
                k.op("act", lambda e: e.activation(out=sc4[:, 1, :], in_=tok[:, 1228:1232], func=AF.Sigmoid), r=[tokB], w=[s4B])
                k.op("act", lambda e: e.activation(out=sz[:], in_=tok[:, 1232:1744], func=AF.Silu), r=[tokB], w=[szB])
                b0, dB = dbl()
                for i, co in enumerate((C_TRI, C_OBD, C_SEL0, C_SEL1)):
                    k.op("pe", lambda e: e.matmul(ps[:, b0, i * 4:i * 4 + 4], lhsT=cst[:, co:co + 128], rhs=sc4[:, 0, :], start=True, stop=True),
                         r=[s4B, cstB], w=dB)
                k.op("dve", lambda e: e.tensor_copy(out=gg[:], in_=ps[:, b0, 0:16]), r=dB, w=[s4B])
                k.op("act", lambda e: e.activation(out=sc4[:, 2, :], in_=gg[:, 0:4], func=AF.Exp), r=[s4B], w=[s4B])
                k.op("dve", lambda e: e.tensor_tensor(out=sc4[:, 8, :], in0=gg[:, 4:8], in1=gg[:, 0:4], op=ALU.subtract), r=[s4B], w=[s4B])
                k.op("act", lambda e: e.activation(out=sc4[:, 3, :], in_=sc4[:, 8, :], func=AF.Exp), r=[s4B], w=[s4B])
                k.op("act", lambda e: e.activation(out=sc4[:, 6, :], in_=gg[:, 8:12], func=AF.Exp), r=[s4B], w=[s4B])
                k.op("act", lambda e: e.activation(out=sc4[:, 7, :], in_=gg[:, 12:16], func=AF.Exp), r=[s4B], w=[s4B])
                k.op("dve", lambda e: e.tensor_tensor(out=sc4[:, 4, :], in0=sc4[:, 1, :], in1=sc4[:, 2, :], op=ALU.mult), r=[s4B], w=[s4B])
                k.op("dve", lambda e: e.tensor_scalar(out=sc4[:, 5, :], in0=sc4[:, 1, :], scalar1=-1.0, scalar2=None, op0=ALU.mult), r=[s4B], w=[s4B])

            def gdn_pro(t):
                for grp in range(3):
                    b0, dB = dbl()
                    for q4 in range(4):
                        cc = grp * 4 + q4
                        for kc in range(8):
                            k.op("pe", lambda e: e.matmul(ps[:, b0, q4 * 128:(q4 + 1) * 128], lhsT=win[:, kc, NTM + cc * 128:NTM + (cc + 1) * 128],
                                                          rhs=uTt[:, kc, :], start=(kc == 0), stop=(kc == 7)), r=[uB, wB_], w=dB)
                    k.op("act", lambda e: e.activation(out=xc[:, grp * 4:(grp + 1) * 4, 3:131],
                                                       in_=ps[:, b0, :].rearrange("p (a c) -> p a c", a=4), func=AF.Identity),
                         r=dB, w=[xcB])
                    yield
                for cc in range(12):
                    k.op("dve", lambda e: e.tensor_scalar(out=ycv(cc), in0=xc[:, cc, 3:131], scalar1=convw[:, cc, 3:4], scalar2=None,
                                                          op0=ALU.mult), r=[xcB, gpB], w=[ycB(cc)])
                    for j in range(3):
                        k.op("dve", lambda e: e.scalar_tensor_tensor(out=ycv(cc), in0=xc[:, cc, j:j + 128], scalar=convw[:, cc, j:j + 1],
                                                                     in1=ycv(cc), op0=ALU.mult, op1=ALU.add), r=[xcB, gpB, ycB(cc)], w=[ycB(cc)])
                    if cc % 3 == 2:
                        yield
                k.op("pool", lambda e: e.tensor_copy(out=xc[:, :, 0:3], in_=xc[:, :, 128:131]), r=[xcB], w=[xcB])
                for hh in (2, 3):
                    k.op("act", lambda e: e.activation(out=hd[hh][:], in_=hd[hh][:], func=AF.Silu), r=hdB[hh], w=hdB[hh])
                for grp in range(3):
                    b0, dB = dbl()
                    for q4 in range(4):
                        cc = grp * 4 + q4
                        k.op("pe", lambda e: e.transpose(ps[:, b0, q4 * 128:(q4 + 1) * 128], ycv(cc), ident), r=[ycB(cc), cstB], w=dB)
                    k.op("act", lambda e: e.activation(out=tm[:, grp * 512:(grp + 1) * 512], in_=ps[:, b0, :], func=AF.Identity), r=dB, w=[tmB])
                    yield
                for g8 in range(8):
                    k.op("act", lambda e: e.activation(out=junk[:], in_=tm[:, g8 * 128:(g8 + 1) * 128], func=AF.Square,
                                                       accum_out=ss[:, g8:g8 + 1]), r=[tmB], w=[jkB, s4B])
                k.op("dve", lambda e: e.tensor_scalar(out=rs[:], in0=ss[:], scalar1=1e-6, scalar2=None, op0=ALU.add), r=[s4B], w=[s4B])
                k.op("act", lambda e: e.activation(out=rs[:], in_=rs[:], func=AF.Sqrt), r=[s4B], w=[s4B])
                k.op("dve", lambda e: e.reciprocal(out=rs[:], in_=rs[:]), r=[s4B], w=[s4B])
                k.op("dve", lambda e: e.tensor_scalar(out=sc4[:, 9, :], in0=rs[:, 0:4], scalar1=float(128 ** -0.5), scalar2=None, op0=ALU.mult),
                     r=[s4B], w=[s4B])
                k.op("dve", lambda e: e.tensor_tensor(out=sc4[:, 10, :], in0=rs[:, 4:8], in1=sc4[:, 4, :], op=ALU.mult), r=[s4B], w=[s4B])
                k.op("dve", lambda e: e.tensor_tensor(out=sc4[:, 11, :], in0=rs[:, 4:8], in1=sc4[:, 3, :], op=ALU.mult), r=[s4B], w=[s4B])
                k.op("dve", lambda e: e.tensor_tensor(out=sc4[:, 12, :], in0=sc4[:, 9, :], in1=sc4[:, 2, :], op=ALU.mult), r=[s4B], w=[s4B])
                yield

            def attn_path(t):
                W = (t + 1) * 128
                if t >= 2:
                    nrel = 0
                    for s0 in range(0, W, 512):
                        s1 = min(s0 + 512, W)
                        sw = s1 - s0
                        for h in range(8):
                            ri = nrel % 3
                            rb = 4 + nrel % 4
                            nrel += 1
                            k.op("pe", lambda e: e.matmul(ps[:, rb, 0:sw], lhsT=qiT[:, h, :], rhs=kiT[:, s0:s1], start=True, stop=True),
                                 r=[qB, kvB], w=[psB[rb]])
                            k.op("act", lambda e: e.activation(out=rel[ri][:, 0:sw], in_=ps[:, rb, 0:sw], func=AF.Relu,
                                                               scale=absw[:, h:h + 1]), r=[psB[rb], awB], w=[relB[ri]])
                            if h == 0:
                                k.op("dve", lambda e: e.tensor_scalar(out=score[:, s0:s1], in0=rel[ri][:, 0:sw], scalar1=sgn[:, 0:1],
                                                                      scalar2=None, op0=ALU.mult), r=[relB[ri], awB], w=[scB])
                            else:
                                k.op("dve", lambda e: e.scalar_tensor_tensor(out=score[:, s0:s1], in0=rel[ri][:, 0:sw], scalar=sgn[:, h:h + 1],
                                                                             in1=score[:, s0:s1], op0=ALU.mult, op1=ALU.add),
                                     r=[relB[ri], awB, scB], w=[scB])
                            if h % 2 == 1:
                                yield
                    k.op("pool", lambda e: e.affine_select(out=score[:, t * 128:W], in_=score[:, t * 128:W], pattern=[[-1, 128]],
                                                           compare_op=ALU.is_ge, fill=NEGFILL, base=0, channel_multiplier=1),
                         r=[scB], w=[scB])
                    k.op("dve", lambda e: e.tensor_reduce(out=bs[:, 0:1], in_=score[:, 0:t * 128], axis=AX.X, op=ALU.min), r=[scB], w=[m8B])
                    k.op("dve", lambda e: e.tensor_reduce(out=bs[:, 1:2], in_=score[:, 0:W], axis=AX.X, op=ALU.max), r=[scB], w=[m8B])
                    k.op("dve", lambda e: e.tensor_tensor(out=bs[:, 2:3], in0=bs[:, 1:2], in1=bs[:, 0:1], op=ALU.subtract), r=[m8B], w=[m8B])
                    k.op("dve", lambda e: e.tensor_scalar(out=wkt[:], in0=cst[:, C_POW:C_POW + 24], scalar1=bs[:, 2:3], scalar2=None, op0=ALU.mult),
                         r=[m8B, cstB], w=[m8B])
                    k.op("dve", lambda e: e.tensor_tensor(out=bs[:, 3:4], in0=bs[:, 0:1], in1=wkt[:, 0:1], op=ALU.add), r=[m8B], w=[m8B])
                    yield
                    for kk in range(NBIS):
                        k.op("dve", lambda e: e.tensor_scalar(out=mbias[:, 0:W], in0=score[:, 0:W], scalar1=bs[:, 3:4], scalar2=0.0,
                                                              op0=ALU.is_ge, op1=ALU.add, accum_out=bs[:, 6:7]), r=[scB, m8B], w=[mbB, m8B])
                        k.op("dve", lambda e: e.scalar_tensor_tensor(out=bs[:, 4:5], in0=bs[:, 6:7], scalar=255.5, in1=wkt[:, kk:kk + 1],
                                                                     op0=ALU.is_ge, op1=ALU.mult), r=[m8B], w=[m8B])
                        k.op("dve", lambda e: e.scalar_tensor_tensor(out=bs[:, 3:4], in0=bs[:, 4:5], scalar=wkt[:, kk + 1:kk + 2], in1=bs[:, 3:4],
                                                                     op0=ALU.subtract, op1=ALU.add), r=[m8B], w=[m8B])
                        yield
                    k.op("dve", lambda e: e.tensor_tensor(out=bs[:, 5:6], in0=bs[:, 3:4], in1=wkt[:, NBIS:NBIS + 1], op=ALU.subtract), r=[m8B], w=[m8B])
                    k.op("dve", lambda e: e.tensor_scalar(out=mbias[:, 0:W], in0=score[:, 0:W], scalar1=bs[:, 5:6], scalar2=MASKV,
                                                          op0=ALU.is_lt, op1=ALU.mult), r=[scB, m8B], w=[mbB])
                else:
                    k.op("pool", lambda e: e.memset(mbias[:, 0:W], 0.0), w=[mbB])
                    k.op("pool", lambda e: e.affine_select(out=mbias[:, t * 128:W], in_=mbias[:, t * 128:W], pattern=[[-1, 128]],
                                                           compare_op=ALU.is_ge, fill=MASKV, base=0, channel_multiplier=1),
                         r=[mbB], w=[mbB])
                yield
                pvB = [psB[6], psB[7]]
                items = [(kb, hf) for kb in range(t + 1) for hf in range(2)]

                def emit_st(i):
                    kb, hf = items[i]
                    sbk = 4 + i % 2
                    pi = i % 3
                    k.op("pe", lambda e: e.matmul(ps[:, sbk, :], lhsT=kT[:, kb * 128:(kb + 1) * 128],
                                                  rhs=qT[:, hf * 4:(hf + 1) * 4, :].rearrange("p h t -> p (h t)"),
                                                  start=True, stop=False), r=[kvB, qB], w=[psB[sbk]])
                    k.op("pe", lambda e: e.matmul(ps[:, sbk, :], lhsT=mbias[:, kb * 128:(kb + 1) * 128], rhs=i8b[:],
                                                  start=False, stop=True), r=[mbB, cstB], w=[psB[sbk]])
                    k.op("act", lambda e: e.activation(out=PT[pi][:], in_=ps[:, sbk, :], func=AF.Exp, scale=0.125),
                         r=[psB[sbk]], w=[PTB[pi]])

                def emit_pv(i):
                    kb, hf = items[i]
                    pi = i % 3
                    for hh in range(4):
                        k.op("pe", lambda e: e.matmul(ps[:, 6 + hf, hh * 128:hh * 128 + 65], lhsT=PT[pi][:, hh * 128:(hh + 1) * 128],
                                                      rhs=vaug[:, kb, :], start=(kb == 0 and hh == 0), stop=(kb == t),
                                                      skip_group_check=True), r=[PTB[pi], kvB], w=[psB[6 + hf]])

                for i in range(len(items)):
                    emit_st(i)
                    if i >= 1:
                        emit_pv(i - 1)
                    if i % 2 == 1:
                        yield
                emit_pv(len(items) - 1)
                pv = ps[:, 6:8, :].rearrange("p a (h c) -> p (a h) c", c=128)
                k.op("dve", lambda e: e.reciprocal(out=rec[:], in_=pv[:, :, 64]), r=pvB, w=[atB])
                k.op("dve", lambda e: e.tensor_tensor(out=attn[:], in0=pv[:, :, 0:64], in1=rec[:].unsqueeze(2).to_broadcast([128, 8, 64]),
                                                      op=ALU.mult), r=pvB + [atB], w=[atB])
                pbf = ps[:, 4, :].bitcast(BF16)
                for j in range(4):
                    k.op("pe", lambda e: e.transpose(pbf[:, j * 128:(j + 1) * 128], attn[:, 2 * j:2 * j + 2, :].rearrange("p h d -> p (h d)"), idb[:]),
                         r=[atB, cstB], w=[psB[4]])
                k.op("act", lambda e: e.activation(out=catT[:, 0:4, :].rearrange("p a t -> p (a t)"), in_=pbf[:, 0:512], func=AF.Identity),
                     r=[psB[4]], w=[catB])
                yield

            KH, KBG, QS, QD, VB, DG = range(6)

            def gdn_path(t):
                H4 = range(4)
                col = lambda s, h: sc4[:, s, h:h + 1]
                bk = lambda h: [psB[h]]
                for _ in gdn_pro(t):
                    yield
                for h in H4:
                    ksl = tm[:, 512 + h * 128:512 + (h + 1) * 128]
                    qsl = tm[:, h * 128:(h + 1) * 128]
                    vsl = tm[:, 1024 + h * 128:1024 + (h + 1) * 128]
                    for dst, dB_, src, sc_ in ((hd[h][:, KH, :], hdB[h][KH], ksl, rs[:, 4 + h:5 + h]),
                                               (hd[h][:, QS, :], hdB[h][QS], qsl, col(9, h)),
                                               (hd[h][:, QD, :], hdB[h][QD], qsl, col(12, h)),
                                               (hd[h][:, KBG, :], hdB[h][KBG], ksl, col(10, h)),
                                               (kd[h][:], kdB[h], ksl, col(11, h)),
                                               (hd[h][:, VB, :], hdB[h][VB], vsl, col(1, h))):
                        k.op("act", lambda e: e.activation(out=dst, in_=src, func=AF.Identity, scale=sc_), r=[tmB, s4B], w=[dB_])
                    k.op("dve", lambda e: e.tensor_scalar(out=hd[h][:, DG, :], in0=ident, scalar1=gg[:, h:h + 1], scalar2=None, op0=ALU.mult),
                         r=[cstB, s4B], w=[hdB[h][DG]])
                    if h % 2 == 1:
                        yield
                for h in H4:
                    for i, src in enumerate((KH, QS, QD)):
                        k.op("pe", lambda e: e.transpose(ps[:, h, i * 128:(i + 1) * 128], hd[h][:, src, :], ident), r=[hdB[h][src], cstB], w=bk(h))
                yield
                for h in H4:
                    k.op("act", lambda e: e.activation(out=T3[h][:].rearrange("p a t -> p (a t)"), in_=ps[:, h, 0:384], func=AF.Identity),
                         r=bk(h), w=[T3B[h]])
                yield
                for h in H4:
                    k.op("pe", lambda e: e.matmul(ps[:, h, 0:128], lhsT=T3[h][:, 0, :], rhs=T3[h][:, 0, :], start=True, stop=True), r=[T3B[h]], w=bk(h))
                    k.op("pe", lambda e: e.matmul(ps[:, h, 128:256], lhsT=T3[h][:, 0, :], rhs=T3[h][:, 1, :], start=True, stop=True), r=[T3B[h]], w=bk(h))
                    k.op("pe", lambda e: e.matmul(ps[:, h, 256:384], lhsT=cst[:, C_ONES:C_ONES + 128], rhs=hd[h][:, DG, :], start=True, stop=True),
                         r=[hdB[h][DG], cstB], w=bk(h))
                yield
                for h in H4:
                    tt1 = AN[h][1][:, 0, :]
                    tt2 = AN[h][1][:, 1, :]
                    k.op("dve", lambda e: e.scalar_tensor_tensor(out=tt1, in0=ps[:, h, 256:384], scalar=gg[:, h:h + 1],
                                                                 in1=cst[:, C_MB1:C_MB1 + 128], op0=ALU.subtract, op1=ALU.subtract),
                         r=bk(h) + [s4B, cstB], w=[ANB[h][1]])
                    k.op("dve", lambda e: e.scalar_tensor_tensor(out=tt2, in0=ps[:, h, 256:384], scalar=gg[:, h:h + 1],
                                                                 in1=cst[:, C_MB2:C_MB2 + 128], op0=ALU.subtract, op1=ALU.add),
                         r=bk(h) + [s4B, cstB], w=[ANB[h][1]])
                yield
                for h in H4:
                    k.op("act", lambda e: e.activation(out=AN[h][1][:, 0, :], in_=AN[h][1][:, 0, :], func=AF.Exp, scale=-1.0),
                         r=[ANB[h][1]], w=[ANB[h][1]])
                    k.op("act", lambda e: e.activation(out=AN[h][1][:, 1, :], in_=AN[h][1][:, 1, :], func=AF.Exp), r=[ANB[h][1]], w=[ANB[h][1]])
                yield
                for h in H4:
                    k.op("dve", lambda e: e.scalar_tensor_tensor(out=AN[h][0][:, 1, :], in0=ps[:, h, 0:128], scalar=col(5, h), in1=AN[h][1][:, 0, :],
                                                                 op0=ALU.mult, op1=ALU.mult), r=bk(h) + [s4B, ANB[h][1]], w=[ANB[h][0]])
                    k.op("dve", lambda e: e.tensor_tensor(out=aqk[h][:], in0=ps[:, h, 128:256], in1=AN[h][1][:, 1, :], op=ALU.mult),
                         r=bk(h) + [ANB[h][1]], w=[aqB[h]])
                yield
                for h in H4:
                    k.op("pe", lambda e: e.transpose(ps[:, h, 0:128], AN[h][0][:, 1, :], ident), r=[ANB[h][0], cstB], w=bk(h))
                yield
                for h in H4:
                    k.op("act", lambda e: e.activation(out=AN[h][0][:, 0, :], in_=ps[:, h, 0:128], func=AF.Identity), r=bk(h), w=[ANB[h][0]])
                yield
                for h in H4:
                    k.op("dve", lambda e: e.tensor_tensor(out=Mm[h][0][:], in0=AN[h][0][:, 0, :], in1=ident, op=ALU.add),
                         r=[ANB[h][0], cstB], w=[MB[h][0]])

                def sq(h, cur):
                    k.op("pe", lambda e: e.matmul(ps[:, h, 0:128], lhsT=AN[h][cur][:, 1, :], rhs=AN[h][cur][:, 0, :], start=True, stop=True),
                         r=[ANB[h][cur]], w=bk(h))
                    k.op("pe", lambda e: e.matmul(ps[:, h, 128:256], lhsT=AN[h][cur][:, 0, :], rhs=AN[h][cur][:, 1, :], start=True, stop=True),
                         r=[ANB[h][cur]], w=bk(h))

                def ev(h, nxt):
                    k.op("act", lambda e: e.activation(out=AN[h][nxt][:].rearrange("p a c -> p (a c)"), in_=ps[:, h, 0:256], func=AF.Identity),
                         r=bk(h), w=[ANB[h][nxt]])

                def pr(h, an, mc):
                    k.op("pe", lambda e: e.matmul(ps[:, h, 256:384], lhsT=AN[h][an][:, 1, :], rhs=Mm[h][mc][:], start=True, stop=True),
                         r=[ANB[h][an], MB[h][mc]], w=bk(h))

                def ad(h, mc):
                    k.op("dve", lambda e: e.tensor_tensor(out=Mm[h][1 - mc][:], in0=ps[:, h, 256:384], in1=Mm[h][mc][:], op=ALU.add),
                         r=bk(h) + [MB[h][mc]], w=[MB[h][1 - mc]])

                for h in H4:
                    sq(h, 0)
                yield
                for h in H4:
                    ev(h, 1)
                yield
                for it in range(1, 6):
                    an = it % 2
                    for h in H4:
                        if it < 5:
                            sq(h, an)
                        pr(h, an, (it - 1) % 2)
                    yield
                    for h in H4:
                        if it < 5:
                            ev(h, 1 - an)
                        ad(h, (it - 1) % 2)
                    yield
                mf = 5 % 2
                for h in H4:
                    k.op("pe", lambda e: e.matmul(ps[:, h, 0:128], lhsT=Mm[h][mf][:], rhs=hd[h][:, VB, :], start=True, stop=True),
                         r=[MB[h][mf], hdB[h][VB]], w=bk(h))
                    k.op("pe", lambda e: e.matmul(ps[:, h, 128:256], lhsT=hd[h][:, KBG, :], rhs=Mm[h][mf][:], start=True, stop=True),
                         r=[MB[h][mf], hdB[h][KBG]], w=bk(h))
                yield
                for h in H4:
                    k.op("act", lambda e: e.activation(out=U[h][:], in_=ps[:, h, 0:128], func=AF.Identity), r=bk(h), w=[UB[h]])
                    k.op("act", lambda e: e.activation(out=WT[h][:], in_=ps[:, h, 128:256], func=AF.Identity), r=bk(h), w=[WTB[h]])
                yield
                for c in range(2):
                    r0, r1 = 64 * c, 64 * c + 64
                    for h in H4:
                        k.op("pe", lambda e: e.matmul(ps[:, h, 0:128], lhsT=WT[h][:], rhs=Sb[:, h, :], start=True, stop=True),
                             r=[WTB[h], SbB[h]], w=bk(h))
                    yield
                    for h in H4:
                        k.op("dve", lambda e: e.tensor_tensor(out=dl[h][r0:r1, :], in0=U[h][r0:r1, :], in1=ps[r0:r1, h, 0:128], op=ALU.subtract),
                             r=bk(h) + [UB[h]], w=[dlB[h]])
                    yield
                    for h in H4:
                        k.op("pe", lambda e: e.matmul(ps[:, h, 256:384], lhsT=kd[h][r0:r1, :], rhs=dl[h][r0:r1, :], start=True, stop=True),
                             r=[kdB[h], dlB[h]], w=bk(h))
                        k.op("pe", lambda e: e.matmul(ps[:, h, 128:256], lhsT=T3[h][:, 2, :], rhs=Sb[:, h, :], start=True, stop=False),
                             r=[T3B[h], SbB[h]], w=bk(h))
                        k.op("pe", lambda e: e.matmul(ps[:, h, 128:256], lhsT=aqk[h][r0:r1, :], rhs=dl[h][r0:r1, :], start=False, stop=True),
                             r=[aqB[h], dlB[h]], w=bk(h))
                    yield
                    for h in H4:
                        k.op("dve", lambda e: e.scalar_tensor_tensor(out=Sb[:, h, :], in0=S[:, h, :], scalar=sc4[:, 6 + c, h:h + 1],
                                                                     in1=ps[:, h, 256:384], op0=ALU.mult, op1=ALU.add),
                             r=bk(h) + [s4B, SB_[h]], w=[SbB[h]])
                        k.op("dve", lambda e: e.scalar_tensor_tensor(out=S[:, h, :], in0=S[:, h, :], scalar=sc4[:, 6 + c, h:h + 1],
                                                                     in1=ps[:, h, 256:384], op0=ALU.mult, op1=ALU.add),
                             r=bk(h) + [s4B, SB_[h]], w=[SB_[h]])
                        k.op("act", lambda e: e.activation(out=otm[r0:r1, h, :], in_=ps[r0:r1, h, 128:256], func=AF.Identity), r=bk(h), w=[otB[h]])
                    yield
                for h in H4:
                    k.op("act", lambda e: e.activation(out=junk[:], in_=otm[:, h, :], func=AF.Square, accum_out=ss[:, h:h + 1]),
                         r=[otB[h]], w=[jkB, s4B])
                k.op("dve", lambda e: e.tensor_scalar(out=rs[:, 0:4], in0=ss[:, 0:4], scalar1=float(1.0 / 128), scalar2=1e-6,
                                                      op0=ALU.mult, op1=ALU.add), r=[s4B], w=[s4B])
                k.op("act", lambda e: e.activation(out=rs[:, 0:4], in_=rs[:, 0:4], func=AF.Sqrt), r=[s4B], w=[s4B])
                k.op("dve", lambda e: e.reciprocal(out=rs[:, 0:4], in_=rs[:, 0:4]), r=[s4B], w=[s4B])
                yield
                for h in H4:
                    k.op("dve", lambda e: e.scalar_tensor_tensor(out=otm[:, h, :], in0=otm[:, h, :], scalar=rs[:, h:h + 1], in1=dng[:],
                                                                 op0=ALU.mult, op1=ALU.mult), r=[otB[h], s4B, gpB], w=[otB[h]])
                k.op("dve", lambda e: e.tensor_tensor(out=dn[:], in0=otm[:].rearrange("p h d -> p (h d)"), in1=sz[:], op=ALU.mult),
                     r=otB + [szB], w=[dnB])
                yield
                pbf = ps[:, 0, :].bitcast(BF16)
                for j in range(4):
                    k.op("pe", lambda e: e.transpose(pbf[:, j * 128:(j + 1) * 128], dn[:, j * 128:(j + 1) * 128], idb[:]), r=[dnB, cstB], w=[psB[0]])
                k.op("act", lambda e: e.activation(out=catT[:, 4:8, :].rearrange("p a t -> p (a t)"), in_=pbf[:, 0:512], func=AF.Identity),
                     r=[psB[0]], w=[catB])
                yield

            def epilogue(t):
                xa, xB1 = xts[t % 2], xtB[t % 2]
                b0, dB = 2, [psB[2], psB[3]]
                for hf in range(2):
                    for fc in range(8):
                        k.op("pe", lambda e: e.matmul(ps[:, b0 + hf, :], lhsT=catT[:, fc, :], rhs=wout[:, fc, hf * 512:(hf + 1) * 512],
                                                      start=(fc == 0), stop=(fc == 7)), r=[catB, wB_], w=dB)
                k.op("dve", lambda e: e.scalar_tensor_tensor(out=xa[:].rearrange("p (a c) -> p a c", a=2),
                                                             in0=xa[:].rearrange("p (a c) -> p a c", a=2), scalar=ALPHA,
                                                             in1=ps[:, b0:b0 + 2, :], op0=ALU.mult, op1=ALU.add), r=dB + [xB1], w=[xB1])
                layer_norm_tile(xa[:], xB1, lnbc, lnB, st6, mv, tmpc, smB)
                k.dma(xs_d[t * 128:(t + 1) * 128, :], xa[:], r=[xB1], w=[xsB[t]])

            for t in range(NT):
                prologue(t)
                ga = attn_path(t)
                gg_ = gdn_path(t)
                alive = [ga, gg_]
                if NOINTER:
                    for _ in ga:
                        pass
                    for _ in gg_:
                        pass
                    alive = []
                while alive:
                    for g in list(alive):
                        try:
                            next(g)
                        except StopIteration:
                            alive.remove(g)
                epilogue(t)

        for b in range(2):
            ffn_phase(b, 0)
            if stage >= 2:
                mixer_phase(b)
            if stage >= 3:
                ffn_phase(b, 1)
        if stage < 3:
            with nc.sbuf_tensor("dbgt", [128, D], F32) as dbg:
                dB_ = Bf()
                for t in range(NT):
                    k.dma(dbg[:], xs_d[t * 128:(t + 1) * 128, :], r=[xsB[t]], w=[dB_])
                    k.dma(out_d[1, t * 128:(t + 1) * 128, :], dbg[:], r=[dB_], w=[outB[NT + t]])
        k.barrier()
    return nc


_PERM = None


def _perm():
    a_q = np.arange(0, 512); a_k = np.arange(512, 576); a_v = np.arange(576, 640)
    i_q = np.arange(640, 1152); i_k = np.arange(1152, 1216); i_w = np.arange(1216, 1224)
    b_q = np.arange(1224, 1736); b_k = np.arange(1736, 2248); b_v = np.arange(2248, 2760)
    b_z = np.arange(2760, 3272); b_a = np.arange(3272, 3276); b_b = np.arange(3276, 3280)
    return np.concatenate([a_q, i_q, a_k, i_k, a_v, i_w, b_a, b_b, b_z, b_q, b_k, b_v])


def make_in_maps(inp):
    f = lambda a: np.ascontiguousarray(np.asarray(a))
    shared = {}
    shared["wada"] = f(np.asarray(inp["w_ada"])[0].reshape(8, 128, 72, 128).transpose(2, 1, 0, 3))
    shared["badaT"] = f(np.asarray(inp["b_ada"])[0].reshape(72, 128).T)
    ffn = ((1, inp["ffn1_w1"], inp["ffn1_w3"], inp["ffn1_w2"]), (2, inp["ffn2_w1"], inp["ffn2_w3"], inp["ffn2_w2"]))
    for i, a1, a3, a2 in ffn:
        for nm, w in (("w1", a1), ("w3", a3)):
            w = np.asarray(w)[0]
            shared["%s_%d" % (nm, i)] = f(w.reshape(8, 128, NCH, 128).transpose(2, 1, 0, 3).reshape(NCH, 128, 1024))
        shared["w2_%d" % i] = f(np.asarray(a2)[0].reshape(NCH, 128, 1024))
    win = np.asarray(inp["w_in"])[0][:, _perm()]
    shared["win"] = f(win.reshape(8, 128, NCOL).transpose(1, 0, 2))
    shared["wout"] = f(np.asarray(inp["w_out"])[0].reshape(8, 128, D).transpose(1, 0, 2))
    shared["lnp"] = f(np.stack([np.asarray(inp[n])[0] for n in ("ln1_g", "ln1_b", "ln2_g", "ln2_b", "ln3_g", "ln3_b")]))
    shared["convT"] = f(np.asarray(inp["conv_w"])[0].reshape(4, 12, 128).transpose(2, 1, 0))
    shared["gdnp"] = f(np.concatenate([np.asarray(inp["a_log"])[0], np.asarray(inp["dt_bias"])[0]]))
    shared["dng"] = f(np.asarray(inp["dn_norm_g"])[0])
    shared["consts"] = make_consts()
    x = np.asarray(inp["x"]); c = np.asarray(inp["c"]); pos = np.asarray(inp["positions"])
    maps = []
    for core in range(8):
        m = dict(shared)
        b0 = 2 * core
        m["x"] = f(x[b0:b0 + 2])
        m["cT"] = f(c[b0:b0 + 2].reshape(2, 8, 128).transpose(2, 1, 0))
        m["pos"] = f(pos[b0:b0 + 2].reshape(2, NT, 128).transpose(2, 0, 1).astype(np.int32))
        maps.append(m)
    return maps


def kernel(**inputs):
    nc = build(STAGE)
    maps = make_in_maps(inputs)
    res = run_bass_kernel_spmd(nc, maps, core_ids=list(range(8)))
    out = np.concatenate([np.asarray(r["out"]) for r in res.results], axis=0)
    return out.astype(np.float32)
```

```python
import numpy as np
from contextlib import ExitStack
import concourse.bass as bass
import concourse.mybir as mybir
from concourse.bass_utils import run_bass_kernel_spmd

F32 = mybir.dt.float32
BF16 = mybir.dt.bfloat16
I32 = mybir.dt.int32
AF = mybir.ActivationFunctionType
ALU = mybir.AluOpType
AX = mybir.AxisListType

D = 1024
SEQ = 2048
NT = 16
DFF = 2816
NCH = 22
GROUPS = [4, 4, 4, 5, 5]
ALPHA = 2.0 ** 0.25
NCOL = 3280
NTM = 1744
NDS = 16
BIG = 1.0e5
NEGFILL = -1.0e30
MASKV = -30000.0
NOINTER = False
PFMAX = 99
STAGE = 99


class Bf:
    __slots__ = ("w", "r", "x")

    def __init__(self, x=False):
        self.w = None
        self.r = {}
        self.x = x


def Bs(n):
    return [Bf() for _ in range(n)]


class K:
    def __init__(self, nc, es):
        self.nc = nc
        self.eng = {"pe": nc.tensor, "dve": nc.vector, "act": nc.scalar, "pool": nc.gpsimd, "sp": nc.sync}
        self.sem = {e: es.enter_context(nc.semaphore("s_" + e)) for e in self.eng}
        self.cnt = {e: 0 for e in self.eng}
        self.seen = {e: {} for e in self.eng}
        self.dsem = [es.enter_context(nc.semaphore("dq%d" % i)) for i in range(NDS)]
        self.dval = [0] * NDS
        self.dnext = 0

    def _deps(self, e, r, w):
        deps = {}

        def add(ev):
            if ev is None:
                return
            key, sem, val = ev
            if key == "pe" and e == "pe":
                return
            if key not in deps or deps[key][1] < val:
                deps[key] = (sem, val)

        for b in r:
            add(b.w)
            if b.x:
                for ek, ev in b.r.items():
                    if ek != e:
                        add(ev)
        for b in w:
            add(b.w)
            for ev in b.r.values():
                add(ev)
        for key, (sem, val) in deps.items():
            if self.seen[e].get(key, 0) < val:
                self.eng[e].wait_ge(sem, val)
                self.seen[e][key] = val

    def _mark(self, e, ev, r, w):
        for b in r:
            b.r[ev[0]] = ev
        for b in w:
            b.w = ev
            b.r = {}

    def op(self, e, fn, r=(), w=()):
        self._deps(e, r, w)
        inst = fn(self.eng[e])
        self.cnt[e] += 1
        inst.then_inc(self.sem[e], 1)
        ev = (e, self.sem[e], self.cnt[e])
        self._mark(e, ev, r, w)
        return ev

    def dma(self, out, in_, r=(), w=(), e="sp"):
        i = self.dnext
        self.dnext = (i + 1) % NDS
        self._deps(e, r, w)
        key = ("d", i)
        if self.dval[i] > self.seen[e].get(key, 0):
            self.eng[e].wait_ge(self.dsem[i], self.dval[i])
            self.seen[e][key] = self.dval[i]
        inst = self.eng[e].dma_start(out=out, in_=in_)
        self.dval[i] += 16
        inst.then_inc(self.dsem[i], 16)
        ev = (key, self.dsem[i], self.dval[i])
        self._mark(e, ev, r, w)
        return ev

    def barrier(self):
        for e in self.eng:
            for o in self.eng:
                if o != e and self.cnt[o] > self.seen[e].get(o, 0):
                    self.eng[e].wait_ge(self.sem[o], self.cnt[o])
                    self.seen[e][o] = self.cnt[o]
            for i in range(NDS):
                key = ("d", i)
                if self.dval[i] > self.seen[e].get(key, 0):
                    self.eng[e].wait_ge(self.dsem[i], self.dval[i])
                    self.seen[e][key] = self.dval[i]


C_ID, C_ONES, C_TRI, C_OBD, C_SEL0, C_SEL1, C_MB1, C_MB2, C_INVF, C_I8, C_POW, C_END = (
    0, 128, 256, 384, 512, 640, 768, 896, 1024, 1056, 1568, 1592)
NBIS = 22


def make_consts():
    c = np.zeros((128, C_END), np.float32)
    idx = np.arange(128)
    same = (idx[:, None] // 64) == (idx[None, :] // 64)
    c[:, C_ID:C_ID + 128] = np.eye(128)
    c[:, C_ONES:C_ONES + 128] = 1.0
    c[:, C_TRI:C_TRI + 128] = (same & (idx[:, None] <= idx[None, :]))
    c[:, C_OBD:C_OBD + 128] = same
    c[:, C_SEL0:C_SEL0 + 128] = (idx[:, None] < 64)
    c[:, C_SEL1:C_SEL1 + 128] = (idx[:, None] >= 64)
    c[:, C_MB1:C_MB1 + 128] = np.where(same & (idx[:, None] > idx[None, :]), 0.0, -BIG)
    c[:, C_MB2:C_MB2 + 128] = np.where(same & (idx[None, :] >= idx[:, None]), 0.0, -BIG)
    inv = (10000.0 ** (-np.arange(0, 64, 2, dtype=np.float32) / np.float32(64))).astype(np.float32)
    c[:, C_INVF:C_INVF + 32] = inv[None, :]
    c[:, C_I8:C_I8 + 512] = np.tile(np.eye(128, dtype=np.float32), (1, 4))
    c[:, C_POW:C_POW + 24] = (0.5 ** np.arange(1, 25, dtype=np.float64))[None, :]
    return c


def build(stage=99):
    nc = bass.Bass("TRN2", target_bir_lowering=False)
    dt = lambda n, s, d=F32, kind="ExternalInput": nc.dram_tensor(n, s, d, kind=kind).ap()
    x_d = dt("x", [2, SEQ, D])
    cT_d = dt("cT", [128, 8, 2])
    pos_d = dt("pos", [128, 2, NT], I32)
    wada_d = dt("wada", [72, 128, 8, 128])
    bada_d = dt("badaT", [128, 72])
    fw = []
    for i in (1, 2):
        fw.append((dt("w1_%d" % i, [NCH, 128, 1024]), dt("w3_%d" % i, [NCH, 128, 1024]), dt("w2_%d" % i, [NCH, 128, 1024])))
    win_d = dt("win", [128, 8, NCOL])
    wout_d = dt("wout", [128, 8, D])
    lnp_d = dt("lnp", [6, D])
    conv_d = dt("convT", [128, 12, 4])
    gdnp_d = dt("gdnp", [8])
    dng_d = dt("dng", [128])
    const_d = dt("consts", [128, C_END])
    out_d = dt("out", [2, SEQ, D], kind="ExternalOutput")
    xs_d = dt("xs", [SEQ, D], kind="Internal")

    with ExitStack() as es:
        k = K(nc, es)
        uid = [0]

        def _alloc(stack, n, s, d):
            uid[0] += 1
            return stack.enter_context(nc.sbuf_tensor("%s_%d" % (n, uid[0]), s, d))
        sb = lambda n, s, d=F32: _alloc(es, n, s, d)
        ps = es.enter_context(nc.psum_tensor("ps", [128, 8, 512], F32))
        cst = sb("cst", [128, C_END])
        cstB = Bf()
        idb = sb("idb", [128, 128], BF16)
        i8b = sb("i8b", [128, 512], BF16)
        modT = sb("modT", [128, 72, 2])
        modB = Bf()
        xsB = Bs(NT)
        outB = Bs(2 * NT)
        psB = [Bf(True) for _ in range(8)]
        ident = cst[:, C_ID:C_ID + 128]

        k.dma(cst[:], const_d, w=[cstB])
        k.op("pool", lambda e: e.tensor_copy(out=idb[:], in_=cst[:, C_ID:C_ID + 128]), r=[cstB], w=[cstB])
        k.op("pool", lambda e: e.tensor_copy(out=i8b[:], in_=cst[:, C_I8:C_I8 + 512]), r=[cstB], w=[cstB])

        with ExitStack() as es2:
            sb2 = lambda n, s, d=F32: _alloc(es2, n, s, d)
            wsl = [sb2("wsl%d" % i, [128, 8, 1024]) for i in range(2)]
            wslB = Bs(2)
            scT = sb2("scT", [128, 8, 2])
            bad = sb2("bad", [128, 72])
            scB = Bf()
            k.dma(scT[:], cT_d, w=[scB])
            k.dma(bad[:], bada_d, w=[scB])
            k.op("act", lambda e: e.activation(out=scT[:], in_=scT[:], func=AF.Silu), r=[scB], w=[scB])
            for slab in range(9):
                s = slab % 2
                k.dma(wsl[s][:], wada_d[slab * 8:(slab + 1) * 8].rearrange("j p k f -> p j (k f)"), w=[wslB[s]])
                for jj in range(8):
                    j = slab * 8 + jj
                    for kc in range(8):
                        k.op("pe", lambda e: e.matmul(ps[:, 0, 2 * j:2 * j + 2], lhsT=wsl[s][:, jj, kc * 128:(kc + 1) * 128],
                                                      rhs=scT[:, kc, :], start=(kc == 0), stop=(kc == 7)),
                             r=[wslB[s], scB], w=[psB[0]])
            for b in range(2):
                k.op("dve", lambda e: e.tensor_tensor(out=modT[:, :, b], in0=ps[:, 0, b:144:2], in1=bad[:], op=ALU.add),
                     r=[psB[0], scB], w=[modB])
            for v in (1, 4, 7):
                k.op("dve", lambda e: e.tensor_scalar(out=modT[:, v * 8:v * 8 + 8, :], in0=modT[:, v * 8:v * 8 + 8, :],
                                                      scalar1=1.0, scalar2=None, op0=ALU.add), r=[modB], w=[modB])
            for v in (2, 8):
                k.op("dve", lambda e: e.tensor_scalar(out=modT[:, v * 8:v * 8 + 8, :], in0=modT[:, v * 8:v * 8 + 8, :],
                                                      scalar1=0.5, scalar2=None, op0=ALU.mult), r=[modB], w=[modB])
            k.barrier()

        def gen_gate_row(v, b, gbc, gbcB, colbc, colB):
            for kc in range(8):
                k.op("dve", lambda e: e.tensor_scalar(out=colbc[:], in0=cst[:, C_ONES:C_ONES + 128],
                                                      scalar1=modT[:, v * 8 + kc, b:b + 1], scalar2=None, op0=ALU.mult),
                     r=[modB, cstB], w=[colB])
                bank = 6 + kc // 4
                k.op("pe", lambda e: e.matmul(ps[:, bank, (kc % 4) * 128:(kc % 4 + 1) * 128], lhsT=colbc[:], rhs=ident,
                                              start=True, stop=True), r=[colB, cstB], w=[psB[bank]])
            for hb in range(2):
                k.op("act", lambda e: e.activation(out=gbc[:, hb * 512:(hb + 1) * 512], in_=ps[:, 6 + hb, :], func=AF.Identity),
                     r=[psB[6 + hb]], w=[gbcB])

        def layer_norm_tile(xt_ap, xB, lnbc, lnB, st6, mv, tmpc, smB):
            for hh in range(2):
                k.op("dve", lambda e: e.bn_stats(out=st6[:, hh, :], in_=xt_ap[:, hh * 512:(hh + 1) * 512]), r=[xB], w=[smB])
            k.op("dve", lambda e: e.bn_aggr(out=mv[:], in_=st6[:].rearrange("p a b -> p (a b)")), r=[smB], w=[smB])
            k.op("dve", lambda e: e.tensor_scalar(out=tmpc[:], in0=mv[:, 1:2], scalar1=1e-5, scalar2=None, op0=ALU.add),
                 r=[smB], w=[smB])
            k.op("act", lambda e: e.activation(out=tmpc[:], in_=tmpc[:], func=AF.Sqrt), r=[smB], w=[smB])
            k.op("dve", lambda e: e.reciprocal(out=tmpc[:], in_=tmpc[:]), r=[smB], w=[smB])
            k.op("dve", lambda e: e.tensor_scalar(out=xt_ap, in0=xt_ap, scalar1=mv[:, 0:1], scalar2=tmpc[:, 0:1],
                                                  op0=ALU.subtract, op1=ALU.mult), r=[smB, xB], w=[xB])
            k.op("dve", lambda e: e.tensor_tensor(out=xt_ap, in0=xt_ap, in1=lnbc[:, 0, :], op=ALU.mult), r=[xB, lnB], w=[xB])
            k.op("pool", lambda e: e.tensor_tensor(out=xt_ap, in0=xt_ap, in1=lnbc[:, 1, :], op=ALU.add), r=[xB, lnB], w=[xB])

        def ffn_phase(b, which):
            v0 = 0 if which == 0 else 6
            w1d, w3d, w2d = fw[which]
            lnrow = 0 if which == 0 else 4
            with ExitStack() as es2:
                sb2 = lambda n, s, d=F32: _alloc(es2, n, s, d)
                xres = sb2("xres", [128, NT, D])
                xB = Bs(NT)
                uT = sb2("uT", [128, 8, SEQ], BF16)
                uB = Bs(NT)
                wb = [[sb2("wg%d_%d" % (s, m), [128, 5, 1024], BF16) for m in range(3)] for s in range(2)]
                wB = [[Bs(3) for _ in range(5)] for _ in range(2)]
                stg = [sb2("stg%d" % i, [128, 1024]) for i in range(3)]
                stgB = Bs(3)
                gbc = sb2("gbc", [128, 1024])
                gbcB = Bf()
                colbc = sb2("colbc", [128, 128])
                colB = Bf()
                lnbc = sb2("lnbc", [128, 2, D])
                lnB = Bf()
                sT = [sb2("sT%d" % i, [128, 256], BF16) for i in range(3)]
                sTB = Bs(3)
                gT = [sb2("gT%d" % i, [128, 256], BF16) for i in range(3)]
                gTB = Bs(3)
                HBK = (0, 1, 6)
                st6 = sb2("st6", [128, 2, 6])
                mv = sb2("mv", [128, 2])
                tmpc = sb2("tmpc", [128, 1])
                smB = Bf()
                poB = Bs(2)

                gen_gate_row(v0 + 2, b, gbc, gbcB, colbc, colB)
                for i in range(2):
                    k.dma(lnbc[:, i, :], lnp_d[lnrow + i, :].partition_broadcast(128), w=[lnB])
                nstg = [0]

                def load_chunk(cg, slot, ci):
                    for m, src in enumerate((w1d, w3d, w2d)):
                        s = nstg[0] % 3
                        nstg[0] += 1
                        k.dma(stg[s][:], src[cg], w=[stgB[s]])
                        if m < 2:
                            k.op("act", lambda e: e.activation(out=wb[slot][m][:, ci, :], in_=stg[s][:], func=AF.Identity),
                                 r=[stgB[s]], w=[wB[slot][ci][m]])
                        else:
                            k.op("pool", lambda e: e.tensor_tensor(out=wb[slot][m][:, ci, :], in0=stg[s][:], in1=gbc[:], op=ALU.mult),
                                 r=[stgB[s], gbcB], w=[wB[slot][ci][m]])

                gstart = [sum(GROUPS[:g]) for g in range(len(GROUPS))]
                def load_tile(t):
                    if which == 0:
                        k.dma(xres[:, t, :], x_d[b, t * 128:(t + 1) * 128, :], w=[xB[t]])
                    else:
                        k.dma(xres[:, t, :], xs_d[t * 128:(t + 1) * 128, :], r=[xsB[t]], w=[xB[t]])

                for ci in range(GROUPS[0]):
                    load_chunk(ci, 0, ci)
                    load_tile(2 * ci)
                    load_tile(2 * ci + 1)
                for t in range(2 * GROUPS[0], NT):
                    load_tile(t)

                def prep_round(t, rnd):
                    bank = 7
                    if True:
                        for kc in range(4 * rnd, 4 * rnd + 4):
                            k.op("pe", lambda e: e.transpose(ps[:, bank, (kc % 4) * 128:(kc % 4 + 1) * 128],
                                                             xres[:, t, kc * 128:(kc + 1) * 128], ident),
                                 r=[xB[t], cstB], w=[psB[bank]])
                        for kc in range(4 * rnd, 4 * rnd + 4):
                            k.op("act", lambda e: e.activation(out=uT[:, kc, t * 128:(t + 1) * 128],
                                                               in_=ps[:, bank, (kc % 4) * 128:(kc % 4 + 1) * 128], func=AF.Identity,
                                                               scale=modT[:, (v0 + 1) * 8 + kc, b:b + 1], bias=modT[:, v0 * 8 + kc, b:b + 1]),
                                 r=[psB[bank], modB], w=[uB[t]])

                def finish_tile(t):
                    layer_norm_tile(xres[:, t, :], xB[t], lnbc, lnB, st6, mv, tmpc, smB)
                    if which == 0:
                        k.dma(xs_d[t * 128:(t + 1) * 128, :], xres[:, t, :], r=[xB[t]], w=[xsB[t]])
                    else:
                        k.dma(out_d[b, t * 128:(t + 1) * 128, :], xres[:, t, :], r=[xB[t]], w=[outB[b * NT + t]])

                pending = []
                step = 0
                for tt_ in range(2):
                    for rnd_ in range(2):
                        prep_round(tt_, rnd_)
                for g, gs in enumerate(GROUPS):
                    slot = g % 2
                    for blk in range(8):
                        if g + 1 < len(GROUPS) and blk < GROUPS[g + 1]:
                            load_chunk(gstart[g + 1] + blk, 1 - slot, blk)
                        for ci in range(gs):
                            if g == 0 and ci < 4 and blk + 1 < 8:
                                prep_round(2 * blk + 2 + ci // 2, ci % 2)
                            hi = step % 3
                            hb = HBK[hi]
                            step += 1
                            for m in range(2):
                                for kc in range(8):
                                    k.op("pe", lambda e: e.matmul(ps[:, hb, m * 256:(m + 1) * 256],
                                                                  lhsT=wb[slot][m][:, ci, kc * 128:(kc + 1) * 128],
                                                                  rhs=uT[:, kc, blk * 256:(blk + 1) * 256], start=(kc == 0), stop=(kc == 7)),
                                         r=[uB[2 * blk], uB[2 * blk + 1], wB[slot][ci][m]], w=[psB[hb]])
                            k.op("act", lambda e: e.activation(out=sT[hi][:], in_=ps[:, hb, 0:256], func=AF.Silu),
                                 r=[psB[hb]], w=[sTB[hi]])
                            k.op("dve", lambda e: e.tensor_tensor(out=gT[hi][:], in0=sT[hi][:], in1=ps[:, hb, 256:512], op=ALU.mult),
                                 r=[sTB[hi], psB[hb]], w=[gTB[hi]])
                            while len(pending) >= 2:
                                pending.pop(0)()

                            def w2_step(hb=hi, ci=ci, slot=slot, gs=gs, blk=blk, g=g):
                                for tt in range(2):
                                    for hf in range(2):
                                        k.op("pe", lambda e: e.matmul(ps[:, 2 + 2 * tt + hf, :], lhsT=gT[hb][:, tt * 128:(tt + 1) * 128],
                                                                      rhs=wb[slot][2][:, ci, hf * 512:(hf + 1) * 512],
                                                                      start=(ci == 0), stop=(ci == gs - 1)),
                                             r=[gTB[hb], wB[slot][ci][2]], w=[poB[tt]])
                                if ci == gs - 1:
                                    for tt in range(2):
                                        t = 2 * blk + tt
                                        xv = xres[:, t, :].rearrange("p (a c) -> p a c", a=2)
                                        if g == 0:
                                            k.op("dve", lambda e: e.scalar_tensor_tensor(out=xv, in0=xv, scalar=ALPHA, in1=ps[:, 2 + 2 * tt:4 + 2 * tt, :],
                                                                                         op0=ALU.mult, op1=ALU.add), r=[poB[tt], xB[t]], w=[xB[t]])
                                        else:
                                            k.op("dve", lambda e: e.tensor_tensor(out=xv, in0=xv, in1=ps[:, 2 + 2 * tt:4 + 2 * tt, :], op=ALU.add),
                                                 r=[poB[tt], xB[t]], w=[xB[t]])
                                        if g == len(GROUPS) - 1:
                                            finish_tile(t)
                            pending.append(w2_step)
                for fn in pending:
                    fn()
                k.barrier()

        def mixer_phase(b):
            with ExitStack() as es2:
                sb2 = lambda n, s, d=F32: _alloc(es2, n, s, d)
                win = sb2("win", [128, 8, NCOL], BF16)
                wout = sb2("wout", [128, 8, D], BF16)
                wB_ = Bf()
                with ExitStack() as es3:
                    sb3 = lambda n, s, d=F32: _alloc(es3, n, s, d)
                    stg = [sb3("mstg%d" % i, [128, 1024]) for i in range(3)]
                    stgB = Bs(3)
                    gbc = sb3("mgbc", [128, 1024])
                    gbcB = Bf()
                    colbc = sb3("mcolbc", [128, 128])
                    colB = Bf()
                    gen_gate_row(5, b, gbc, gbcB, colbc, colB)
                    n = 0
                    for kc in range(8):
                        for c0 in range(0, NCOL, 1024):
                            cw = min(1024, NCOL - c0)
                            s = n % 3
                            n += 1
                            k.dma(stg[s][:, 0:cw], win_d[:, kc, c0:c0 + cw], w=[stgB[s]])
                            k.op("pool", lambda e: e.tensor_copy(out=win[:, kc, c0:c0 + cw], in_=stg[s][:, 0:cw]), r=[stgB[s]], w=[wB_])
                        s = n % 3
                        n += 1
                        k.dma(stg[s][:], wout_d[:, kc, :], w=[stgB[s]])
                        k.op("pool", lambda e: e.tensor_tensor(out=wout[:, kc, :], in0=stg[s][:], in1=gbc[:], op=ALU.mult),
                             r=[stgB[s], gbcB], w=[wB_])
                    k.barrier()
                mixer_tiles(b, win, wout, wB_, sb2)
                k.barrier()

        def mixer_tiles(b, win, wout, wB_, sb2):
            cosT = sb2("cosT", [128, NT, 32])
            sinT = sb2("sinT", [128, NT, 32])
            csB = Bf()
            with ExitStack() as es4:
                sb4 = lambda n, s, d=F32: _alloc(es4, n, s, d)
                posi = sb4("posi", [128, NT], I32)
                posf = sb4("posf", [128, NT])
                ang = sb4("ang", [128, NT, 32])
                angi = sb4("angi", [128, NT, 32], I32)
                angf = sb4("angf", [128, NT, 32])
                angm = sb4("angm", [128, NT, 32])
                k.dma(posi[:], pos_d[:, b, :], w=[csB])
                k.op("dve", lambda e: e.tensor_copy(out=posf[:], in_=posi[:]), r=[csB], w=[csB])
                k.op("dve", lambda e: e.tensor_tensor(out=ang[:], in0=posf[:].unsqueeze(2).to_broadcast([128, NT, 32]),
                                                      in1=cst[:, C_INVF:C_INVF + 32].unsqueeze(1).to_broadcast([128, NT, 32]), op=ALU.mult),
                     r=[csB, cstB], w=[csB])
                TWO_PI = 2.0 * np.pi
                C1 = 6.28125
                C2 = float(TWO_PI - C1)

                def reduce_sin(dst, shift):
                    k.op("dve", lambda e: e.tensor_scalar(out=angf[:], in0=ang[:], scalar1=float(shift), scalar2=float(1.0 / TWO_PI),
                                                          op0=ALU.add, op1=ALU.mult), r=[csB], w=[csB])
                    k.op("dve", lambda e: e.tensor_copy(out=angi[:], in_=angf[:]), r=[csB], w=[csB])
                    k.op("dve", lambda e: e.tensor_copy(out=angf[:], in_=angi[:]), r=[csB], w=[csB])
                    k.op("dve", lambda e: e.scalar_tensor_tensor(out=angm[:], in0=angf[:], scalar=-C1, in1=ang[:], op0=ALU.mult, op1=ALU.add),
                         r=[csB], w=[csB])
                    k.op("dve", lambda e: e.scalar_tensor_tensor(out=angm[:], in0=angf[:], scalar=-C2, in1=angm[:], op0=ALU.mult, op1=ALU.add),
                         r=[csB], w=[csB])
                    if shift != 0.0:
                        k.op("dve", lambda e: e.tensor_scalar(out=angm[:], in0=angm[:], scalar1=float(shift), scalar2=None, op0=ALU.add),
                             r=[csB], w=[csB])
                    k.op("dve", lambda e: e.tensor_scalar(out=angf[:], in0=angm[:], scalar1=float(np.pi), scalar2=-TWO_PI,
                                                          op0=ALU.is_gt, op1=ALU.mult), r=[csB], w=[csB])
                    k.op("dve", lambda e: e.tensor_tensor(out=angm[:], in0=angm[:], in1=angf[:], op=ALU.add), r=[csB], w=[csB])
                    k.op("dve", lambda e: e.tensor_scalar(out=angf[:], in0=angm[:], scalar1=float(-np.pi), scalar2=TWO_PI,
                                                          op0=ALU.is_lt, op1=ALU.mult), r=[csB], w=[csB])
                    k.op("dve", lambda e: e.tensor_tensor(out=angm[:], in0=angm[:], in1=angf[:], op=ALU.add), r=[csB], w=[csB])
                    k.op("dve", lambda e: e.tensor_scalar(out=angm[:], in0=angm[:], scalar1=float(-np.pi), scalar2=float(np.pi),
                                                          op0=ALU.max, op1=ALU.min), r=[csB], w=[csB])
                    k.op("act", lambda e: e.activation(out=dst[:], in_=angm[:], func=AF.Sin), r=[csB], w=[csB])

                reduce_sin(sinT, 0.0)
                reduce_sin(cosT, float(np.pi / 2))
                k.barrier()

            lnbc = sb2("mlnbc", [128, 2, D])
            lnB = Bf()
            kT = sb2("kT", [64, SEQ], BF16)
            kiT = sb2("kiT", [64, SEQ], BF16)
            vaug = sb2("vaug", [128, NT, 65], BF16)
            kvBs = Bs(NT)
            xts = [sb2("xt%d" % i, [128, D]) for i in range(2)]
            xtB = Bs(2)
            uTt = sb2("uTt", [128, 8, 128], BF16)
            uB = Bf()
            roped = sb2("roped", [128, 18, 64], BF16)
            ropB = Bf()
            qTs = [sb2("qT%d" % i, [64, 8, 128], BF16) for i in range(2)]
            qBs = Bs(2)
            qiT = sb2("qiT", [64, 8, 128], BF16)
            qiB = Bf()
            absw = sb2("absw", [128, 8])
            sgn = sb2("sgn", [128, 8])
            awB = Bf()
            score = sb2("score", [128, SEQ])
            scB = Bf()
            work = sb2("work", [128, SEQ])
            wkB = Bf()
            tok = work[:, 0:NTM]
            tokB = wkB
            mbias = sb2("mbias", [128, SEQ], BF16)
            mbB = Bf()
            bs = sb2("bs", [128, 8])
            wkt = sb2("wkt", [128, 24])
            m8B = Bf()
            rel = [sb2("rel%d" % i, [128, 512]) for i in range(3)]
            relB = Bs(3)
            PT = [sb2("PT%d" % i, [128, 512], BF16) for i in range(3)]
            PTB = Bs(3)
            rec = sb2("rec", [128, 8])
            attn = sb2("attn", [128, 8, 64], BF16)
            atB = Bf()
            catT = sb2("catT", [128, 8, 128], BF16)
            catB = Bf()
            st6 = sb2("mst6", [128, 2, 6])
            mv = sb2("mmv", [128, 2])
            tmpc = sb2("mtmpc", [128, 1])
            smB = Bf()
            xc = sb2("xc", [128, 12, 131])
            xcB = Bf()
            tm = sb2("tm", [128, 1536])
            tmB = Bf()
            junk = sb2("junk", [128, 128])
            jkB = Bf()
            ss = sb2("ss", [128, 8])
            rs = sb2("rs", [128, 8])
            sc4 = sb2("sc4", [128, 16, 4])
            s4B = Bf()
            gg = sb2("gg", [128, 16])
            gdnp = sb2("gdnp", [128, 8])
            negA = sb2("negA", [128, 4])
            dng = sb2("dng", [128, 128])
            convw = sb2("convw", [128, 12, 4])
            gpB = Bf()
            hd = [sb2("hd%d" % h, [128, 6, 128]) for h in range(4)]
            hdB = [Bs(6) for _ in range(4)]
            ycv = lambda cc: hd[2 + cc // 6][:, cc % 6, :]
            ycB = lambda cc: hdB[2 + cc // 6][cc % 6]
            rA = tm[:, 0:576].rearrange("p (h d) -> p h d", d=32)
            rBt = tm[:, 576:1152].rearrange("p (h d) -> p h d", d=32)
            rpB = [tmB]
            kd = [sb2("kd%d" % h, [128, 128], BF16) for h in range(4)]
            kdB = Bs(4)
            T3 = [sb2("T3%d" % h, [128, 3, 128], BF16) for h in range(4)]
            T3B = Bs(4)
            AN = [[sb2("AN%d_%d" % (h, i), [128, 2, 128]) for i in range(2)] for h in range(4)]
            ANB = [Bs(2) for _ in range(4)]
            Mm = [[sb2("Mm%d_%d" % (h, i), [128, 128]) for i in range(2)] for h in range(4)]
            MB = [Bs(2) for _ in range(4)]
            aqk = [sb2("aqk%d" % h, [128, 128], BF16) for h in range(4)]
            aqB = Bs(4)
            U = [sb2("U%d" % h, [128, 128]) for h in range(4)]
            UB = Bs(4)
            WT = [sb2("WT%d" % h, [128, 128], BF16) for h in range(4)]
            WTB = Bs(4)
            dl = [sb2("dl%d" % h, [128, 128], BF16) for h in range(4)]
            dlB = Bs(4)
            S = sb2("S", [128, 4, 128])
            Sb = sb2("Sb", [128, 4, 128], BF16)
            SB_ = Bs(4)
            SbB = Bs(4)
            otm = sb2("otm", [128, 4, 128])
            otB = Bs(4)
            sz = sb2("sz", [128, 512])
            szB = Bf()
            dn = sb2("dn", [128, 512], BF16)
            dnB = Bf()

            dbank = [0]

            def dbl():
                i = dbank[0] % 2
                dbank[0] += 1
                b0 = (0, 2)[i]
                return b0, [psB[b0], psB[b0 + 1]]

            for i in range(2):
                k.dma(lnbc[:, i, :], lnp_d[2 + i, :].partition_broadcast(128), w=[lnB])
            k.dma(gdnp[:], gdnp_d.partition_broadcast(128), w=[gpB])
            k.dma(dng[:], dng_d.partition_broadcast(128), w=[gpB])
            k.dma(convw[:], conv_d, w=[gpB])
            k.op("act", lambda e: e.activation(out=negA[:], in_=gdnp[:, 0:4], func=AF.Exp), r=[gpB], w=[gpB])
            k.op("dve", lambda e: e.tensor_scalar(out=negA[:], in0=negA[:], scalar1=-1.0, scalar2=None, op0=ALU.mult), r=[gpB], w=[gpB])
            k.op("pool", lambda e: e.memset(S[:], 0.0), w=SB_)
            k.op("pool", lambda e: e.memset(Sb[:], 0.0), w=SbB)
            k.op("pool", lambda e: e.memset(xc[:], 0.0), w=[xcB])
            k.op("pool", lambda e: e.memset(vaug[:], 1.0), w=kvBs)
            WC = float(8 ** -0.5 * 64 ** -0.5)

            def p1(t):
                xa, xB1 = xts[t % 2], xtB[t % 2]
                k.dma(xa[:], xs_d[t * 128:(t + 1) * 128, :], r=[xsB[t]], w=[xB1])
                b0, dB = 4, [psB[4], psB[5]]
                for kc in range(8):
                    bank = b0 + kc // 4
                    k.op("pe", lambda e: e.transpose(ps[:, bank, (kc % 4) * 128:(kc % 4 + 1) * 128], xa[:, kc * 128:(kc + 1) * 128], ident),
                         r=[xB1, cstB], w=dB)
                for kc in range(8):
                    bank = b0 + kc // 4
                    k.op("act", lambda e: e.activation(out=uTt[:, kc, :], in_=ps[:, bank, (kc % 4) * 128:(kc % 4 + 1) * 128],
                                                       func=AF.Identity, scale=modT[:, 32 + kc, b:b + 1], bias=modT[:, 24 + kc, b:b + 1]),
                         r=dB + [modB], w=[uB])
                yield
                for (c0, c1) in ((0, 1024), (1024, NTM)):
                    b0, dB = 4, [psB[4], psB[5]]
                    for s0 in range(c0, c1, 512):
                        s1 = min(s0 + 512, c1)
                        bank = b0 + (s0 - c0) // 512
                        for kc in range(8):
                            k.op("pe", lambda e: e.matmul(ps[:, bank, 0:s1 - s0], lhsT=uTt[:, kc, :], rhs=win[:, kc, s0:s1],
                                                          start=(kc == 0), stop=(kc == 7)), r=[uB, wB_], w=dB)
                        k.op("act", lambda e: e.activation(out=tok[:, s0:s1], in_=ps[:, bank, 0:s1 - s0], func=AF.Identity),
                             r=dB, w=[tokB])
                    yield
                tk = tok[:, 0:1152].rearrange("p (h d) -> p h d", d=64)
                cb = cosT[:, t, :].unsqueeze(1).to_broadcast([128, 18, 32])
                sbb = sinT[:, t, :].unsqueeze(1).to_broadcast([128, 18, 32])
                k.op("dve", lambda e: e.tensor_tensor(out=rA, in0=tk[:, :, 0:32], in1=cb, op=ALU.mult), r=[tokB, csB], w=rpB)
                k.op("dve", lambda e: e.tensor_tensor(out=rBt, in0=tk[:, :, 32:64], in1=sbb, op=ALU.mult), r=[tokB, csB], w=rpB)
                k.op("dve", lambda e: e.tensor_tensor(out=roped[:, :, 0:32], in0=rA, in1=rBt, op=ALU.subtract), r=rpB, w=[ropB])
                k.op("dve", lambda e: e.tensor_tensor(out=rA, in0=tk[:, :, 32:64], in1=cb, op=ALU.mult), r=[tokB, csB, ropB], w=rpB)
                k.op("dve", lambda e: e.tensor_tensor(out=rBt, in0=tk[:, :, 0:32], in1=sbb, op=ALU.mult), r=[tokB, csB], w=rpB)
                k.op("dve", lambda e: e.tensor_tensor(out=roped[:, :, 32:64], in0=rA, in1=rBt, op=ALU.add), r=rpB, w=[ropB])
                yield
                k.op("act", lambda e: e.activation(out=vaug[:, t, 0:64], in_=tok[:, 1152:1216], func=AF.Identity), r=[tokB], w=[kvBs[t]])
                k.op("act", lambda e: e.activation(out=sgn[:], in_=tok[:, 1216:1224], func=AF.Sign), r=[tokB], w=[awB])
                k.op("dve", lambda e: e.scalar_tensor_tensor(out=absw[:], in0=tok[:, 1216:1224], scalar=WC, in1=sgn[:],
                                                             op0=ALU.mult, op1=ALU.mult), r=[tokB, awB], w=[awB])
                b0, dB = 4, [psB[4], psB[5]]
                pbf = ps[:, b0:b0 + 2, :].bitcast(BF16)
                for h in range(16):
                    k.op("pe", lambda e: e.transpose(pbf[0:64, h // 8, (h % 8) * 128:(h % 8 + 1) * 128], roped[:, h, :], idb[:]),
                         r=[ropB, cstB], w=dB)
                k.op("act", lambda e: e.activation(out=qTs[t % 2][:].rearrange("p h t -> p (h t)"), in_=pbf[0:64, 0, :], func=AF.Identity),
                     r=dB, w=[qBs[t % 2]])
                k.op("act", lambda e: e.activation(out=qiT[:].rearrange("p h t -> p (h t)"), in_=pbf[0:64, 1, :], func=AF.Identity),
                     r=dB, w=[qiB])
                yield
                b0, dB = 4, [psB[4], psB[5]]
                pbf2 = ps[:, b0, :].bitcast(BF16)
                for h in range(2):
                    k.op("pe", lambda e: e.transpose(pbf2[0:64, h * 128:(h + 1) * 128], roped[:, 16 + h, :], idb[:]),
                         r=[ropB, cstB], w=dB)
                k.op("act", lambda e: e.activation(out=kT[:, t * 128:(t + 1) * 128], in_=pbf2[0:64, 0:128], func=AF.Identity), r=dB, w=[kvBs[t]])
                k.op("act", lambda e: e.activation(out=kiT[:, t * 128:(t + 1) * 128], in_=pbf2[0:64, 128:256], func=AF.Identity), r=dB, w=[kvBs[t]])

            def gdn_pro(t):
                k.op("dve", lambda e: e.tensor_tensor(out=sc4[:, 0, :], in0=tok[:, 1224:1228], in1=gdnp[:, 4:8], op=ALU.add),
                     r=[tokB, gpB], w=[s4B])
                k.op("act", lambda e: e.activation(out=sc4[:, 0, :], in_=sc4[:, 0, :], func=AF.Exp), r=[s4B], w=[s4B])
                k.op("act", lambda e: e.activation(out=sc4[:, 0, :], in_=sc4[:, 0, :], func=AF.Ln, bias=1.0), r=[s4B], w=[s4B])
                k.op("dve", lambda e: e.tensor_tensor(out=sc4[:, 0, :], in0=sc4[:, 0, :], in1=negA[:], op=ALU.mult), r=[s4B, gpB], w=[s4B])
                k.op("act", lambda e: e.activation(out=sc4[:, 1, :], in_=tok[:, 1228:1232], func=AF.Sigmoid), r=[tokB], w=[s4B])
                k.op("act", lambda e: e.activation(out=sz[:], in_=tok[:, 1232:1744], func=AF.Silu), r=[tokB], w=[szB])
                b0, dB = dbl()
                for i, co in enumerate((C_TRI, C_OBD, C_SEL0, C_SEL1)):
                    k.op("pe", lambda e: e.matmul(ps[:, b0, i * 4:i * 4 + 4], lhsT=cst[:, co:co + 128], rhs=sc4[:, 0, :], start=True, stop=True),
                         r=[s4B, cstB], w=dB)
                k.op("dve", lambda e: e.tensor_copy(out=gg[:], in_=ps[:, b0, 0:16]), r=dB, w=[s4B])
                k.op("act", lambda e: e.activation(out=sc4[:, 2, :], in_=gg[:, 0:4], func=AF.Exp), r=[s4B], w=[s4B])
                k.op("dve", lambda e: e.tensor_tensor(out=sc4[:, 8, :], in0=gg[:, 4:8], in1=gg[:, 0:4], op=ALU.subtract), r=[s4B], w=[s4B])
                k.op("act", lambda e: e.activation(out=sc4[:, 3, :], in_=sc4[:, 8, :], func=AF.Exp), r=[s4B], w=[s4B])
                k.op("act", lambda e: e.activation(out=sc4[:, 6, :], in_=gg[:, 8:12], func=AF.Exp), r=[s4B], w=[s4B])
                k.op("act", lambda e: e.activation(out=sc4[:, 7, :], in_=gg[:, 12:16], func=AF.Exp), r=[s4B], w=[s4B])
                k.op("dve", lambda e: e.tensor_tensor(out=sc4[:, 4, :], in0=sc4[:, 1, :], in1=sc4[:, 2, :], op=ALU.mult), r=[s4B], w=[s4B])
                k.op("dve", lambda e: e.tensor_scalar(out=sc4[:, 5, :], in0=sc4[:, 1, :], scalar1=-1.0, scalar2=None, op0=ALU.mult), r=[s4B], w=[s4B])
                yield
                for grp in range(3):
                    b0, dB = dbl()
                    for q4 in range(4):
                        cc = grp * 4 + q4
                        for kc in range(8):
                            k.op("pe", lambda e: e.matmul(ps[:, b0, q4 * 128:(q4 + 1) * 128], lhsT=win[:, kc, NTM + cc * 128:NTM + (cc + 1) * 128],
                                                          rhs=uTt[:, kc, :], start=(kc == 0), stop=(kc == 7)), r=[uB, wB_], w=dB)
                    k.op("act", lambda e: e.activation(out=xc[:, grp * 4:(grp + 1) * 4, 3:131],
                                                       in_=ps[:, b0, :].rearrange("p (a c) -> p a c", a=4), func=AF.Identity),
                         r=dB, w=[xcB])
                    yield
                for cc in range(12):
                    k.op("dve", lambda e: e.tensor_scalar(out=ycv(cc), in0=xc[:, cc, 3:131], scalar1=convw[:, cc, 3:4], scalar2=None,
                                                          op0=ALU.mult), r=[xcB, gpB], w=[ycB(cc)])
                    for j in range(3):
                        k.op("dve", lambda e: e.scalar_tensor_tensor(out=ycv(cc), in0=xc[:, cc, j:j + 128], scalar=convw[:, cc, j:j + 1],
                                                                     in1=ycv(cc), op0=ALU.mult, op1=ALU.add), r=[xcB, gpB, ycB(cc)], w=[ycB(cc)])
                    if cc % 3 == 2:
                        yield
                k.op("pool", lambda e: e.tensor_copy(out=xc[:, :, 0:3], in_=xc[:, :, 128:131]), r=[xcB], w=[xcB])
                for hh in (2, 3):
                    k.op("act", lambda e: e.activation(out=hd[hh][:], in_=hd[hh][:], func=AF.Silu), r=hdB[hh], w=hdB[hh])
                for grp in range(3):
                    b0, dB = dbl()
                    for q4 in range(4):
                        cc = grp * 4 + q4
                        k.op("pe", lambda e: e.transpose(ps[:, b0, q4 * 128:(q4 + 1) * 128], ycv(cc), ident), r=[ycB(cc), cstB], w=dB)
                    k.op("act", lambda e: e.activation(out=tm[:, grp * 512:(grp + 1) * 512], in_=ps[:, b0, :], func=AF.Identity), r=dB, w=[tmB])
                    yield
                for g8 in range(8):
                    k.op("act", lambda e: e.activation(out=junk[:], in_=tm[:, g8 * 128:(g8 + 1) * 128], func=AF.Square,
                                                       accum_out=ss[:, g8:g8 + 1]), r=[tmB], w=[jkB, s4B])
                k.op("dve", lambda e: e.tensor_scalar(out=rs[:], in0=ss[:], scalar1=1e-6, scalar2=None, op0=ALU.add), r=[s4B], w=[s4B])
                k.op("act", lambda e: e.activation(out=rs[:], in_=rs[:], func=AF.Sqrt), r=[s4B], w=[s4B])
                k.op("dve", lambda e: e.reciprocal(out=rs[:], in_=rs[:]), r=[s4B], w=[s4B])
                k.op("dve", lambda e: e.tensor_scalar(out=sc4[:, 9, :], in0=rs[:, 0:4], scalar1=float(128 ** -0.5), scalar2=None, op0=ALU.mult),
                     r=[s4B], w=[s4B])
                k.op("dve", lambda e: e.tensor_tensor(out=sc4[:, 10, :], in0=rs[:, 4:8], in1=sc4[:, 4, :], op=ALU.mult), r=[s4B], w=[s4B])
                k.op("dve", lambda e: e.tensor_tensor(out=sc4[:, 11, :], in0=rs[:, 4:8], in1=sc4[:, 3, :], op=ALU.mult), r=[s4B], w=[s4B])
                k.op("dve", lambda e: e.tensor_tensor(out=sc4[:, 12, :], in0=sc4[:, 9, :], in1=sc4[:, 2, :], op=ALU.mult), r=[s4B], w=[s4B])
                yield

            flags = {"idx": -1, "g1": -1}

            def attn_path(t):
                W = (t + 1) * 128
                if t < 2:
                    flags["idx"] = t
                if t >= 2:
                    nrel = 0
                    for s0 in range(0, W, 512):
                        s1 = min(s0 + 512, W)
                        sw = s1 - s0
                        for h in range(8):
                            ri = nrel % 3
                            rb = 4 + nrel % 4
                            nrel += 1
                            k.op("pe", lambda e: e.matmul(ps[:, rb, 0:sw], lhsT=qiT[:, h, :], rhs=kiT[:, s0:s1], start=True, stop=True),
                                 r=[qiB] + kvBs[s0 // 128:(s1 + 127) // 128], w=[psB[rb]])
                            k.op("act", lambda e: e.activation(out=rel[ri][:, 0:sw], in_=ps[:, rb, 0:sw], func=AF.Relu,
                                                               scale=absw[:, h:h + 1]), r=[psB[rb], awB], w=[relB[ri]])
                            if h == 0:
                                k.op("dve", lambda e: e.tensor_scalar(out=score[:, s0:s1], in0=rel[ri][:, 0:sw], scalar1=sgn[:, 0:1],
                                                                      scalar2=None, op0=ALU.mult), r=[relB[ri], awB], w=[scB])
                            else:
                                k.op("dve", lambda e: e.scalar_tensor_tensor(out=score[:, s0:s1], in0=rel[ri][:, 0:sw], scalar=sgn[:, h:h + 1],
                                                                             in1=score[:, s0:s1], op0=ALU.mult, op1=ALU.add),
                                     r=[relB[ri], awB, scB], w=[scB])
                            if h % 2 == 1:
                                yield
                    flags["idx"] = t
                    k.op("pool", lambda e: e.affine_select(out=score[:, t * 128:W], in_=score[:, t * 128:W], pattern=[[-1, 128]],
                                                           compare_op=ALU.is_ge, fill=NEGFILL, base=0, channel_multiplier=1),
                         r=[scB], w=[scB])
                    k.op("dve", lambda e: e.tensor_reduce(out=bs[:, 0:1], in_=score[:, 0:t * 128], axis=AX.X, op=ALU.min), r=[scB], w=[m8B])
                    k.op("dve", lambda e: e.tensor_reduce(out=bs[:, 1:2], in_=score[:, 0:W], axis=AX.X, op=ALU.max), r=[scB], w=[m8B])
                    k.op("dve", lambda e: e.tensor_tensor(out=bs[:, 2:3], in0=bs[:, 1:2], in1=bs[:, 0:1], op=ALU.subtract), r=[m8B], w=[m8B])
                    k.op("dve", lambda e: e.tensor_scalar(out=wkt[:], in0=cst[:, C_POW:C_POW + 24], scalar1=bs[:, 2:3], scalar2=None, op0=ALU.mult),
                         r=[m8B, cstB], w=[m8B])
                    k.op("dve", lambda e: e.tensor_tensor(out=bs[:, 3:4], in0=bs[:, 0:1], in1=wkt[:, 0:1], op=ALU.add), r=[m8B], w=[m8B])
                    yield
                    for kk in range(NBIS):
                        k.op("dve", lambda e: e.tensor_scalar(out=mbias[:, 0:W], in0=score[:, 0:W], scalar1=bs[:, 3:4], scalar2=0.0,
                                                              op0=ALU.is_ge, op1=ALU.add, accum_out=bs[:, 6:7]), r=[scB, m8B], w=[mbB, m8B])
                        k.op("dve", lambda e: e.scalar_tensor_tensor(out=bs[:, 4:5], in0=bs[:, 6:7], scalar=255.5, in1=wkt[:, kk:kk + 1],
                                                                     op0=ALU.is_ge, op1=ALU.mult), r=[m8B], w=[m8B])
                        k.op("dve", lambda e: e.scalar_tensor_tensor(out=bs[:, 3:4], in0=bs[:, 4:5], scalar=wkt[:, kk + 1:kk + 2], in1=bs[:, 3:4],
                                                                     op0=ALU.subtract, op1=ALU.add), r=[m8B], w=[m8B])
                        yield
                    k.op("dve", lambda e: e.tensor_tensor(out=bs[:, 5:6], in0=bs[:, 3:4], in1=wkt[:, NBIS:NBIS + 1], op=ALU.subtract), r=[m8B], w=[m8B])
                    k.op("dve", lambda e: e.tensor_scalar(out=mbias[:, 0:W], in0=score[:, 0:W], scalar1=bs[:, 5:6], scalar2=MASKV,
                                                          op0=ALU.is_lt, op1=ALU.mult), r=[scB, m8B], w=[mbB])
                else:
                    k.op("pool", lambda e: e.memset(mbias[:, 0:W], 0.0), w=[mbB])
                    k.op("pool", lambda e: e.affine_select(out=mbias[:, t * 128:W], in_=mbias[:, t * 128:W], pattern=[[-1, 128]],
                                                           compare_op=ALU.is_ge, fill=MASKV, base=0, channel_multiplier=1),
                         r=[mbB], w=[mbB])
                yield
                pvB = [psB[6], psB[7]]
                items = [(kb, hf) for kb in range(t + 1) for hf in range(2)]

                def emit_st(i):
                    kb, hf = items[i]
                    sbk = 4 + i % 2
                    pi = i % 3
                    k.op("pe", lambda e: e.matmul(ps[:, sbk, :], lhsT=kT[:, kb * 128:(kb + 1) * 128],
                                                  rhs=qTs[t % 2][:, hf * 4:(hf + 1) * 4, :].rearrange("p h t -> p (h t)"),
                                                  start=True, stop=False), r=[kvBs[kb], qBs[t % 2]], w=[psB[sbk]])
                    k.op("pe", lambda e: e.matmul(ps[:, sbk, :], lhsT=mbias[:, kb * 128:(kb + 1) * 128], rhs=i8b[:],
                                                  start=False, stop=True), r=[mbB, cstB], w=[psB[sbk]])
                    k.op("act", lambda e: e.activation(out=PT[pi][:], in_=ps[:, sbk, :], func=AF.Exp, scale=0.125),
                         r=[psB[sbk]], w=[PTB[pi]])

                def emit_pv(i):
                    kb, hf = items[i]
                    pi = i % 3
                    for hh in range(4):
                        k.op("pe", lambda e: e.matmul(ps[:, 6 + hf, hh * 128:hh * 128 + 65], lhsT=PT[pi][:, hh * 128:(hh + 1) * 128],
                                                      rhs=vaug[:, kb, :], start=(kb == 0 and hh == 0), stop=(kb == t),
                                                      skip_group_check=True), r=[PTB[pi], kvBs[kb]], w=[psB[6 + hf]])

                for i in range(len(items)):
                    emit_st(i)
                    if i >= 1:
                        emit_pv(i - 1)
                    if i % 2 == 1:
                        yield
                emit_pv(len(items) - 1)
                pv = ps[:, 6:8, :].rearrange("p a (h c) -> p (a h) c", c=128)
                k.op("dve", lambda e: e.reciprocal(out=rec[:], in_=pv[:, :, 64]), r=pvB, w=[atB])
                k.op("dve", lambda e: e.tensor_tensor(out=attn[:], in0=pv[:, :, 0:64], in1=rec[:].unsqueeze(2).to_broadcast([128, 8, 64]),
                                                      op=ALU.mult), r=pvB + [atB], w=[atB])
                pbf = ps[:, 4, :].bitcast(BF16)
                for j in range(4):
                    k.op("pe", lambda e: e.transpose(pbf[:, j * 128:(j + 1) * 128], attn[:, 2 * j:2 * j + 2, :].rearrange("p h d -> p (h d)"), idb[:]),
                         r=[atB, cstB], w=[psB[4]])
                k.op("act", lambda e: e.activation(out=catT[:, 0:4, :].rearrange("p a t -> p (a t)"), in_=pbf[:, 0:512], func=AF.Identity),
                     r=[psB[4]], w=[catB])
                yield

            KH, KBG, QS, QD, VB, DG = range(6)

            def gdn_path(t):
                H4 = range(4)
                col = lambda s, h: sc4[:, s, h:h + 1]
                bk = lambda h: [psB[h]]
                for _ in gdn_pro(t):
                    yield
                for h in H4:
                    ksl = tm[:, 512 + h * 128:512 + (h + 1) * 128]
                    qsl = tm[:, h * 128:(h + 1) * 128]
                    vsl = tm[:, 1024 + h * 128:1024 + (h + 1) * 128]
                    for dst, dB_, src, sc_ in ((hd[h][:, KH, :], hdB[h][KH], ksl, rs[:, 4 + h:5 + h]),
                                               (hd[h][:, QS, :], hdB[h][QS], qsl, col(9, h)),
                                               (hd[h][:, QD, :], hdB[h][QD], qsl, col(12, h)),
                                               (hd[h][:, KBG, :], hdB[h][KBG], ksl, col(10, h)),
                                               (kd[h][:], kdB[h], ksl, col(11, h)),
                                               (hd[h][:, VB, :], hdB[h][VB], vsl, col(1, h))):
                        k.op("act", lambda e: e.activation(out=dst, in_=src, func=AF.Identity, scale=sc_), r=[tmB, s4B], w=[dB_])
                    k.op("dve", lambda e: e.tensor_scalar(out=hd[h][:, DG, :], in0=ident, scalar1=gg[:, h:h + 1], scalar2=None, op0=ALU.mult),
                         r=[cstB, s4B], w=[hdB[h][DG]])
                    if h % 2 == 1:
                        yield
                flags["g1"] = t
                for h in H4:
                    for i, src in enumerate((KH, QS, QD)):
                        k.op("pe", lambda e: e.transpose(ps[:, h, i * 128:(i + 1) * 128], hd[h][:, src, :], ident), r=[hdB[h][src], cstB], w=bk(h))
                yield
                for h in H4:
                    k.op("act", lambda e: e.activation(out=T3[h][:].rearrange("p a t -> p (a t)"), in_=ps[:, h, 0:384], func=AF.Identity),
                         r=bk(h), w=[T3B[h]])
                yield
                for h in H4:
                    k.op("pe", lambda e: e.matmul(ps[:, h, 0:128], lhsT=T3[h][:, 0, :], rhs=T3[h][:, 0, :], start=True, stop=True), r=[T3B[h]], w=bk(h))
                    k.op("pe", lambda e: e.matmul(ps[:, h, 128:256], lhsT=T3[h][:, 0, :], rhs=T3[h][:, 1, :], start=True, stop=True), r=[T3B[h]], w=bk(h))
                    k.op("pe", lambda e: e.matmul(ps[:, h, 256:384], lhsT=cst[:, C_ONES:C_ONES + 128], rhs=hd[h][:, DG, :], start=True, stop=True),
                         r=[hdB[h][DG], cstB], w=bk(h))
                yield
                for h in H4:
                    tt1 = AN[h][1][:, 0, :]
                    tt2 = AN[h][1][:, 1, :]
                    k.op("dve", lambda e: e.scalar_tensor_tensor(out=tt1, in0=ps[:, h, 256:384], scalar=gg[:, h:h + 1],
                                                                 in1=cst[:, C_MB1:C_MB1 + 128], op0=ALU.subtract, op1=ALU.subtract),
                         r=bk(h) + [s4B, cstB], w=[ANB[h][1]])
                    k.op("dve", lambda e: e.scalar_tensor_tensor(out=tt2, in0=ps[:, h, 256:384], scalar=gg[:, h:h + 1],
                                                                 in1=cst[:, C_MB2:C_MB2 + 128], op0=ALU.subtract, op1=ALU.add),
                         r=bk(h) + [s4B, cstB], w=[ANB[h][1]])
                yield
                for h in H4:
                    k.op("act", lambda e: e.activation(out=AN[h][1][:, 0, :], in_=AN[h][1][:, 0, :], func=AF.Exp, scale=-1.0),
                         r=[ANB[h][1]], w=[ANB[h][1]])
                    k.op("act", lambda e: e.activation(out=AN[h][1][:, 1, :], in_=AN[h][1][:, 1, :], func=AF.Exp), r=[ANB[h][1]], w=[ANB[h][1]])
                yield
                for h in H4:
                    k.op("dve", lambda e: e.scalar_tensor_tensor(out=AN[h][0][:, 1, :], in0=ps[:, h, 0:128], scalar=col(5, h), in1=AN[h][1][:, 0, :],
                                                                 op0=ALU.mult, op1=ALU.mult), r=bk(h) + [s4B, ANB[h][1]], w=[ANB[h][0]])
                    k.op("dve", lambda e: e.tensor_tensor(out=aqk[h][:], in0=ps[:, h, 128:256], in1=AN[h][1][:, 1, :], op=ALU.mult),
                         r=bk(h) + [ANB[h][1]], w=[aqB[h]])
                yield
                for h in H4:
                    k.op("pe", lambda e: e.transpose(ps[:, h, 0:128], AN[h][0][:, 1, :], ident), r=[ANB[h][0], cstB], w=bk(h))
                yield
                for h in H4:
                    k.op("act", lambda e: e.activation(out=AN[h][0][:, 0, :], in_=ps[:, h, 0:128], func=AF.Identity), r=bk(h), w=[ANB[h][0]])
                yield
                for h in H4:
                    k.op("dve", lambda e: e.tensor_tensor(out=Mm[h][0][:], in0=AN[h][0][:, 0, :], in1=ident, op=ALU.add),
                         r=[ANB[h][0], cstB], w=[MB[h][0]])

                def sq(h, cur):
                    k.op("pe", lambda e: e.matmul(ps[:, h, 0:128], lhsT=AN[h][cur][:, 1, :], rhs=AN[h][cur][:, 0, :], start=True, stop=True),
                         r=[ANB[h][cur]], w=bk(h))
                    k.op("pe", lambda e: e.matmul(ps[:, h, 128:256], lhsT=AN[h][cur][:, 0, :], rhs=AN[h][cur][:, 1, :], start=True, stop=True),
                         r=[ANB[h][cur]], w=bk(h))

                def ev(h, nxt):
                    k.op("act", lambda e: e.activation(out=AN[h][nxt][:].rearrange("p a c -> p (a c)"), in_=ps[:, h, 0:256], func=AF.Identity),
                         r=bk(h), w=[ANB[h][nxt]])

                def pr(h, an, mc):
                    k.op("pe", lambda e: e.matmul(ps[:, h, 256:384], lhsT=AN[h][an][:, 1, :], rhs=Mm[h][mc][:], start=True, stop=True),
                         r=[ANB[h][an], MB[h][mc]], w=bk(h))

                def ad(h, mc):
                    k.op("dve", lambda e: e.tensor_tensor(out=Mm[h][1 - mc][:], in0=ps[:, h, 256:384], in1=Mm[h][mc][:], op=ALU.add),
                         r=bk(h) + [MB[h][mc]], w=[MB[h][1 - mc]])

                for h in H4:
                    sq(h, 0)
                yield
                for h in H4:
                    ev(h, 1)
                yield
                for it in range(1, 6):
                    an = it % 2
                    for h in H4:
                        if it < 5:
                            sq(h, an)
                        pr(h, an, (it - 1) % 2)
                    yield
                    for h in H4:
                        if it < 5:
                            ev(h, 1 - an)
                        ad(h, (it - 1) % 2)
                    yield
                mf = 5 % 2
                for h in H4:
                    k.op("pe", lambda e: e.matmul(ps[:, h, 0:128], lhsT=Mm[h][mf][:], rhs=hd[h][:, VB, :], start=True, stop=True),
                         r=[MB[h][mf], hdB[h][VB]], w=bk(h))
                    k.op("pe", lambda e: e.matmul(ps[:, h, 128:256], lhsT=hd[h][:, KBG, :], rhs=Mm[h][mf][:], start=True, stop=True),
                         r=[MB[h][mf], hdB[h][KBG]], w=bk(h))
                yield
                for h in H4:
                    k.op("act", lambda e: e.activation(out=U[h][:], in_=ps[:, h, 0:128], func=AF.Identity), r=bk(h), w=[UB[h]])
                    k.op("act", lambda e: e.activation(out=WT[h][:], in_=ps[:, h, 128:256], func=AF.Identity), r=bk(h), w=[WTB[h]])
                yield
                for c in range(2):
                    r0, r1 = 64 * c, 64 * c + 64
                    for h in H4:
                        k.op("pe", lambda e: e.matmul(ps[:, h, 0:128], lhsT=WT[h][:], rhs=Sb[:, h, :], start=True, stop=True),
                             r=[WTB[h], SbB[h]], w=bk(h))
                    yield
                    for h in H4:
                        k.op("dve", lambda e: e.tensor_tensor(out=dl[h][r0:r1, :], in0=U[h][r0:r1, :], in1=ps[r0:r1, h, 0:128], op=ALU.subtract),
                             r=bk(h) + [UB[h]], w=[dlB[h]])
                    yield
                    for h in H4:
                        k.op("pe", lambda e: e.matmul(ps[:, h, 256:384], lhsT=kd[h][r0:r1, :], rhs=dl[h][r0:r1, :], start=True, stop=True),
                             r=[kdB[h], dlB[h]], w=bk(h))
                        k.op("pe", lambda e: e.matmul(ps[:, h, 128:256], lhsT=T3[h][:, 2, :], rhs=Sb[:, h, :], start=True, stop=False),
                             r=[T3B[h], SbB[h]], w=bk(h))
                        k.op("pe", lambda e: e.matmul(ps[:, h, 128:256], lhsT=aqk[h][r0:r1, :], rhs=dl[h][r0:r1, :], start=False, stop=True),
                             r=[aqB[h], dlB[h]], w=bk(h))
                    yield
                    for h in H4:
                        k.op("dve", lambda e: e.scalar_tensor_tensor(out=Sb[:, h, :], in0=S[:, h, :], scalar=sc4[:, 6 + c, h:h + 1],
                                                                     in1=ps[:, h, 256:384], op0=ALU.mult, op1=ALU.add),
                             r=bk(h) + [s4B, SB_[h]], w=[SbB[h]])
                        k.op("dve", lambda e: e.scalar_tensor_tensor(out=S[:, h, :], in0=S[:, h, :], scalar=sc4[:, 6 + c, h:h + 1],
                                                                     in1=ps[:, h, 256:384], op0=ALU.mult, op1=ALU.add),
                             r=bk(h) + [s4B, SB_[h]], w=[SB_[h]])
                        k.op("act", lambda e: e.activation(out=otm[r0:r1, h, :], in_=ps[r0:r1, h, 128:256], func=AF.Identity), r=bk(h), w=[otB[h]])
                    yield
                for h in H4:
                    k.op("act", lambda e: e.activation(out=junk[:], in_=otm[:, h, :], func=AF.Square, accum_out=ss[:, h:h + 1]),
                         r=[otB[h]], w=[jkB, s4B])
                k.op("dve", lambda e: e.tensor_scalar(out=rs[:, 0:4], in0=ss[:, 0:4], scalar1=float(1.0 / 128), scalar2=1e-6,
                                                      op0=ALU.mult, op1=ALU.add), r=[s4B], w=[s4B])
                k.op("act", lambda e: e.activation(out=rs[:, 0:4], in_=rs[:, 0:4], func=AF.Sqrt), r=[s4B], w=[s4B])
                k.op("dve", lambda e: e.reciprocal(out=rs[:, 0:4], in_=rs[:, 0:4]), r=[s4B], w=[s4B])
                yield
                for h in H4:
                    k.op("dve", lambda e: e.scalar_tensor_tensor(out=otm[:, h, :], in0=otm[:, h, :], scalar=rs[:, h:h + 1], in1=dng[:],
                                                                 op0=ALU.mult, op1=ALU.mult), r=[otB[h], s4B, gpB], w=[otB[h]])
                k.op("dve", lambda e: e.tensor_tensor(out=dn[:], in0=otm[:].rearrange("p h d -> p (h d)"), in1=sz[:], op=ALU.mult),
                     r=otB + [szB], w=[dnB])
                yield
                pbf = ps[:, 0, :].bitcast(BF16)
                for j in range(4):
                    k.op("pe", lambda e: e.transpose(pbf[:, j * 128:(j + 1) * 128], dn[:, j * 128:(j + 1) * 128], idb[:]), r=[dnB, cstB], w=[psB[0]])
                k.op("act", lambda e: e.activation(out=catT[:, 4:8, :].rearrange("p a t -> p (a t)"), in_=pbf[:, 0:512], func=AF.Identity),
                     r=[psB[0]], w=[catB])
                yield

            def epilogue(t):
                xa, xB1 = xts[t % 2], xtB[t % 2]
                b0, dB = 2, [psB[2], psB[3]]
                for hf in range(2):
                    for fc in range(8):
                        k.op("pe", lambda e: e.matmul(ps[:, b0 + hf, :], lhsT=catT[:, fc, :], rhs=wout[:, fc, hf * 512:(hf + 1) * 512],
                                                      start=(fc == 0), stop=(fc == 7)), r=[catB, wB_], w=dB)
                k.op("dve", lambda e: e.scalar_tensor_tensor(out=xa[:].rearrange("p (a c) -> p a c", a=2),
                                                             in0=xa[:].rearrange("p (a c) -> p a c", a=2), scalar=ALPHA,
                                                             in1=ps[:, b0:b0 + 2, :], op0=ALU.mult, op1=ALU.add), r=dB + [xB1], w=[xB1])
                layer_norm_tile(xa[:], xB1, lnbc, lnB, st6, mv, tmpc, smB)
                k.dma(xs_d[t * 128:(t + 1) * 128, :], xa[:], r=[xB1], w=[xsB[t]])

            for _ in p1(0):
                pass
            for t in range(NT):
                ga = attn_path(t)
                gg_ = gdn_path(t)
                gp = p1(t + 1) if t + 1 < NT else None
                npf = 0
                alive = [ga, gg_]
                while alive:
                    for g in list(alive):
                        try:
                            next(g)
                        except StopIteration:
                            alive.remove(g)
                    if gp is not None and not NOINTER and flags["idx"] == t and flags["g1"] == t and npf < PFMAX:
                        npf += 1
                        try:
                            next(gp)
                        except StopIteration:
                            gp = None
                if gp is not None:
                    for _ in gp:
                        pass
                epilogue(t)

        for b in range(2):
            ffn_phase(b, 0)
            if stage >= 2:
                mixer_phase(b)
            if stage >= 3:
                ffn_phase(b, 1)
        if stage < 3:
            with nc.sbuf_tensor("dbgt", [128, D], F32) as dbg:
                dB_ = Bf()
                for t in range(NT):
                    k.dma(dbg[:], xs_d[t * 128:(t + 1) * 128, :], r=[xsB[t]], w=[dB_])
                    k.dma(out_d[1, t * 128:(t + 1) * 128, :], dbg[:], r=[dB_], w=[outB[NT + t]])
        k.barrier()
    return nc


_PERM = None


def _perm():
    a_q = np.arange(0, 512); a_k = np.arange(512, 576); a_v = np.arange(576, 640)
    i_q = np.arange(640, 1152); i_k = np.arange(1152, 1216); i_w = np.arange(1216, 1224)
    b_q = np.arange(1224, 1736); b_k = np.arange(1736, 2248); b_v = np.arange(2248, 2760)
    b_z = np.arange(2760, 3272); b_a = np.arange(3272, 3276); b_b = np.arange(3276, 3280)
    return np.concatenate([a_q, i_q, a_k, i_k, a_v, i_w, b_a, b_b, b_z, b_q, b_k, b_v])


def make_in_maps(inp):
    f = lambda a: np.ascontiguousarray(np.asarray(a))
    shared = {}
    shared["wada"] = f(np.asarray(inp["w_ada"])[0].reshape(8, 128, 72, 128).transpose(2, 1, 0, 3))
    shared["badaT"] = f(np.asarray(inp["b_ada"])[0].reshape(72, 128).T)
    ffn = ((1, inp["ffn1_w1"], inp["ffn1_w3"], inp["ffn1_w2"]), (2, inp["ffn2_w1"], inp["ffn2_w3"], inp["ffn2_w2"]))
    for i, a1, a3, a2 in ffn:
        for nm, w in (("w1", a1), ("w3", a3)):
            w = np.asarray(w)[0]
            shared["%s_%d" % (nm, i)] = f(w.reshape(8, 128, NCH, 128).transpose(2, 1, 0, 3).reshape(NCH, 128, 1024))
        shared["w2_%d" % i] = f(np.asarray(a2)[0].reshape(NCH, 128, 1024))
    win = np.asarray(inp["w_in"])[0][:, _perm()]
    shared["win"] = f(win.reshape(8, 128, NCOL).transpose(1, 0, 2))
    shared["wout"] = f(np.asarray(inp["w_out"])[0].reshape(8, 128, D).transpose(1, 0, 2))
    shared["lnp"] = f(np.stack([np.asarray(inp[n])[0] for n in ("ln1_g", "ln1_b", "ln2_g", "ln2_b", "ln3_g", "ln3_b")]))
    shared["convT"] = f(np.asarray(inp["conv_w"])[0].reshape(4, 12, 128).transpose(2, 1, 0))
    shared["gdnp"] = f(np.concatenate([np.asarray(inp["a_log"])[0], np.asarray(inp["dt_bias"])[0]]))
    shared["dng"] = f(np.asarray(inp["dn_norm_g"])[0])
    shared["consts"] = make_consts()
    x = np.asarray(inp["x"]); c = np.asarray(inp["c"]); pos = np.asarray(inp["positions"])
    maps = []
    for core in range(8):
        m = dict(shared)
        b0 = 2 * core
        m["x"] = f(x[b0:b0 + 2])
        m["cT"] = f(c[b0:b0 + 2].reshape(2, 8, 128).transpose(2, 1, 0))
        m["pos"] = f(pos[b0:b0 + 2].reshape(2, NT, 128).transpose(2, 0, 1).astype(np.int32))
        maps.append(m)
    return maps


def kernel(**inputs):
    nc = build(STAGE)
    maps = make_in_maps(inputs)
    res = run_bass_kernel_spmd(nc, maps, core_ids=list(range(8)))
    out = np.concatenate([np.asarray(r["out"]) for r in res.results], axis=0)
    return out.astype(np.float32)
```

```python
import numpy as np
from contextlib import ExitStack
import concourse.bass as bass
import concourse.mybir as mybir
from concourse.bass_utils import run_bass_kernel_spmd

F32 = mybir.dt.float32
BF16 = mybir.dt.bfloat16
I32 = mybir.dt.int32
AF = mybir.ActivationFunctionType
ALU = mybir.AluOpType
AX = mybir.AxisListType

D = 1024
SEQ = 2048
NT = 16
DFF = 2816
NCH = 22
GROUPS = [4, 4, 4, 5, 5]
ALPHA = 2.0 ** 0.25
NCOL = 3280
NTM = 1744
NDS = 16
BIG = 1.0e5
NEGFILL = -1.0e30
MASKV = -30000.0
NOINTER = False
PFMAX = 99
STAGE = 99


class Bf:
    __slots__ = ("w", "r", "x")

    def __init__(self, x=False):
        self.w = None
        self.r = {}
        self.x = x


def Bs(n):
    return [Bf() for _ in range(n)]


class K:
    def __init__(self, nc, es):
        self.nc = nc
        self.eng = {"pe": nc.tensor, "dve": nc.vector, "act": nc.scalar, "pool": nc.gpsimd, "sp": nc.sync}
        self.sem = {e: es.enter_context(nc.semaphore("s_" + e)) for e in self.eng}
        self.cnt = {e: 0 for e in self.eng}
        self.seen = {e: {} for e in self.eng}
        self.dsem = [es.enter_context(nc.semaphore("dq%d" % i)) for i in range(NDS)]
        self.dval = [0] * NDS
        self.dnext = 0

    def _deps(self, e, r, w):
        deps = {}

        def add(ev):
            if ev is None:
                return
            key, sem, val = ev
            if key == "pe" and e == "pe":
                return
            if key not in deps or deps[key][1] < val:
                deps[key] = (sem, val)

        for b in r:
            add(b.w)
            if b.x:
                for ek, ev in b.r.items():
                    if ek != e:
                        add(ev)
        for b in w:
            add(b.w)
            for ev in b.r.values():
                add(ev)
        for key, (sem, val) in deps.items():
            if self.seen[e].get(key, 0) < val:
                self.eng[e].wait_ge(sem, val)
                self.seen[e][key] = val

    def _mark(self, e, ev, r, w):
        for b in r:
            b.r[ev[0]] = ev
        for b in w:
            b.w = ev
            b.r = {}

    def op(self, e, fn, r=(), w=()):
        self._deps(e, r, w)
        inst = fn(self.eng[e])
        self.cnt[e] += 1
        inst.then_inc(self.sem[e], 1)
        ev = (e, self.sem[e], self.cnt[e])
        self._mark(e, ev, r, w)
        return ev

    def dma(self, out, in_, r=(), w=(), e="sp"):
        i = self.dnext
        self.dnext = (i + 1) % NDS
        self._deps(e, r, w)
        key = ("d", i)
        if self.dval[i] > self.seen[e].get(key, 0):
            self.eng[e].wait_ge(self.dsem[i], self.dval[i])
            self.seen[e][key] = self.dval[i]
        inst = self.eng[e].dma_start(out=out, in_=in_)
        self.dval[i] += 16
        inst.then_inc(self.dsem[i], 16)
        ev = (key, self.dsem[i], self.dval[i])
        self._mark(e, ev, r, w)
        return ev

    def barrier(self):
        for e in self.eng:
            for o in self.eng:
                if o != e and self.cnt[o] > self.seen[e].get(o, 0):
                    self.eng[e].wait_ge(self.sem[o], self.cnt[o])
                    self.seen[e][o] = self.cnt[o]
            for i in range(NDS):
                key = ("d", i)
                if self.dval[i] > self.seen[e].get(key, 0):
                    self.eng[e].wait_ge(self.dsem[i], self.dval[i])
                    self.seen[e][key] = self.dval[i]


C_ID, C_ONES, C_TRI, C_OBD, C_SEL0, C_SEL1, C_MB1, C_MB2, C_INVF, C_I8, C_POW, C_END = (
    0, 128, 256, 384, 512, 640, 768, 896, 1024, 1056, 1568, 1592)
NBIS = 22


def make_consts():
    c = np.zeros((128, C_END), np.float32)
    idx = np.arange(128)
    same = (idx[:, None] // 64) == (idx[None, :] // 64)
    c[:, C_ID:C_ID + 128] = np.eye(128)
    c[:, C_ONES:C_ONES + 128] = 1.0
    c[:, C_TRI:C_TRI + 128] = (same & (idx[:, None] <= idx[None, :]))
    c[:, C_OBD:C_OBD + 128] = same
    c[:, C_SEL0:C_SEL0 + 128] = (idx[:, None] < 64)
    c[:, C_SEL1:C_SEL1 + 128] = (idx[:, None] >= 64)
    c[:, C_MB1:C_MB1 + 128] = np.where(same & (idx[:, None] > idx[None, :]), 0.0, -BIG)
    c[:, C_MB2:C_MB2 + 128] = np.where(same & (idx[None, :] >= idx[:, None]), 0.0, -BIG)
    inv = (10000.0 ** (-np.arange(0, 64, 2, dtype=np.float32) / np.float32(64))).astype(np.float32)
    c[:, C_INVF:C_INVF + 32] = inv[None, :]
    c[:, C_I8:C_I8 + 512] = np.tile(np.eye(128, dtype=np.float32), (1, 4))
    c[:, C_POW:C_POW + 24] = (0.5 ** np.arange(1, 25, dtype=np.float64))[None, :]
    return c


def build(stage=99):
    nc = bass.Bass("TRN2", target_bir_lowering=False)
    dt = lambda n, s, d=F32, kind="ExternalInput": nc.dram_tensor(n, s, d, kind=kind).ap()
    x_d = dt("x", [2, SEQ, D])
    cT_d = dt("cT", [128, 8, 2])
    pos_d = dt("pos", [128, 2, NT], I32)
    wada_d = dt("wada", [72, 128, 8, 128])
    bada_d = dt("badaT", [128, 72])
    fw = []
    for i in (1, 2):
        fw.append((dt("w1_%d" % i, [NCH, 128, 1024]), dt("w3_%d" % i, [NCH, 128, 1024]), dt("w2_%d" % i, [NCH, 128, 1024])))
    win_d = dt("win", [128, 8, NCOL])
    wout_d = dt("wout", [128, 8, D])
    lnp_d = dt("lnp", [6, D])
    conv_d = dt("convT", [128, 12, 4])
    gdnp_d = dt("gdnp", [8])
    dng_d = dt("dng", [128])
    const_d = dt("consts", [128, C_END])
    out_d = dt("out", [2, SEQ, D], kind="ExternalOutput")
    xs_d = dt("xs", [SEQ, D], kind="Internal")

    with ExitStack() as es:
        k = K(nc, es)
        uid = [0]

        def _alloc(stack, n, s, d):
            uid[0] += 1
            return stack.enter_context(nc.sbuf_tensor("%s_%d" % (n, uid[0]), s, d))
        sb = lambda n, s, d=F32: _alloc(es, n, s, d)
        ps = es.enter_context(nc.psum_tensor("ps", [128, 8, 512], F32))
        cst = sb("cst", [128, C_END])
        cstB = Bf()
        idb = sb("idb", [128, 128], BF16)
        i8b = sb("i8b", [128, 512], BF16)
        modT = sb("modT", [128, 72, 2])
        modB = Bf()
        xsB = Bs(NT)
        outB = Bs(2 * NT)
        psB = [Bf(True) for _ in range(8)]
        ident = cst[:, C_ID:C_ID + 128]

        k.dma(cst[:], const_d, w=[cstB])
        k.op("pool", lambda e: e.tensor_copy(out=idb[:], in_=cst[:, C_ID:C_ID + 128]), r=[cstB], w=[cstB])
        k.op("pool", lambda e: e.tensor_copy(out=i8b[:], in_=cst[:, C_I8:C_I8 + 512]), r=[cstB], w=[cstB])

        with ExitStack() as es2:
            sb2 = lambda n, s, d=F32: _alloc(es2, n, s, d)
            wsl = [sb2("wsl%d" % i, [128, 8, 1024]) for i in range(2)]
            wslB = Bs(2)
            scT = sb2("scT", [128, 8, 2])
            bad = sb2("bad", [128, 72])
            scB = Bf()
            k.dma(scT[:], cT_d, w=[scB])
            k.dma(bad[:], bada_d, w=[scB])
            k.op("act", lambda e: e.activation(out=scT[:], in_=scT[:], func=AF.Silu), r=[scB], w=[scB])
            for slab in range(9):
                s = slab % 2
                k.dma(wsl[s][:], wada_d[slab * 8:(slab + 1) * 8].rearrange("j p k f -> p j (k f)"), w=[wslB[s]])
                for jj in range(8):
                    j = slab * 8 + jj
                    for kc in range(8):
                        k.op("pe", lambda e: e.matmul(ps[:, 0, 2 * j:2 * j + 2], lhsT=wsl[s][:, jj, kc * 128:(kc + 1) * 128],
                                                      rhs=scT[:, kc, :], start=(kc == 0), stop=(kc == 7)),
                             r=[wslB[s], scB], w=[psB[0]])
            for b in range(2):
                k.op("dve", lambda e: e.tensor_tensor(out=modT[:, :, b], in0=ps[:, 0, b:144:2], in1=bad[:], op=ALU.add),
                     r=[psB[0], scB], w=[modB])
            for v in (1, 4, 7):
                k.op("dve", lambda e: e.tensor_scalar(out=modT[:, v * 8:v * 8 + 8, :], in0=modT[:, v * 8:v * 8 + 8, :],
                                                      scalar1=1.0, scalar2=None, op0=ALU.add), r=[modB], w=[modB])
            for v in (2, 8):
                k.op("dve", lambda e: e.tensor_scalar(out=modT[:, v * 8:v * 8 + 8, :], in0=modT[:, v * 8:v * 8 + 8, :],
                                                      scalar1=0.5, scalar2=None, op0=ALU.mult), r=[modB], w=[modB])
            k.barrier()

        def gen_gate_row(v, b, gbc, gbcB, colbc, colB):
            for kc in range(8):
                k.op("dve", lambda e: e.tensor_scalar(out=colbc[:], in0=cst[:, C_ONES:C_ONES + 128],
                                                      scalar1=modT[:, v * 8 + kc, b:b + 1], scalar2=None, op0=ALU.mult),
                     r=[modB, cstB], w=[colB])
                bank = 6 + kc // 4
                k.op("pe", lambda e: e.matmul(ps[:, bank, (kc % 4) * 128:(kc % 4 + 1) * 128], lhsT=colbc[:], rhs=ident,
                                              start=True, stop=True), r=[colB, cstB], w=[psB[bank]])
            for hb in range(2):
                k.op("act", lambda e: e.activation(out=gbc[:, hb * 512:(hb + 1) * 512], in_=ps[:, 6 + hb, :], func=AF.Identity),
                     r=[psB[6 + hb]], w=[gbcB])

        def layer_norm_tile(xt_ap, xB, lnbc, lnB, st6, mv, tmpc, smB):
            for hh in range(2):
                k.op("dve", lambda e: e.bn_stats(out=st6[:, hh, :], in_=xt_ap[:, hh * 512:(hh + 1) * 512]), r=[xB], w=[smB])
            k.op("dve", lambda e: e.bn_aggr(out=mv[:], in_=st6[:].rearrange("p a b -> p (a b)")), r=[smB], w=[smB])
            k.op("dve", lambda e: e.tensor_scalar(out=tmpc[:], in0=mv[:, 1:2], scalar1=1e-5, scalar2=None, op0=ALU.add),
                 r=[smB], w=[smB])
            k.op("act", lambda e: e.activation(out=tmpc[:], in_=tmpc[:], func=AF.Sqrt), r=[smB], w=[smB])
            k.op("dve", lambda e: e.reciprocal(out=tmpc[:], in_=tmpc[:]), r=[smB], w=[smB])
            k.op("dve", lambda e: e.tensor_scalar(out=xt_ap, in0=xt_ap, scalar1=mv[:, 0:1], scalar2=tmpc[:, 0:1],
                                                  op0=ALU.subtract, op1=ALU.mult), r=[smB, xB], w=[xB])
            k.op("dve", lambda e: e.tensor_tensor(out=xt_ap, in0=xt_ap, in1=lnbc[:, 0, :], op=ALU.mult), r=[xB, lnB], w=[xB])
            k.op("pool", lambda e: e.tensor_tensor(out=xt_ap, in0=xt_ap, in1=lnbc[:, 1, :], op=ALU.add), r=[xB, lnB], w=[xB])

        def ffn_phase(b, which):
            v0 = 0 if which == 0 else 6
            w1d, w3d, w2d = fw[which]
            lnrow = 0 if which == 0 else 4
            with ExitStack() as es2:
                sb2 = lambda n, s, d=F32: _alloc(es2, n, s, d)
                xres = sb2("xres", [128, NT, D])
                xB = Bs(NT)
                uT = sb2("uT", [128, 8, SEQ], BF16)
                uB = Bs(NT)
                wb = [[sb2("wg%d_%d" % (s, m), [128, 5, 1024], BF16) for m in range(3)] for s in range(2)]
                wB = [[Bs(3) for _ in range(5)] for _ in range(2)]
                stg = [sb2("stg%d" % i, [128, 1024]) for i in range(3)]
                stgB = Bs(3)
                gbc = sb2("gbc", [128, 1024])
                gbcB = Bf()
                colbc = sb2("colbc", [128, 128])
                colB = Bf()
                lnbc = sb2("lnbc", [128, 2, D])
                lnB = Bf()
                sT = [sb2("sT%d" % i, [128, 256], BF16) for i in range(3)]
                sTB = Bs(3)
                gT = [sb2("gT%d" % i, [128, 256], BF16) for i in range(3)]
                gTB = Bs(3)
                HBK = (0, 1, 6)
                st6 = sb2("st6", [128, 2, 6])
                mv = sb2("mv", [128, 2])
                tmpc = sb2("tmpc", [128, 1])
                smB = Bf()
                poB = Bs(2)

                gen_gate_row(v0 + 2, b, gbc, gbcB, colbc, colB)
                for i in range(2):
                    k.dma(lnbc[:, i, :], lnp_d[lnrow + i, :].partition_broadcast(128), w=[lnB])
                nstg = [0]

                def load_chunk(cg, slot, ci):
                    for m, src in enumerate((w1d, w3d, w2d)):
                        s = nstg[0] % 3
                        nstg[0] += 1
                        k.dma(stg[s][:], src[cg], w=[stgB[s]])
                        if m < 2:
                            k.op("act", lambda e: e.activation(out=wb[slot][m][:, ci, :], in_=stg[s][:], func=AF.Identity),
                                 r=[stgB[s]], w=[wB[slot][ci][m]])
                        else:
                            k.op("pool", lambda e: e.tensor_tensor(out=wb[slot][m][:, ci, :], in0=stg[s][:], in1=gbc[:], op=ALU.mult),
                                 r=[stgB[s], gbcB], w=[wB[slot][ci][m]])

                gstart = [sum(GROUPS[:g]) for g in range(len(GROUPS))]
                def load_tile(t):
                    if which == 0:
                        k.dma(xres[:, t, :], x_d[b, t * 128:(t + 1) * 128, :], w=[xB[t]])
                    else:
                        k.dma(xres[:, t, :], xs_d[t * 128:(t + 1) * 128, :], r=[xsB[t]], w=[xB[t]])

                for ci in range(GROUPS[0]):
                    load_chunk(ci, 0, ci)
                    load_tile(2 * ci)
                    load_tile(2 * ci + 1)
                for t in range(2 * GROUPS[0], NT):
                    load_tile(t)

                def prep_round(t, rnd):
                    bank = 7
                    if True:
                        for kc in range(4 * rnd, 4 * rnd + 4):
                            k.op("pe", lambda e: e.transpose(ps[:, bank, (kc % 4) * 128:(kc % 4 + 1) * 128],
                                                             xres[:, t, kc * 128:(kc + 1) * 128], ident),
                                 r=[xB[t], cstB], w=[psB[bank]])
                        for kc in range(4 * rnd, 4 * rnd + 4):
                            k.op("act", lambda e: e.activation(out=uT[:, kc, t * 128:(t + 1) * 128],
                                                               in_=ps[:, bank, (kc % 4) * 128:(kc % 4 + 1) * 128], func=AF.Identity,
                                                               scale=modT[:, (v0 + 1) * 8 + kc, b:b + 1], bias=modT[:, v0 * 8 + kc, b:b + 1]),
                                 r=[psB[bank], modB], w=[uB[t]])

                def finish_tile(t):
                    layer_norm_tile(xres[:, t, :], xB[t], lnbc, lnB, st6, mv, tmpc, smB)
                    if which == 0:
                        k.dma(xs_d[t * 128:(t + 1) * 128, :], xres[:, t, :], r=[xB[t]], w=[xsB[t]])
                    else:
                        k.dma(out_d[b, t * 128:(t + 1) * 128, :], xres[:, t, :], r=[xB[t]], w=[outB[b * NT + t]])

                pending = []
                step = 0
                for tt_ in range(2):
                    for rnd_ in range(2):
                        prep_round(tt_, rnd_)
                for g, gs in enumerate(GROUPS):
                    slot = g % 2
                    for blk in range(8):
                        if g + 1 < len(GROUPS) and blk < GROUPS[g + 1]:
                            load_chunk(gstart[g + 1] + blk, 1 - slot, blk)
                        for ci in range(gs):
                            if g == 0 and ci < 4 and blk + 1 < 8:
                                prep_round(2 * blk + 2 + ci // 2, ci % 2)
                            hi = step % 3
                            hb = HBK[hi]
                            step += 1
                            for m in range(2):
                                for kc in range(8):
                                    k.op("pe", lambda e: e.matmul(ps[:, hb, m * 256:(m + 1) * 256],
                                                                  lhsT=wb[slot][m][:, ci, kc * 128:(kc + 1) * 128],
                                                                  rhs=uT[:, kc, blk * 256:(blk + 1) * 256], start=(kc == 0), stop=(kc == 7)),
                                         r=[uB[2 * blk], uB[2 * blk + 1], wB[slot][ci][m]], w=[psB[hb]])
                            k.op("act", lambda e: e.activation(out=sT[hi][:], in_=ps[:, hb, 0:256], func=AF.Silu),
                                 r=[psB[hb]], w=[sTB[hi]])
                            k.op("dve", lambda e: e.tensor_tensor(out=gT[hi][:], in0=sT[hi][:], in1=ps[:, hb, 256:512], op=ALU.mult),
                                 r=[sTB[hi], psB[hb]], w=[gTB[hi]])
                            while len(pending) >= 2:
                                pending.pop(0)()

                            def w2_step(hb=hi, ci=ci, slot=slot, gs=gs, blk=blk, g=g):
                                for tt in range(2):
                                    for hf in range(2):
                                        k.op("pe", lambda e: e.matmul(ps[:, 2 + 2 * tt + hf, :], lhsT=gT[hb][:, tt * 128:(tt + 1) * 128],
                                                                      rhs=wb[slot][2][:, ci, hf * 512:(hf + 1) * 512],
                                                                      start=(ci == 0), stop=(ci == gs - 1)),
                                             r=[gTB[hb], wB[slot][ci][2]], w=[poB[tt]])
                                if ci == gs - 1:
                                    for tt in range(2):
                                        t = 2 * blk + tt
                                        xv = xres[:, t, :].rearrange("p (a c) -> p a c", a=2)
                                        if g == 0:
                                            k.op("dve", lambda e: e.scalar_tensor_tensor(out=xv, in0=xv, scalar=ALPHA, in1=ps[:, 2 + 2 * tt:4 + 2 * tt, :],
                                                                                         op0=ALU.mult, op1=ALU.add), r=[poB[tt], xB[t]], w=[xB[t]])
                                        else:
                                            k.op("dve", lambda e: e.tensor_tensor(out=xv, in0=xv, in1=ps[:, 2 + 2 * tt:4 + 2 * tt, :], op=ALU.add),
                                                 r=[poB[tt], xB[t]], w=[xB[t]])
                                        if g == len(GROUPS) - 1:
                                            finish_tile(t)
                            pending.append(w2_step)
                for fn in pending:
                    fn()
                k.barrier()

        def mixer_phase(b):
            with ExitStack() as es2:
                sb2 = lambda n, s, d=F32: _alloc(es2, n, s, d)
                win = sb2("win", [128, 8, NCOL], BF16)
                wout = sb2("wout", [128, 8, D], BF16)
                wB_ = Bf()
                with ExitStack() as es3:
                    sb3 = lambda n, s, d=F32: _alloc(es3, n, s, d)
                    stg = [sb3("mstg%d" % i, [128, 1024]) for i in range(3)]
                    stgB = Bs(3)
                    gbc = sb3("mgbc", [128, 1024])
                    gbcB = Bf()
                    colbc = sb3("mcolbc", [128, 128])
                    colB = Bf()
                    gen_gate_row(5, b, gbc, gbcB, colbc, colB)
                    n = 0
                    for kc in range(8):
                        for c0 in range(0, NCOL, 1024):
                            cw = min(1024, NCOL - c0)
                            s = n % 3
                            n += 1
                            k.dma(stg[s][:, 0:cw], win_d[:, kc, c0:c0 + cw], w=[stgB[s]])
                            if c0 < 2048:
                                k.op("act", lambda e: e.activation(out=win[:, kc, c0:c0 + cw], in_=stg[s][:, 0:cw], func=AF.Identity),
                                     r=[stgB[s]], w=[wB_])
                            elif c0 < 3072:
                                k.op("dve", lambda e: e.tensor_copy(out=win[:, kc, c0:c0 + cw], in_=stg[s][:, 0:cw]), r=[stgB[s]], w=[wB_])
                            else:
                                k.op("pool", lambda e: e.tensor_copy(out=win[:, kc, c0:c0 + cw], in_=stg[s][:, 0:cw]), r=[stgB[s]], w=[wB_])
                        s = n % 3
                        n += 1
                        k.dma(stg[s][:], wout_d[:, kc, :], w=[stgB[s]])
                        k.op("dve", lambda e: e.tensor_tensor(out=wout[:, kc, :], in0=stg[s][:], in1=gbc[:], op=ALU.mult),
                             r=[stgB[s], gbcB], w=[wB_])
                    k.barrier()
                mixer_tiles(b, win, wout, wB_, sb2)
                k.barrier()

        def mixer_tiles(b, win, wout, wB_, sb2):
            cosT = sb2("cosT", [128, NT, 32])
            sinT = sb2("sinT", [128, NT, 32])
            csB = Bf()
            with ExitStack() as es4:
                sb4 = lambda n, s, d=F32: _alloc(es4, n, s, d)
                posi = sb4("posi", [128, NT], I32)
                posf = sb4("posf", [128, NT])
                ang = sb4("ang", [128, NT, 32])
                angi = sb4("angi", [128, NT, 32], I32)
                angf = sb4("angf", [128, NT, 32])
                angm = sb4("angm", [128, NT, 32])
                k.dma(posi[:], pos_d[:, b, :], w=[csB])
                k.op("dve", lambda e: e.tensor_copy(out=posf[:], in_=posi[:]), r=[csB], w=[csB])
                k.op("dve", lambda e: e.tensor_tensor(out=ang[:], in0=posf[:].unsqueeze(2).to_broadcast([128, NT, 32]),
                                                      in1=cst[:, C_INVF:C_INVF + 32].unsqueeze(1).to_broadcast([128, NT, 32]), op=ALU.mult),
                     r=[csB, cstB], w=[csB])
                TWO_PI = 2.0 * np.pi
                C1 = 6.28125
                C2 = float(TWO_PI - C1)

                def reduce_sin(dst, shift):
                    k.op("dve", lambda e: e.tensor_scalar(out=angf[:], in0=ang[:], scalar1=float(shift), scalar2=float(1.0 / TWO_PI),
                                                          op0=ALU.add, op1=ALU.mult), r=[csB], w=[csB])
                    k.op("dve", lambda e: e.tensor_copy(out=angi[:], in_=angf[:]), r=[csB], w=[csB])
                    k.op("dve", lambda e: e.tensor_copy(out=angf[:], in_=angi[:]), r=[csB], w=[csB])
                    k.op("dve", lambda e: e.scalar_tensor_tensor(out=angm[:], in0=angf[:], scalar=-C1, in1=ang[:], op0=ALU.mult, op1=ALU.add),
                         r=[csB], w=[csB])
                    k.op("dve", lambda e: e.scalar_tensor_tensor(out=angm[:], in0=angf[:], scalar=-C2, in1=angm[:], op0=ALU.mult, op1=ALU.add),
                         r=[csB], w=[csB])
                    if shift != 0.0:
                        k.op("dve", lambda e: e.tensor_scalar(out=angm[:], in0=angm[:], scalar1=float(shift), scalar2=None, op0=ALU.add),
                             r=[csB], w=[csB])
                    k.op("dve", lambda e: e.tensor_scalar(out=angf[:], in0=angm[:], scalar1=float(np.pi), scalar2=-TWO_PI,
                                                          op0=ALU.is_gt, op1=ALU.mult), r=[csB], w=[csB])
                    k.op("dve", lambda e: e.tensor_tensor(out=angm[:], in0=angm[:], in1=angf[:], op=ALU.add), r=[csB], w=[csB])
                    k.op("dve", lambda e: e.tensor_scalar(out=angf[:], in0=angm[:], scalar1=float(-np.pi), scalar2=TWO_PI,
                                                          op0=ALU.is_lt, op1=ALU.mult), r=[csB], w=[csB])
                    k.op("dve", lambda e: e.tensor_tensor(out=angm[:], in0=angm[:], in1=angf[:], op=ALU.add), r=[csB], w=[csB])
                    k.op("dve", lambda e: e.tensor_scalar(out=angm[:], in0=angm[:], scalar1=float(-np.pi), scalar2=float(np.pi),
                                                          op0=ALU.max, op1=ALU.min), r=[csB], w=[csB])
                    k.op("act", lambda e: e.activation(out=dst[:], in_=angm[:], func=AF.Sin), r=[csB], w=[csB])

                reduce_sin(sinT, 0.0)
                reduce_sin(cosT, float(np.pi / 2))
                k.barrier()

            lnbc = sb2("mlnbc", [128, 2, D])
            lnB = Bf()
            kT = sb2("kT", [64, SEQ], BF16)
            kiT = sb2("kiT", [64, SEQ], BF16)
            vaug = sb2("vaug", [128, NT, 65], BF16)
            kvBs = Bs(NT)
            xts = [sb2("xt%d" % i, [128, D]) for i in range(2)]
            xtB = Bs(2)
            uTt = sb2("uTt", [128, 8, 128], BF16)
            uB = Bf()
            roped = sb2("roped", [128, 18, 64], BF16)
            ropB = Bf()
            qTs = [sb2("qT%d" % i, [64, 8, 128], BF16) for i in range(2)]
            qBs = Bs(2)
            qiT = sb2("qiT", [64, 8, 128], BF16)
            qiB = Bf()
            absw = sb2("absw", [128, 8])
            sgn = sb2("sgn", [128, 8])
            awB = Bf()
            score = sb2("score", [128, SEQ])
            scB = Bf()
            work = sb2("work", [128, SEQ])
            wkB = Bf()
            tok = work[:, 0:NTM]
            tokB = wkB
            mbias = sb2("mbias", [128, SEQ], BF16)
            mbB = Bf()
            bs = sb2("bs", [128, 8])
            wkt = sb2("wkt", [128, 24])
            m8B = Bf()
            rel = [sb2("rel%d" % i, [128, 512]) for i in range(3)]
            relB = Bs(3)
            PT = [sb2("PT%d" % i, [128, 512], BF16) for i in range(3)]
            PTB = Bs(3)
            rec = sb2("rec", [128, 8])
            attn = sb2("attn", [128, 8, 64], BF16)
            atB = Bf()
            catT = sb2("catT", [128, 8, 128], BF16)
            catB = Bf()
            st6 = sb2("mst6", [128, 2, 6])
            mv = sb2("mmv", [128, 2])
            tmpc = sb2("mtmpc", [128, 1])
            smB = Bf()
            xc = sb2("xc", [128, 12, 131])
            xcB = Bf()
            tm = sb2("tm", [128, 1536])
            tmB = Bf()
            junk = sb2("junk", [128, 128])
            jkB = Bf()
            ss = sb2("ss", [128, 8])
            rs = sb2("rs", [128, 8])
            sc4 = sb2("sc4", [128, 16, 4])
            s4B = Bf()
            gg = sb2("gg", [128, 16])
            gdnp = sb2("gdnp", [128, 8])
            negA = sb2("negA", [128, 4])
            dng = sb2("dng", [128, 128])
            convw = sb2("convw", [128, 12, 4])
            gpB = Bf()
            hd = [sb2("hd%d" % h, [128, 6, 128]) for h in range(4)]
            hdB = [Bs(6) for _ in range(4)]
            ycv = lambda cc: hd[2 + cc // 6][:, cc % 6, :]
            ycB = lambda cc: hdB[2 + cc // 6][cc % 6]
            rA = tm[:, 0:576].rearrange("p (h d) -> p h d", d=32)
            rBt = tm[:, 576:1152].rearrange("p (h d) -> p h d", d=32)
            rpB = [tmB]
            kd = [sb2("kd%d" % h, [128, 128], BF16) for h in range(4)]
            kdB = Bs(4)
            T3 = [sb2("T3%d" % h, [128, 3, 128], BF16) for h in range(4)]
            T3B = Bs(4)
            AN = [[sb2("AN%d_%d" % (h, i), [128, 2, 128]) for i in range(2)] for h in range(4)]
            ANB = [Bs(2) for _ in range(4)]
            Mm = [[sb2("Mm%d_%d" % (h, i), [128, 128]) for i in range(2)] for h in range(4)]
            MB = [Bs(2) for _ in range(4)]
            aqk = [sb2("aqk%d" % h, [128, 128], BF16) for h in range(4)]
            aqB = Bs(4)
            U = [sb2("U%d" % h, [128, 128]) for h in range(4)]
            UB = Bs(4)
            WT = [sb2("WT%d" % h, [128, 128], BF16) for h in range(4)]
            WTB = Bs(4)
            dl = [sb2("dl%d" % h, [128, 128], BF16) for h in range(4)]
            dlB = Bs(4)
            S = sb2("S", [128, 4, 128])
            Sb = sb2("Sb", [128, 4, 128], BF16)
            SB_ = Bs(4)
            SbB = Bs(4)
            otm = sb2("otm", [128, 4, 128])
            otB = Bs(4)
            sz = sb2("sz", [128, 512])
            szB = Bf()
            dn = sb2("dn", [128, 512], BF16)
            dnB = Bf()

            dbank = [0]

            def dbl():
                i = dbank[0] % 2
                dbank[0] += 1
                b0 = (0, 2)[i]
                return b0, [psB[b0], psB[b0 + 1]]

            for i in range(2):
                k.dma(lnbc[:, i, :], lnp_d[2 + i, :].partition_broadcast(128), w=[lnB])
            k.dma(gdnp[:], gdnp_d.partition_broadcast(128), w=[gpB])
            k.dma(dng[:], dng_d.partition_broadcast(128), w=[gpB])
            k.dma(convw[:], conv_d, w=[gpB])
            k.op("act", lambda e: e.activation(out=negA[:], in_=gdnp[:, 0:4], func=AF.Exp), r=[gpB], w=[gpB])
            k.op("dve", lambda e: e.tensor_scalar(out=negA[:], in0=negA[:], scalar1=-1.0, scalar2=None, op0=ALU.mult), r=[gpB], w=[gpB])
            k.op("pool", lambda e: e.memset(S[:], 0.0), w=SB_)
            k.op("pool", lambda e: e.memset(Sb[:], 0.0), w=SbB)
            k.op("pool", lambda e: e.memset(xc[:], 0.0), w=[xcB])
            k.op("pool", lambda e: e.memset(vaug[:], 1.0), w=kvBs)
            WC = float(8 ** -0.5 * 64 ** -0.5)

            def p1(t):
                xa, xB1 = xts[t % 2], xtB[t % 2]
                k.dma(xa[:], xs_d[t * 128:(t + 1) * 128, :], r=[xsB[t]], w=[xB1])
                b0, dB = 4, [psB[4], psB[5]]
                for kc in range(8):
                    bank = b0 + kc // 4
                    k.op("pe", lambda e: e.transpose(ps[:, bank, (kc % 4) * 128:(kc % 4 + 1) * 128], xa[:, kc * 128:(kc + 1) * 128], ident),
                         r=[xB1, cstB], w=dB)
                for kc in range(8):
                    bank = b0 + kc // 4
                    k.op("act", lambda e: e.activation(out=uTt[:, kc, :], in_=ps[:, bank, (kc % 4) * 128:(kc % 4 + 1) * 128],
                                                       func=AF.Identity, scale=modT[:, 32 + kc, b:b + 1], bias=modT[:, 24 + kc, b:b + 1]),
                         r=dB + [modB], w=[uB])
                yield
                for (c0, c1) in ((0, 1024), (1024, NTM)):
                    b0, dB = 4, [psB[4], psB[5]]
                    for s0 in range(c0, c1, 512):
                        s1 = min(s0 + 512, c1)
                        bank = b0 + (s0 - c0) // 512
                        for kc in range(8):
                            k.op("pe", lambda e: e.matmul(ps[:, bank, 0:s1 - s0], lhsT=uTt[:, kc, :], rhs=win[:, kc, s0:s1],
                                                          start=(kc == 0), stop=(kc == 7)), r=[uB, wB_], w=dB)
                        k.op("act", lambda e: e.activation(out=tok[:, s0:s1], in_=ps[:, bank, 0:s1 - s0], func=AF.Identity),
                             r=dB, w=[tokB])
                    yield
                tk = tok[:, 0:1152].rearrange("p (h d) -> p h d", d=64)
                cb = cosT[:, t, :].unsqueeze(1).to_broadcast([128, 18, 32])
                sbb = sinT[:, t, :].unsqueeze(1).to_broadcast([128, 18, 32])
                k.op("dve", lambda e: e.tensor_tensor(out=rA, in0=tk[:, :, 0:32], in1=cb, op=ALU.mult), r=[tokB, csB], w=rpB)
                k.op("dve", lambda e: e.tensor_tensor(out=rBt, in0=tk[:, :, 32:64], in1=sbb, op=ALU.mult), r=[tokB, csB], w=rpB)
                k.op("dve", lambda e: e.tensor_tensor(out=roped[:, :, 0:32], in0=rA, in1=rBt, op=ALU.subtract), r=rpB, w=[ropB])
                k.op("dve", lambda e: e.tensor_tensor(out=rA, in0=tk[:, :, 32:64], in1=cb, op=ALU.mult), r=[tokB, csB, ropB], w=rpB)
                k.op("dve", lambda e: e.tensor_tensor(out=rBt, in0=tk[:, :, 0:32], in1=sbb, op=ALU.mult), r=[tokB, csB], w=rpB)
                k.op("dve", lambda e: e.tensor_tensor(out=roped[:, :, 32:64], in0=rA, in1=rBt, op=ALU.add), r=rpB, w=[ropB])
                yield
                k.op("act", lambda e: e.activation(out=vaug[:, t, 0:64], in_=tok[:, 1152:1216], func=AF.Identity), r=[tokB], w=[kvBs[t]])
                k.op("act", lambda e: e.activation(out=sgn[:], in_=tok[:, 1216:1224], func=AF.Sign), r=[tokB], w=[awB])
                k.op("dve", lambda e: e.scalar_tensor_tensor(out=absw[:], in0=tok[:, 1216:1224], scalar=WC, in1=sgn[:],
                                                             op0=ALU.mult, op1=ALU.mult), r=[tokB, awB], w=[awB])
                b0, dB = 4, [psB[4], psB[5]]
                pbf = ps[:, b0:b0 + 2, :].bitcast(BF16)
                for h in range(16):
                    k.op("pe", lambda e: e.transpose(pbf[0:64, h // 8, (h % 8) * 128:(h % 8 + 1) * 128], roped[:, h, :], idb[:]),
                         r=[ropB, cstB], w=dB)
                k.op("act", lambda e: e.activation(out=qTs[t % 2][:].rearrange("p h t -> p (h t)"), in_=pbf[0:64, 0, :], func=AF.Identity),
                     r=dB, w=[qBs[t % 2]])
                k.op("act", lambda e: e.activation(out=qiT[:].rearrange("p h t -> p (h t)"), in_=pbf[0:64, 1, :], func=AF.Identity),
                     r=dB, w=[qiB])
                yield
                b0, dB = 4, [psB[4], psB[5]]
                pbf2 = ps[:, b0, :].bitcast(BF16)
                for h in range(2):
                    k.op("pe", lambda e: e.transpose(pbf2[0:64, h * 128:(h + 1) * 128], roped[:, 16 + h, :], idb[:]),
                         r=[ropB, cstB], w=dB)
                k.op("act", lambda e: e.activation(out=kT[:, t * 128:(t + 1) * 128], in_=pbf2[0:64, 0:128], func=AF.Identity), r=dB, w=[kvBs[t]])
                k.op("act", lambda e: e.activation(out=kiT[:, t * 128:(t + 1) * 128], in_=pbf2[0:64, 128:256], func=AF.Identity), r=dB, w=[kvBs[t]])

            def gdn_pro(t):
                k.op("dve", lambda e: e.tensor_tensor(out=sc4[:, 0, :], in0=tok[:, 1224:1228], in1=gdnp[:, 4:8], op=ALU.add),
                     r=[tokB, gpB], w=[s4B])
                k.op("act", lambda e: e.activation(out=sc4[:, 0, :], in_=sc4[:, 0, :], func=AF.Exp), r=[s4B], w=[s4B])
                k.op("act", lambda e: e.activation(out=sc4[:, 0, :], in_=sc4[:, 0, :], func=AF.Ln, bias=1.0), r=[s4B], w=[s4B])
                k.op("dve", lambda e: e.tensor_tensor(out=sc4[:, 0, :], in0=sc4[:, 0, :], in1=negA[:], op=ALU.mult), r=[s4B, gpB], w=[s4B])
                k.op("act", lambda e: e.activation(out=sc4[:, 1, :], in_=tok[:, 1228:1232], func=AF.Sigmoid), r=[tokB], w=[s4B])
                k.op("act", lambda e: e.activation(out=sz[:], in_=tok[:, 1232:1744], func=AF.Silu), r=[tokB], w=[szB])
                b0, dB = dbl()
                for i, co in enumerate((C_TRI, C_OBD, C_SEL0, C_SEL1)):
                    k.op("pe", lambda e: e.matmul(ps[:, b0, i * 4:i * 4 + 4], lhsT=cst[:, co:co + 128], rhs=sc4[:, 0, :], start=True, stop=True),
                         r=[s4B, cstB], w=dB)
                k.op("dve", lambda e: e.tensor_copy(out=gg[:], in_=ps[:, b0, 0:16]), r=dB, w=[s4B])
                k.op("act", lambda e: e.activation(out=sc4[:, 2, :], in_=gg[:, 0:4], func=AF.Exp), r=[s4B], w=[s4B])
                k.op("dve", lambda e: e.tensor_tensor(out=sc4[:, 8, :], in0=gg[:, 4:8], in1=gg[:, 0:4], op=ALU.subtract), r=[s4B], w=[s4B])
                k.op("act", lambda e: e.activation(out=sc4[:, 3, :], in_=sc4[:, 8, :], func=AF.Exp), r=[s4B], w=[s4B])
                k.op("act", lambda e: e.activation(out=sc4[:, 6, :], in_=gg[:, 8:12], func=AF.Exp), r=[s4B], w=[s4B])
                k.op("act", lambda e: e.activation(out=sc4[:, 7, :], in_=gg[:, 12:16], func=AF.Exp), r=[s4B], w=[s4B])
                k.op("dve", lambda e: e.tensor_tensor(out=sc4[:, 4, :], in0=sc4[:, 1, :], in1=sc4[:, 2, :], op=ALU.mult), r=[s4B], w=[s4B])
                k.op("dve", lambda e: e.tensor_scalar(out=sc4[:, 5, :], in0=sc4[:, 1, :], scalar1=-1.0, scalar2=None, op0=ALU.mult), r=[s4B], w=[s4B])
                yield
                for grp in range(3):
                    b0, dB = dbl()
                    for q4 in range(4):
                        cc = grp * 4 + q4
                        for kc in range(8):
                            k.op("pe", lambda e: e.matmul(ps[:, b0, q4 * 128:(q4 + 1) * 128], lhsT=win[:, kc, NTM + cc * 128:NTM + (cc + 1) * 128],
                                                          rhs=uTt[:, kc, :], start=(kc == 0), stop=(kc == 7)), r=[uB, wB_], w=dB)
                    k.op("act", lambda e: e.activation(out=xc[:, grp * 4:(grp + 1) * 4, 3:131],
                                                       in_=ps[:, b0, :].rearrange("p (a c) -> p a c", a=4), func=AF.Identity),
                         r=dB, w=[xcB])
                    yield
                tmv = tm[:, 0:768].rearrange("p (a c) -> p a c", a=6)
                for half in range(2):
                    yv = hd[2 + half][:]
                    yB = hdB[2 + half]
                    xs_ = lambda j: xc[:, 6 * half:6 * half + 6, j:j + 128]
                    wj_ = lambda j: convw[:, 6 * half:6 * half + 6, j:j + 1].to_broadcast([128, 6, 128])
                    k.op("dve", lambda e: e.tensor_tensor(out=yv, in0=xs_(3), in1=wj_(3), op=ALU.mult), r=[xcB, gpB], w=yB)
                    for j in range(3):
                        k.op("dve", lambda e: e.tensor_tensor(out=tmv, in0=xs_(j), in1=wj_(j), op=ALU.mult), r=[xcB, gpB], w=[tmB])
                        k.op("dve", lambda e: e.tensor_tensor(out=yv, in0=yv, in1=tmv, op=ALU.add), r=yB + [tmB], w=yB)
                    yield
                k.op("pool", lambda e: e.tensor_copy(out=xc[:, :, 0:3], in_=xc[:, :, 128:131]), r=[xcB], w=[xcB])
                for hh in (2, 3):
                    k.op("act", lambda e: e.activation(out=hd[hh][:], in_=hd[hh][:], func=AF.Silu), r=hdB[hh], w=hdB[hh])
                for grp in range(3):
                    b0, dB = dbl()
                    for q4 in range(4):
                        cc = grp * 4 + q4
                        k.op("pe", lambda e: e.transpose(ps[:, b0, q4 * 128:(q4 + 1) * 128], ycv(cc), ident), r=[ycB(cc), cstB], w=dB)
                    k.op("act", lambda e: e.activation(out=tm[:, grp * 512:(grp + 1) * 512], in_=ps[:, b0, :], func=AF.Identity), r=dB, w=[tmB])
                    yield
                for g8 in range(8):
                    k.op("act", lambda e: e.activation(out=junk[:], in_=tm[:, g8 * 128:(g8 + 1) * 128], func=AF.Square,
                                                       accum_out=ss[:, g8:g8 + 1]), r=[tmB], w=[jkB, s4B])
                k.op("dve", lambda e: e.tensor_scalar(out=rs[:], in0=ss[:], scalar1=1e-6, scalar2=None, op0=ALU.add), r=[s4B], w=[s4B])
                k.op("act", lambda e: e.activation(out=rs[:], in_=rs[:], func=AF.Sqrt), r=[s4B], w=[s4B])
                k.op("dve", lambda e: e.reciprocal(out=rs[:], in_=rs[:]), r=[s4B], w=[s4B])
                k.op("dve", lambda e: e.tensor_scalar(out=sc4[:, 9, :], in0=rs[:, 0:4], scalar1=float(128 ** -0.5), scalar2=None, op0=ALU.mult),
                     r=[s4B], w=[s4B])
                k.op("dve", lambda e: e.tensor_tensor(out=sc4[:, 10, :], in0=rs[:, 4:8], in1=sc4[:, 4, :], op=ALU.mult), r=[s4B], w=[s4B])
                k.op("dve", lambda e: e.tensor_tensor(out=sc4[:, 11, :], in0=rs[:, 4:8], in1=sc4[:, 3, :], op=ALU.mult), r=[s4B], w=[s4B])
                k.op("dve", lambda e: e.tensor_tensor(out=sc4[:, 12, :], in0=sc4[:, 9, :], in1=sc4[:, 2, :], op=ALU.mult), r=[s4B], w=[s4B])
                yield

            flags = {"idx": -1, "g1": -1}

            def attn_path(t):
                W = (t + 1) * 128
                if t < 2:
                    flags["idx"] = t
                if t >= 2:
                    nrel = 0
                    for s0 in range(0, W, 512):
                        s1 = min(s0 + 512, W)
                        sw = s1 - s0
                        for h in range(8):
                            ri = nrel % 3
                            rb = 4 + nrel % 4
                            nrel += 1
                            k.op("pe", lambda e: e.matmul(ps[:, rb, 0:sw], lhsT=qiT[:, h, :], rhs=kiT[:, s0:s1], start=True, stop=True),
                                 r=[qiB] + kvBs[s0 // 128:(s1 + 127) // 128], w=[psB[rb]])
                            k.op("act", lambda e: e.activation(out=rel[ri][:, 0:sw], in_=ps[:, rb, 0:sw], func=AF.Relu,
                                                               scale=absw[:, h:h + 1]), r=[psB[rb], awB], w=[relB[ri]])
                            if h == 0:
                                k.op("dve", lambda e: e.tensor_scalar(out=score[:, s0:s1], in0=rel[ri][:, 0:sw], scalar1=sgn[:, 0:1],
                                                                      scalar2=None, op0=ALU.mult), r=[relB[ri], awB], w=[scB])
                            else:
                                k.op("dve", lambda e: e.scalar_tensor_tensor(out=score[:, s0:s1], in0=rel[ri][:, 0:sw], scalar=sgn[:, h:h + 1],
                                                                             in1=score[:, s0:s1], op0=ALU.mult, op1=ALU.add),
                                     r=[relB[ri], awB, scB], w=[scB])
                            if h % 2 == 1:
                                yield
                    flags["idx"] = t
                    k.op("pool", lambda e: e.affine_select(out=score[:, t * 128:W], in_=score[:, t * 128:W], pattern=[[-1, 128]],
                                                           compare_op=ALU.is_ge, fill=NEGFILL, base=0, channel_multiplier=1),
                         r=[scB], w=[scB])
                    k.op("dve", lambda e: e.tensor_reduce(out=bs[:, 0:1], in_=score[:, 0:t * 128], axis=AX.X, op=ALU.min), r=[scB], w=[m8B])
                    k.op("dve", lambda e: e.tensor_reduce(out=bs[:, 1:2], in_=score[:, 0:W], axis=AX.X, op=ALU.max), r=[scB], w=[m8B])
                    k.op("dve", lambda e: e.tensor_tensor(out=bs[:, 2:3], in0=bs[:, 1:2], in1=bs[:, 0:1], op=ALU.subtract), r=[m8B], w=[m8B])
                    k.op("dve", lambda e: e.tensor_scalar(out=wkt[:], in0=cst[:, C_POW:C_POW + 24], scalar1=bs[:, 2:3], scalar2=None, op0=ALU.mult),
                         r=[m8B, cstB], w=[m8B])
                    k.op("dve", lambda e: e.tensor_tensor(out=bs[:, 3:4], in0=bs[:, 0:1], in1=wkt[:, 0:1], op=ALU.add), r=[m8B], w=[m8B])
                    yield
                    for kk in range(NBIS):
                        k.op("dve", lambda e: e.tensor_scalar(out=mbias[:, 0:W], in0=score[:, 0:W], scalar1=bs[:, 3:4], scalar2=0.0,
                                                              op0=ALU.is_ge, op1=ALU.add, accum_out=bs[:, 6:7]), r=[scB, m8B], w=[mbB, m8B])
                        k.op("dve", lambda e: e.scalar_tensor_tensor(out=bs[:, 4:5], in0=bs[:, 6:7], scalar=255.5, in1=wkt[:, kk:kk + 1],
                                                                     op0=ALU.is_ge, op1=ALU.mult), r=[m8B], w=[m8B])
                        k.op("dve", lambda e: e.scalar_tensor_tensor(out=bs[:, 3:4], in0=bs[:, 4:5], scalar=wkt[:, kk + 1:kk + 2], in1=bs[:, 3:4],
                                                                     op0=ALU.subtract, op1=ALU.add), r=[m8B], w=[m8B])
                        yield
                    k.op("dve", lambda e: e.tensor_tensor(out=bs[:, 5:6], in0=bs[:, 3:4], in1=wkt[:, NBIS:NBIS + 1], op=ALU.subtract), r=[m8B], w=[m8B])
                    k.op("dve", lambda e: e.tensor_scalar(out=mbias[:, 0:W], in0=score[:, 0:W], scalar1=bs[:, 5:6], scalar2=MASKV,
                                                          op0=ALU.is_lt, op1=ALU.mult), r=[scB, m8B], w=[mbB])
                else:
                    k.op("pool", lambda e: e.memset(mbias[:, 0:W], 0.0), w=[mbB])
                    k.op("pool", lambda e: e.affine_select(out=mbias[:, t * 128:W], in_=mbias[:, t * 128:W], pattern=[[-1, 128]],
                                                           compare_op=ALU.is_ge, fill=MASKV, base=0, channel_multiplier=1),
                         r=[mbB], w=[mbB])
                yield
                pvB = [psB[6], psB[7]]
                items = [(kb, hf) for kb in range(t + 1) for hf in range(2)]

                def emit_st(i):
                    kb, hf = items[i]
                    sbk = 4 + i % 2
                    pi = i % 3
                    k.op("pe", lambda e: e.matmul(ps[:, sbk, :], lhsT=kT[:, kb * 128:(kb + 1) * 128],
                                                  rhs=qTs[t % 2][:, hf * 4:(hf + 1) * 4, :].rearrange("p h t -> p (h t)"),
                                                  start=True, stop=False), r=[kvBs[kb], qBs[t % 2]], w=[psB[sbk]])
                    k.op("pe", lambda e: e.matmul(ps[:, sbk, :], lhsT=mbias[:, kb * 128:(kb + 1) * 128], rhs=i8b[:],
                                                  start=False, stop=True), r=[mbB, cstB], w=[psB[sbk]])
                    k.op("act", lambda e: e.activation(out=PT[pi][:], in_=ps[:, sbk, :], func=AF.Exp, scale=0.125),
                         r=[psB[sbk]], w=[PTB[pi]])

                def emit_pv(i):
                    kb, hf = items[i]
                    pi = i % 3
                    for hh in range(4):
                        k.op("pe", lambda e: e.matmul(ps[:, 6 + hf, hh * 128:hh * 128 + 65], lhsT=PT[pi][:, hh * 128:(hh + 1) * 128],
                                                      rhs=vaug[:, kb, :], start=(kb == 0 and hh == 0), stop=(kb == t),
                                                      skip_group_check=True), r=[PTB[pi], kvBs[kb]], w=[psB[6 + hf]])

                for i in range(len(items)):
                    emit_st(i)
                    if i >= 1:
                        emit_pv(i - 1)
                    if i % 2 == 1:
                        yield
                emit_pv(len(items) - 1)
                pv = ps[:, 6:8, :].rearrange("p a (h c) -> p (a h) c", c=128)
                k.op("dve", lambda e: e.reciprocal(out=rec[:], in_=pv[:, :, 64]), r=pvB, w=[atB])
                k.op("dve", lambda e: e.tensor_tensor(out=attn[:], in0=pv[:, :, 0:64], in1=rec[:].unsqueeze(2).to_broadcast([128, 8, 64]),
                                                      op=ALU.mult), r=pvB + [atB], w=[atB])
                pbf = ps[:, 4, :].bitcast(BF16)
                for j in range(4):
                    k.op("pe", lambda e: e.transpose(pbf[:, j * 128:(j + 1) * 128], attn[:, 2 * j:2 * j + 2, :].rearrange("p h d -> p (h d)"), idb[:]),
                         r=[atB, cstB], w=[psB[4]])
                k.op("act", lambda e: e.activation(out=catT[:, 0:4, :].rearrange("p a t -> p (a t)"), in_=pbf[:, 0:512], func=AF.Identity),
                     r=[psB[4]], w=[catB])
                yield

            KH, KBG, QS, QD, VB, DG = range(6)

            def gdn_path(t):
                H4 = range(4)
                col = lambda s, h: sc4[:, s, h:h + 1]
                bk = lambda h: [psB[h]]
                for _ in gdn_pro(t):
                    yield
                for h in H4:
                    ksl = tm[:, 512 + h * 128:512 + (h + 1) * 128]
                    qsl = tm[:, h * 128:(h + 1) * 128]
                    vsl = tm[:, 1024 + h * 128:1024 + (h + 1) * 128]
                    for dst, dB_, src, sc_ in ((hd[h][:, KH, :], hdB[h][KH], ksl, rs[:, 4 + h:5 + h]),
                                               (hd[h][:, QS, :], hdB[h][QS], qsl, col(9, h)),
                                               (hd[h][:, QD, :], hdB[h][QD], qsl, col(12, h)),
                                               (hd[h][:, KBG, :], hdB[h][KBG], ksl, col(10, h)),
                                               (kd[h][:], kdB[h], ksl, col(11, h)),
                                               (hd[h][:, VB, :], hdB[h][VB], vsl, col(1, h))):
                        k.op("act", lambda e: e.activation(out=dst, in_=src, func=AF.Identity, scale=sc_), r=[tmB, s4B], w=[dB_])
                    k.op("dve", lambda e: e.tensor_scalar(out=hd[h][:, DG, :], in0=ident, scalar1=gg[:, h:h + 1], scalar2=None, op0=ALU.mult),
                         r=[cstB, s4B], w=[hdB[h][DG]])
                    if h % 2 == 1:
                        yield
                flags["g1"] = t
                for h in H4:
                    for i, src in enumerate((KH, QS, QD)):
                        k.op("pe", lambda e: e.transpose(ps[:, h, i * 128:(i + 1) * 128], hd[h][:, src, :], ident), r=[hdB[h][src], cstB], w=bk(h))
                yield
                for h in H4:
                    k.op("act", lambda e: e.activation(out=T3[h][:].rearrange("p a t -> p (a t)"), in_=ps[:, h, 0:384], func=AF.Identity),
                         r=bk(h), w=[T3B[h]])
                yield
                for h in H4:
                    k.op("pe", lambda e: e.matmul(ps[:, h, 0:128], lhsT=T3[h][:, 0, :], rhs=T3[h][:, 0, :], start=True, stop=True), r=[T3B[h]], w=bk(h))
                    k.op("pe", lambda e: e.matmul(ps[:, h, 128:256], lhsT=T3[h][:, 0, :], rhs=T3[h][:, 1, :], start=True, stop=True), r=[T3B[h]], w=bk(h))
                    k.op("pe", lambda e: e.matmul(ps[:, h, 256:384], lhsT=cst[:, C_ONES:C_ONES + 128], rhs=hd[h][:, DG, :], start=True, stop=True),
                         r=[hdB[h][DG], cstB], w=bk(h))
                yield
                for h in H4:
                    tt1 = AN[h][1][:, 0, :]
                    tt2 = AN[h][1][:, 1, :]
                    k.op("dve", lambda e: e.scalar_tensor_tensor(out=tt1, in0=ps[:, h, 256:384], scalar=gg[:, h:h + 1],
                                                                 in1=cst[:, C_MB1:C_MB1 + 128], op0=ALU.subtract, op1=ALU.subtract),
                         r=bk(h) + [s4B, cstB], w=[ANB[h][1]])
                    k.op("dve", lambda e: e.scalar_tensor_tensor(out=tt2, in0=ps[:, h, 256:384], scalar=gg[:, h:h + 1],
                                                                 in1=cst[:, C_MB2:C_MB2 + 128], op0=ALU.subtract, op1=ALU.add),
                         r=bk(h) + [s4B, cstB], w=[ANB[h][1]])
                yield
                for h in H4:
                    k.op("act", lambda e: e.activation(out=AN[h][1][:, 0, :], in_=AN[h][1][:, 0, :], func=AF.Exp, scale=-1.0),
                         r=[ANB[h][1]], w=[ANB[h][1]])
                    k.op("act", lambda e: e.activation(out=AN[h][1][:, 1, :], in_=AN[h][1][:, 1, :], func=AF.Exp), r=[ANB[h][1]], w=[ANB[h][1]])
                yield
                for h in H4:
                    k.op("dve", lambda e: e.scalar_tensor_tensor(out=AN[h][0][:, 1, :], in0=ps[:, h, 0:128], scalar=col(5, h), in1=AN[h][1][:, 0, :],
                                                                 op0=ALU.mult, op1=ALU.mult), r=bk(h) + [s4B, ANB[h][1]], w=[ANB[h][0]])
                    k.op("dve", lambda e: e.tensor_tensor(out=aqk[h][:], in0=ps[:, h, 128:256], in1=AN[h][1][:, 1, :], op=ALU.mult),
                         r=bk(h) + [ANB[h][1]], w=[aqB[h]])
                yield
                for h in H4:
                    k.op("pe", lambda e: e.transpose(ps[:, h, 0:128], AN[h][0][:, 1, :], ident), r=[ANB[h][0], cstB], w=bk(h))
                yield
                for h in H4:
                    k.op("act", lambda e: e.activation(out=AN[h][0][:, 0, :], in_=ps[:, h, 0:128], func=AF.Identity), r=bk(h), w=[ANB[h][0]])
                yield
                for h in H4:
                    k.op("dve", lambda e: e.tensor_tensor(out=Mm[h][0][:], in0=AN[h][0][:, 0, :], in1=ident, op=ALU.add),
                         r=[ANB[h][0], cstB], w=[MB[h][0]])

                def sq(h, cur):
                    k.op("pe", lambda e: e.matmul(ps[:, h, 0:128], lhsT=AN[h][cur][:, 1, :], rhs=AN[h][cur][:, 0, :], start=True, stop=True),
                         r=[ANB[h][cur]], w=bk(h))
                    k.op("pe", lambda e: e.matmul(ps[:, h, 128:256], lhsT=AN[h][cur][:, 0, :], rhs=AN[h][cur][:, 1, :], start=True, stop=True),
                         r=[ANB[h][cur]], w=bk(h))

                def ev(h, nxt):
                    k.op("act", lambda e: e.activation(out=AN[h][nxt][:].rearrange("p a c -> p (a c)"), in_=ps[:, h, 0:256], func=AF.Identity),
                         r=bk(h), w=[ANB[h][nxt]])

                def pr(h, an, mc):
                    k.op("pe", lambda e: e.matmul(ps[:, h, 256:384], lhsT=AN[h][an][:, 1, :], rhs=Mm[h][mc][:], start=True, stop=True),
                         r=[ANB[h][an], MB[h][mc]], w=bk(h))

                def ad(h, mc):
                    k.op("dve", lambda e: e.tensor_tensor(out=Mm[h][1 - mc][:], in0=ps[:, h, 256:384], in1=Mm[h][mc][:], op=ALU.add),
                         r=bk(h) + [MB[h][mc]], w=[MB[h][1 - mc]])

                for h in H4:
                    sq(h, 0)
                yield
                for h in H4:
                    ev(h, 1)
                yield
                for it in range(1, 6):
                    an = it % 2
                    for h in H4:
                        if it < 5:
                            sq(h, an)
                        pr(h, an, (it - 1) % 2)
                    yield
                    for h in H4:
                        if it < 5:
                            ev(h, 1 - an)
                        ad(h, (it - 1) % 2)
                    yield
                mf = 5 % 2
                for h in H4:
                    k.op("pe", lambda e: e.matmul(ps[:, h, 0:128], lhsT=Mm[h][mf][:], rhs=hd[h][:, VB, :], start=True, stop=True),
                         r=[MB[h][mf], hdB[h][VB]], w=bk(h))
                    k.op("pe", lambda e: e.matmul(ps[:, h, 128:256], lhsT=hd[h][:, KBG, :], rhs=Mm[h][mf][:], start=True, stop=True),
                         r=[MB[h][mf], hdB[h][KBG]], w=bk(h))
                yield
                for h in H4:
                    k.op("act", lambda e: e.activation(out=U[h][:], in_=ps[:, h, 0:128], func=AF.Identity), r=bk(h), w=[UB[h]])
                    k.op("act", lambda e: e.activation(out=WT[h][:], in_=ps[:, h, 128:256], func=AF.Identity), r=bk(h), w=[WTB[h]])
                yield
                for c in range(2):
                    r0, r1 = 64 * c, 64 * c + 64
                    for h in H4:
                        k.op("pe", lambda e: e.matmul(ps[:, h, 0:128], lhsT=WT[h][:], rhs=Sb[:, h, :], start=True, stop=True),
                             r=[WTB[h], SbB[h]], w=bk(h))
                    yield
                    for h in H4:
                        k.op("dve", lambda e: e.tensor_tensor(out=dl[h][r0:r1, :], in0=U[h][r0:r1, :], in1=ps[r0:r1, h, 0:128], op=ALU.subtract),
                             r=bk(h) + [UB[h]], w=[dlB[h]])
                    yield
                    for h in H4:
                        k.op("pe", lambda e: e.matmul(ps[:, h, 256:384], lhsT=kd[h][r0:r1, :], rhs=dl[h][r0:r1, :], start=True, stop=True),
                             r=[kdB[h], dlB[h]], w=bk(h))
                        k.op("pe", lambda e: e.matmul(ps[:, h, 128:256], lhsT=T3[h][:, 2, :], rhs=Sb[:, h, :], start=True, stop=False),
                             r=[T3B[h], SbB[h]], w=bk(h))
                        k.op("pe", lambda e: e.matmul(ps[:, h, 128:256], lhsT=aqk[h][r0:r1, :], rhs=dl[h][r0:r1, :], start=False, stop=True),
                             r=[aqB[h], dlB[h]], w=bk(h))
                    yield
                    for h in H4:
                        k.op("dve", lambda e: e.scalar_tensor_tensor(out=Sb[:, h, :], in0=S[:, h, :], scalar=sc4[:, 6 + c, h:h + 1],
                                                                     in1=ps[:, h, 256:384], op0=ALU.mult, op1=ALU.add),
                             r=bk(h) + [s4B, SB_[h]], w=[SbB[h]])
                        k.op("dve", lambda e: e.scalar_tensor_tensor(out=S[:, h, :], in0=S[:, h, :], scalar=sc4[:, 6 + c, h:h + 1],
                                                                     in1=ps[:, h, 256:384], op0=ALU.mult, op1=ALU.add),
                             r=bk(h) + [s4B, SB_[h]], w=[SB_[h]])
                        k.op("act", lambda e: e.activation(out=otm[r0:r1, h, :], in_=ps[r0:r1, h, 128:256], func=AF.Identity), r=bk(h), w=[otB[h]])
                    yield
                for h in H4:
                    k.op("act", lambda e: e.activation(out=junk[:], in_=otm[:, h, :], func=AF.Square, accum_out=ss[:, h:h + 1]),
                         r=[otB[h]], w=[jkB, s4B])
                k.op("dve", lambda e: e.tensor_scalar(out=rs[:, 0:4], in0=ss[:, 0:4], scalar1=float(1.0 / 128), scalar2=1e-6,
                                                      op0=ALU.mult, op1=ALU.add), r=[s4B], w=[s4B])
                k.op("act", lambda e: e.activation(out=rs[:, 0:4], in_=rs[:, 0:4], func=AF.Sqrt), r=[s4B], w=[s4B])
                k.op("dve", lambda e: e.reciprocal(out=rs[:, 0:4], in_=rs[:, 0:4]), r=[s4B], w=[s4B])
                yield
                for h in H4:
                    k.op("dve", lambda e: e.scalar_tensor_tensor(out=otm[:, h, :], in0=otm[:, h, :], scalar=rs[:, h:h + 1], in1=dng[:],
                                                                 op0=ALU.mult, op1=ALU.mult), r=[otB[h], s4B, gpB], w=[otB[h]])
                k.op("dve", lambda e: e.tensor_tensor(out=dn[:], in0=otm[:].rearrange("p h d -> p (h d)"), in1=sz[:], op=ALU.mult),
                     r=otB + [szB], w=[dnB])
                yield
                pbf = ps[:, 0, :].bitcast(BF16)
                for j in range(4):
                    k.op("pe", lambda e: e.transpose(pbf[:, j * 128:(j + 1) * 128], dn[:, j * 128:(j + 1) * 128], idb[:]), r=[dnB, cstB], w=[psB[0]])
                k.op("act", lambda e: e.activation(out=catT[:, 4:8, :].rearrange("p a t -> p (a t)"), in_=pbf[:, 0:512], func=AF.Identity),
                     r=[psB[0]], w=[catB])
                yield

            def epilogue(t):
                xa, xB1 = xts[t % 2], xtB[t % 2]
                b0, dB = 2, [psB[2], psB[3]]
                for hf in range(2):
                    for fc in range(8):
                        k.op("pe", lambda e: e.matmul(ps[:, b0 + hf, :], lhsT=catT[:, fc, :], rhs=wout[:, fc, hf * 512:(hf + 1) * 512],
                                                      start=(fc == 0), stop=(fc == 7)), r=[catB, wB_], w=dB)
                k.op("dve", lambda e: e.scalar_tensor_tensor(out=xa[:].rearrange("p (a c) -> p a c", a=2),
                                                             in0=xa[:].rearrange("p (a c) -> p a c", a=2), scalar=ALPHA,
                                                             in1=ps[:, b0:b0 + 2, :], op0=ALU.mult, op1=ALU.add), r=dB + [xB1], w=[xB1])
                layer_norm_tile(xa[:], xB1, lnbc, lnB, st6, mv, tmpc, smB)
                k.dma(xs_d[t * 128:(t + 1) * 128, :], xa[:], r=[xB1], w=[xsB[t]])

            for _ in p1(0):
                pass
            for t in range(NT):
                ga = attn_path(t)
                gg_ = gdn_path(t)
                gp = p1(t + 1) if t + 1 < NT else None
                npf = 0
                alive = [ga, gg_]
                while alive:
                    for g in list(alive):
                        try:
                            next(g)
                        except StopIteration:
                            alive.remove(g)
                    if gp is not None and not NOINTER and flags["idx"] == t and flags["g1"] == t and npf < PFMAX:
                        npf += 1
                        try:
                            next(gp)
                        except StopIteration:
                            gp = None
                if gp is not None:
                    for _ in gp:
                        pass
                epilogue(t)

        for b in range(2):
            ffn_phase(b, 0)
            if stage >= 2:
                mixer_phase(b)
            if stage >= 3:
                ffn_phase(b, 1)
        if stage < 3:
            with nc.sbuf_tensor("dbgt", [128, D], F32) as dbg:
                dB_ = Bf()
                for t in range(NT):
                    k.dma(dbg[:], xs_d[t * 128:(t + 1) * 128, :], r=[xsB[t]], w=[dB_])
                    k.dma(out_d[1, t * 128:(t + 1) * 128, :], dbg[:], r=[dB_], w=[outB[NT + t]])
        k.barrier()
    return nc


_PERM = None


def _perm():
    a_q = np.arange(0, 512); a_k = np.arange(512, 576); a_v = np.arange(576, 640)
    i_q = np.arange(640, 1152); i_k = np.arange(1152, 1216); i_w = np.arange(1216, 1224)
    b_q = np.arange(1224, 1736); b_k = np.arange(1736, 2248); b_v = np.arange(2248, 2760)
    b_z = np.arange(2760, 3272); b_a = np.arange(3272, 3276); b_b = np.arange(3276, 3280)
    return np.concatenate([a_q, i_q, a_k, i_k, a_v, i_w, b_a, b_b, b_z, b_q, b_k, b_v])


def make_in_maps(inp):
    f = lambda a: np.ascontiguousarray(np.asarray(a))
    shared = {}
    shared["wada"] = f(np.asarray(inp["w_ada"])[0].reshape(8, 128, 72, 128).transpose(2, 1, 0, 3))
    shared["badaT"] = f(np.asarray(inp["b_ada"])[0].reshape(72, 128).T)
    ffn = ((1, inp["ffn1_w1"], inp["ffn1_w3"], inp["ffn1_w2"]), (2, inp["ffn2_w1"], inp["ffn2_w3"], inp["ffn2_w2"]))
    for i, a1, a3, a2 in ffn:
        for nm, w in (("w1", a1), ("w3", a3)):
            w = np.asarray(w)[0]
            shared["%s_%d" % (nm, i)] = f(w.reshape(8, 128, NCH, 128).transpose(2, 1, 0, 3).reshape(NCH, 128, 1024))
        shared["w2_%d" % i] = f(np.asarray(a2)[0].reshape(NCH, 128, 1024))
    win = np.asarray(inp["w_in"])[0][:, _perm()]
    shared["win"] = f(win.reshape(8, 128, NCOL).transpose(1, 0, 2))
    shared["wout"] = f(np.asarray(inp["w_out"])[0].reshape(8, 128, D).transpose(1, 0, 2))
    shared["lnp"] = f(np.stack([np.asarray(inp[n])[0] for n in ("ln1_g", "ln1_b", "ln2_g", "ln2_b", "ln3_g", "ln3_b")]))
    shared["convT"] = f(np.asarray(inp["conv_w"])[0].reshape(4, 12, 128).transpose(2, 1, 0))
    shared["gdnp"] = f(np.concatenate([np.asarray(inp["a_log"])[0], np.asarray(inp["dt_bias"])[0]]))
    shared["dng"] = f(np.asarray(inp["dn_norm_g"])[0])
    shared["consts"] = make_consts()
    x = np.asarray(inp["x"]); c = np.asarray(inp["c"]); pos = np.asarray(inp["positions"])
    maps = []
    for core in range(8):
        m = dict(shared)
        b0 = 2 * core
        m["x"] = f(x[b0:b0 + 2])
        m["cT"] = f(c[b0:b0 + 2].reshape(2, 8, 128).transpose(2, 1, 0))
        m["pos"] = f(pos[b0:b0 + 2].reshape(2, NT, 128).transpose(2, 0, 1).astype(np.int32))
        maps.append(m)
    return maps


def kernel(**inputs):
    nc = build(STAGE)
    maps = make_in_maps(inputs)
    res = run_bass_kernel_spmd(nc, maps, core_ids=list(range(8)))
    out = np.concatenate([np.asarray(r["out"]) for r in res.results], axis=0)
    return out.astype(np.float32)
```

```python
import numpy as np
from contextlib import ExitStack
import concourse.bass as bass
import concourse.mybir as mybir
from concourse.bass_utils import run_bass_kernel_spmd

F32 = mybir.dt.float32
BF16 = mybir.dt.bfloat16
I32 = mybir.dt.int32
AF = mybir.ActivationFunctionType
ALU = mybir.AluOpType
AX = mybir.AxisListType

D = 1024
SEQ = 2048
NT = 16
DFF = 2816
NCH = 22
GROUPS = [4, 4, 4, 5, 5]
ALPHA = 2.0 ** 0.25
NCOL = 3280
NTM = 1744
NDS = 16
BIG = 1.0e5
NEGFILL = -1.0e30
MASKV = -30000.0
NOINTER = False
PFMAX = 99
STAGE = 99


class Bf:
    __slots__ = ("w", "r", "x")

    def __init__(self, x=False):
        self.w = None
        self.r = {}
        self.x = x


def Bs(n):
    return [Bf() for _ in range(n)]


class K:
    def __init__(self, nc, es):
        self.nc = nc
        self.eng = {"pe": nc.tensor, "dve": nc.vector, "act": nc.scalar, "pool": nc.gpsimd, "sp": nc.sync}
        self.sem = {e: es.enter_context(nc.semaphore("s_" + e)) for e in self.eng}
        self.cnt = {e: 0 for e in self.eng}
        self.seen = {e: {} for e in self.eng}
        self.dsem = [es.enter_context(nc.semaphore("dq%d" % i)) for i in range(NDS)]
        self.dval = [0] * NDS
        self.dnext = 0

    def _deps(self, e, r, w):
        deps = {}

        def add(ev):
            if ev is None:
                return
            key, sem, val = ev
            if key == "pe" and e == "pe":
                return
            if key not in deps or deps[key][1] < val:
                deps[key] = (sem, val)

        for b in r:
            add(b.w)
            if b.x:
                for ek, ev in b.r.items():
                    if ek != e:
                        add(ev)
        for b in w:
            add(b.w)
            for ev in b.r.values():
                add(ev)
        for key, (sem, val) in deps.items():
            if self.seen[e].get(key, 0) < val:
                self.eng[e].wait_ge(sem, val)
                self.seen[e][key] = val

    def _mark(self, e, ev, r, w):
        for b in r:
            b.r[ev[0]] = ev
        for b in w:
            b.w = ev
            b.r = {}

    def op(self, e, fn, r=(), w=()):
        self._deps(e, r, w)
        inst = fn(self.eng[e])
        self.cnt[e] += 1
        inst.then_inc(self.sem[e], 1)
        ev = (e, self.sem[e], self.cnt[e])
        self._mark(e, ev, r, w)
        return ev

    def dma(self, out, in_, r=(), w=(), e="sp"):
        i = self.dnext
        self.dnext = (i + 1) % NDS
        self._deps(e, r, w)
        key = ("d", i)
        if self.dval[i] > self.seen[e].get(key, 0):
            self.eng[e].wait_ge(self.dsem[i], self.dval[i])
            self.seen[e][key] = self.dval[i]
        inst = self.eng[e].dma_start(out=out, in_=in_)
        self.dval[i] += 16
        inst.then_inc(self.dsem[i], 16)
        ev = (key, self.dsem[i], self.dval[i])
        self._mark(e, ev, r, w)
        return ev

    def barrier(self):
        for e in self.eng:
            for o in self.eng:
                if o != e and self.cnt[o] > self.seen[e].get(o, 0):
                    self.eng[e].wait_ge(self.sem[o], self.cnt[o])
                    self.seen[e][o] = self.cnt[o]
            for i in range(NDS):
                key = ("d", i)
                if self.dval[i] > self.seen[e].get(key, 0):
                    self.eng[e].wait_ge(self.dsem[i], self.dval[i])
                    self.seen[e][key] = self.dval[i]


C_ID, C_ONES, C_TRI, C_OBD, C_SEL0, C_SEL1, C_MB1, C_MB2, C_INVF, C_I8, C_POW, C_END = (
    0, 128, 256, 384, 512, 640, 768, 896, 1024, 1056, 1568, 1592)
NBIS = 22


def make_consts():
    c = np.zeros((128, C_END), np.float32)
    idx = np.arange(128)
    same = (idx[:, None] // 64) == (idx[None, :] // 64)
    c[:, C_ID:C_ID + 128] = np.eye(128)
    c[:, C_ONES:C_ONES + 128] = 1.0
    c[:, C_TRI:C_TRI + 128] = (same & (idx[:, None] <= idx[None, :]))
    c[:, C_OBD:C_OBD + 128] = same
    c[:, C_SEL0:C_SEL0 + 128] = (idx[:, None] < 64)
    c[:, C_SEL1:C_SEL1 + 128] = (idx[:, None] >= 64)
    c[:, C_MB1:C_MB1 + 128] = np.where(same & (idx[:, None] > idx[None, :]), 0.0, -BIG)
    c[:, C_MB2:C_MB2 + 128] = np.where(same & (idx[None, :] >= idx[:, None]), 0.0, -BIG)
    inv = (10000.0 ** (-np.arange(0, 64, 2, dtype=np.float32) / np.float32(64))).astype(np.float32)
    c[:, C_INVF:C_INVF + 32] = inv[None, :]
    c[:, C_I8:C_I8 + 512] = np.tile(np.eye(128, dtype=np.float32), (1, 4))
    c[:, C_POW:C_POW + 24] = (0.5 ** np.arange(1, 25, dtype=np.float64))[None, :]
    return c


def build(stage=99):
    nc = bass.Bass("TRN2", target_bir_lowering=False)
    dt = lambda n, s, d=F32, kind="ExternalInput": nc.dram_tensor(n, s, d, kind=kind).ap()
    x_d = dt("x", [2, SEQ, D])
    cT_d = dt("cT", [128, 8, 2])
    pos_d = dt("pos", [128, 2, NT], I32)
    wada_d = dt("wada", [72, 128, 8, 128])
    bada_d = dt("badaT", [128, 72])
    fw = []
    for i in (1, 2):
        fw.append((dt("w1_%d" % i, [NCH, 128, 1024]), dt("w3_%d" % i, [NCH, 128, 1024]), dt("w2_%d" % i, [NCH, 128, 1024])))
    win_d = dt("win", [128, 8, NCOL])
    wout_d = dt("wout", [128, 8, D])
    lnp_d = dt("lnp", [6, D])
    conv_d = dt("convT", [128, 12, 4])
    gdnp_d = dt("gdnp", [8])
    dng_d = dt("dng", [128])
    const_d = dt("consts", [128, C_END])
    out_d = dt("out", [2, SEQ, D], kind="ExternalOutput")
    xs_d = dt("xs", [SEQ, D], kind="Internal")

    with ExitStack() as es:
        k = K(nc, es)
        uid = [0]

        def _alloc(stack, n, s, d):
            uid[0] += 1
            return stack.enter_context(nc.sbuf_tensor("%s_%d" % (n, uid[0]), s, d))
        sb = lambda n, s, d=F32: _alloc(es, n, s, d)
        ps = es.enter_context(nc.psum_tensor("ps", [128, 8, 512], F32))
        cst = sb("cst", [128, C_END])
        cstB = Bf()
        idb = sb("idb", [128, 128], BF16)
        i8b = sb("i8b", [128, 512], BF16)
        modT = sb("modT", [128, 72, 2])
        modB = Bf()
        xsB = Bs(NT)
        outB = Bs(2 * NT)
        psB = [Bf(True) for _ in range(8)]
        ident = cst[:, C_ID:C_ID + 128]

        k.dma(cst[:], const_d, w=[cstB])
        k.op("pool", lambda e: e.tensor_copy(out=idb[:], in_=cst[:, C_ID:C_ID + 128]), r=[cstB], w=[cstB])
        k.op("pool", lambda e: e.tensor_copy(out=i8b[:], in_=cst[:, C_I8:C_I8 + 512]), r=[cstB], w=[cstB])

        with ExitStack() as es2:
            sb2 = lambda n, s, d=F32: _alloc(es2, n, s, d)
            wsl = [sb2("wsl%d" % i, [128, 8, 1024]) for i in range(2)]
            wslB = Bs(2)
            scT = sb2("scT", [128, 8, 2])
            bad = sb2("bad", [128, 72])
            scB = Bf()
            k.dma(scT[:], cT_d, w=[scB])
            k.dma(bad[:], bada_d, w=[scB])
            k.op("act", lambda e: e.activation(out=scT[:], in_=scT[:], func=AF.Silu), r=[scB], w=[scB])
            for slab in range(9):
                s = slab % 2
                k.dma(wsl[s][:], wada_d[slab * 8:(slab + 1) * 8].rearrange("j p k f -> p j (k f)"), w=[wslB[s]])
                for jj in range(8):
                    j = slab * 8 + jj
                    for kc in range(8):
                        k.op("pe", lambda e: e.matmul(ps[:, 0, 2 * j:2 * j + 2], lhsT=wsl[s][:, jj, kc * 128:(kc + 1) * 128],
                                                      rhs=scT[:, kc, :], start=(kc == 0), stop=(kc == 7)),
                             r=[wslB[s], scB], w=[psB[0]])
            for b in range(2):
                k.op("dve", lambda e: e.tensor_tensor(out=modT[:, :, b], in0=ps[:, 0, b:144:2], in1=bad[:], op=ALU.add),
                     r=[psB[0], scB], w=[modB])
            for v in (1, 4, 7):
                k.op("dve", lambda e: e.tensor_scalar(out=modT[:, v * 8:v * 8 + 8, :], in0=modT[:, v * 8:v * 8 + 8, :],
                                                      scalar1=1.0, scalar2=None, op0=ALU.add), r=[modB], w=[modB])
            for v in (2, 8):
                k.op("dve", lambda e: e.tensor_scalar(out=modT[:, v * 8:v * 8 + 8, :], in0=modT[:, v * 8:v * 8 + 8, :],
                                                      scalar1=0.5, scalar2=None, op0=ALU.mult), r=[modB], w=[modB])
            k.barrier()

        def gen_gate_row(v, b, gbc, gbcB, colbc, colB):
            for kc in range(8):
                k.op("dve", lambda e: e.tensor_scalar(out=colbc[:], in0=cst[:, C_ONES:C_ONES + 128],
                                                      scalar1=modT[:, v * 8 + kc, b:b + 1], scalar2=None, op0=ALU.mult),
                     r=[modB, cstB], w=[colB])
                bank = 6 + kc // 4
                k.op("pe", lambda e: e.matmul(ps[:, bank, (kc % 4) * 128:(kc % 4 + 1) * 128], lhsT=colbc[:], rhs=ident,
                                              start=True, stop=True), r=[colB, cstB], w=[psB[bank]])
            for hb in range(2):
                k.op("act", lambda e: e.activation(out=gbc[:, hb * 512:(hb + 1) * 512], in_=ps[:, 6 + hb, :], func=AF.Identity),
                     r=[psB[6 + hb]], w=[gbcB])

        def layer_norm_tile(xt_ap, xB, lnbc, lnB, st6, mv, tmpc, smB):
            for hh in range(2):
                k.op("dve", lambda e: e.bn_stats(out=st6[:, hh, :], in_=xt_ap[:, hh * 512:(hh + 1) * 512]), r=[xB], w=[smB])
            k.op("dve", lambda e: e.bn_aggr(out=mv[:], in_=st6[:].rearrange("p a b -> p (a b)")), r=[smB], w=[smB])
            k.op("dve", lambda e: e.tensor_scalar(out=tmpc[:, 0:1], in0=mv[:, 1:2], scalar1=1e-5, scalar2=None, op0=ALU.add),
                 r=[smB], w=[smB])
            k.op("act", lambda e: e.activation(out=tmpc[:, 0:1], in_=tmpc[:, 0:1], func=AF.Sqrt), r=[smB], w=[smB])
            k.op("dve", lambda e: e.reciprocal(out=tmpc[:, 0:1], in_=tmpc[:, 0:1]), r=[smB], w=[smB])
            k.op("dve", lambda e: e.tensor_scalar(out=tmpc[:, 1:2], in0=mv[:, 0:1], scalar1=tmpc[:, 0:1], scalar2=-1.0,
                                                  op0=ALU.mult, op1=ALU.mult), r=[smB], w=[smB])
            k.op("act", lambda e: e.activation(out=xt_ap, in_=xt_ap, func=AF.Identity, scale=tmpc[:, 0:1], bias=tmpc[:, 1:2]),
                 r=[smB, xB], w=[xB])
            k.op("dve", lambda e: e.tensor_tensor(out=xt_ap, in0=xt_ap, in1=lnbc[:, 0, :], op=ALU.mult), r=[xB, lnB], w=[xB])
            k.op("pool", lambda e: e.tensor_tensor(out=xt_ap, in0=xt_ap, in1=lnbc[:, 1, :], op=ALU.add), r=[xB, lnB], w=[xB])

        def ffn_phase(b, which):
            v0 = 0 if which == 0 else 6
            w1d, w3d, w2d = fw[which]
            lnrow = 0 if which == 0 else 4
            with ExitStack() as es2:
                sb2 = lambda n, s, d=F32: _alloc(es2, n, s, d)
                xres = sb2("xres", [128, NT, D])
                xB = Bs(NT)
                uT = sb2("uT", [128, 8, SEQ], BF16)
                uB = Bs(NT)
                wb = [[sb2("wg%d_%d" % (s, m), [128, 5, 1024], BF16) for m in range(3)] for s in range(2)]
                wB = [[Bs(3) for _ in range(5)] for _ in range(2)]
                stg = [sb2("stg%d" % i, [128, 1024]) for i in range(3)]
                stgB = Bs(3)
                gbc = sb2("gbc", [128, 1024])
                gbcB = Bf()
                colbc = sb2("colbc", [128, 128])
                colB = Bf()
                lnbc = sb2("lnbc", [128, 2, D])
                lnB = Bf()
                sT = [sb2("sT%d" % i, [128, 256], BF16) for i in range(3)]
                sTB = Bs(3)
                gT = [sb2("gT%d" % i, [128, 256], BF16) for i in range(3)]
                gTB = Bs(3)
                HBK = (0, 1, 6)
                st6 = sb2("st6", [128, 2, 6])
                mv = sb2("mv", [128, 2])
                tmpc = sb2("tmpc", [128, 2])
                smB = Bf()
                poB = Bs(2)

                gen_gate_row(v0 + 2, b, gbc, gbcB, colbc, colB)
                for i in range(2):
                    k.dma(lnbc[:, i, :], lnp_d[lnrow + i, :].partition_broadcast(128), w=[lnB])
                nstg = [0]

                def load_chunk(cg, slot, ci):
                    for m, src in enumerate((w1d, w3d, w2d)):
                        s = nstg[0] % 3
                        nstg[0] += 1
                        k.dma(stg[s][:], src[cg], w=[stgB[s]])
                        if m < 2:
                            k.op("act", lambda e: e.activation(out=wb[slot][m][:, ci, :], in_=stg[s][:], func=AF.Identity),
                                 r=[stgB[s]], w=[wB[slot][ci][m]])
                        else:
                            k.op("pool", lambda e: e.tensor_tensor(out=wb[slot][m][:, ci, :], in0=stg[s][:], in1=gbc[:], op=ALU.mult),
                                 r=[stgB[s], gbcB], w=[wB[slot][ci][m]])

                gstart = [sum(GROUPS[:g]) for g in range(len(GROUPS))]
                def load_tile(t):
                    if which == 0:
                        k.dma(xres[:, t, :], x_d[b, t * 128:(t + 1) * 128, :], w=[xB[t]])
                    else:
                        k.dma(xres[:, t, :], xs_d[t * 128:(t + 1) * 128, :], r=[xsB[t]], w=[xB[t]])

                for ci in range(GROUPS[0]):
                    load_chunk(ci, 0, ci)
                    load_tile(2 * ci)
                    load_tile(2 * ci + 1)
                for t in range(2 * GROUPS[0], NT):
                    load_tile(t)

                def prep_round(t, rnd):
                    bank = 7
                    if True:
                        for kc in range(4 * rnd, 4 * rnd + 4):
                            k.op("pe", lambda e: e.transpose(ps[:, bank, (kc % 4) * 128:(kc % 4 + 1) * 128],
                                                             xres[:, t, kc * 128:(kc + 1) * 128], ident),
                                 r=[xB[t], cstB], w=[psB[bank]])
                        for kc in range(4 * rnd, 4 * rnd + 4):
                            k.op("act", lambda e: e.activation(out=uT[:, kc, t * 128:(t + 1) * 128],
                                                               in_=ps[:, bank, (kc % 4) * 128:(kc % 4 + 1) * 128], func=AF.Identity,
                                                               scale=modT[:, (v0 + 1) * 8 + kc, b:b + 1], bias=modT[:, v0 * 8 + kc, b:b + 1]),
                                 r=[psB[bank], modB], w=[uB[t]])

                def finish_tile(t):
                    layer_norm_tile(xres[:, t, :], xB[t], lnbc, lnB, st6, mv, tmpc, smB)
                    if which == 0:
                        k.dma(xs_d[t * 128:(t + 1) * 128, :], xres[:, t, :], r=[xB[t]], w=[xsB[t]])
                    else:
                        k.dma(out_d[b, t * 128:(t + 1) * 128, :], xres[:, t, :], r=[xB[t]], w=[outB[b * NT + t]])

                pending = []
                step = 0
                for tt_ in range(2):
                    for rnd_ in range(2):
                        prep_round(tt_, rnd_)
                for g, gs in enumerate(GROUPS):
                    slot = g % 2
                    for blk in range(8):
                        if g + 1 < len(GROUPS) and blk < GROUPS[g + 1]:
                            load_chunk(gstart[g + 1] + blk, 1 - slot, blk)
                        for ci in range(gs):
                            if g == 0 and ci < 4 and blk + 1 < 8:
                                prep_round(2 * blk + 2 + ci // 2, ci % 2)
                            hi = step % 3
                            hb = HBK[hi]
                            step += 1
                            for m in range(2):
                                for kc in range(8):
                                    k.op("pe", lambda e: e.matmul(ps[:, hb, m * 256:(m + 1) * 256],
                                                                  lhsT=wb[slot][m][:, ci, kc * 128:(kc + 1) * 128],
                                                                  rhs=uT[:, kc, blk * 256:(blk + 1) * 256], start=(kc == 0), stop=(kc == 7)),
                                         r=[uB[2 * blk], uB[2 * blk + 1], wB[slot][ci][m]], w=[psB[hb]])
                            k.op("act", lambda e: e.activation(out=sT[hi][:], in_=ps[:, hb, 0:256], func=AF.Silu),
                                 r=[psB[hb]], w=[sTB[hi]])
                            k.op("dve", lambda e: e.tensor_tensor(out=gT[hi][:], in0=sT[hi][:], in1=ps[:, hb, 256:512], op=ALU.mult),
                                 r=[sTB[hi], psB[hb]], w=[gTB[hi]])
                            while len(pending) >= 2:
                                pending.pop(0)()

                            def w2_step(hb=hi, ci=ci, slot=slot, gs=gs, blk=blk, g=g):
                                for tt in range(2):
                                    for hf in range(2):
                                        k.op("pe", lambda e: e.matmul(ps[:, 2 + 2 * tt + hf, :], lhsT=gT[hb][:, tt * 128:(tt + 1) * 128],
                                                                      rhs=wb[slot][2][:, ci, hf * 512:(hf + 1) * 512],
                                                                      start=(ci == 0), stop=(ci == gs - 1)),
                                             r=[gTB[hb], wB[slot][ci][2]], w=[poB[tt]])
                                if ci == gs - 1:
                                    for tt in range(2):
                                        t = 2 * blk + tt
                                        xv = xres[:, t, :].rearrange("p (a c) -> p a c", a=2)
                                        if g == 0:
                                            k.op("dve", lambda e: e.scalar_tensor_tensor(out=xv, in0=xv, scalar=ALPHA, in1=ps[:, 2 + 2 * tt:4 + 2 * tt, :],
                                                                                         op0=ALU.mult, op1=ALU.add), r=[poB[tt], xB[t]], w=[xB[t]])
                                        else:
                                            k.op("dve", lambda e: e.tensor_tensor(out=xv, in0=xv, in1=ps[:, 2 + 2 * tt:4 + 2 * tt, :], op=ALU.add),
                                                 r=[poB[tt], xB[t]], w=[xB[t]])
                                        if g == len(GROUPS) - 1:
                                            finish_tile(t)
                            pending.append(w2_step)
                for fn in pending:
                    fn()
                k.barrier()

        def mixer_phase(b):
            with ExitStack() as es2:
                sb2 = lambda n, s, d=F32: _alloc(es2, n, s, d)
                win = sb2("win", [128, 8, NCOL], BF16)
                wout = sb2("wout", [128, 8, D], BF16)
                wB_ = Bf()
                with ExitStack() as es3:
                    sb3 = lambda n, s, d=F32: _alloc(es3, n, s, d)
                    stg = [sb3("mstg%d" % i, [128, 1024]) for i in range(3)]
                    stgB = Bs(3)
                    gbc = sb3("mgbc", [128, 1024])
                    gbcB = Bf()
                    colbc = sb3("mcolbc", [128, 128])
                    colB = Bf()
                    gen_gate_row(5, b, gbc, gbcB, colbc, colB)
                    n = 0
                    for kc in range(8):
                        for c0 in range(0, NCOL, 1024):
                            cw = min(1024, NCOL - c0)
                            s = n % 3
                            n += 1
                            k.dma(stg[s][:, 0:cw], win_d[:, kc, c0:c0 + cw], w=[stgB[s]])
                            if c0 < 2048:
                                k.op("act", lambda e: e.activation(out=win[:, kc, c0:c0 + cw], in_=stg[s][:, 0:cw], func=AF.Identity),
                                     r=[stgB[s]], w=[wB_])
                            elif c0 < 3072:
                                k.op("dve", lambda e: e.tensor_copy(out=win[:, kc, c0:c0 + cw], in_=stg[s][:, 0:cw]), r=[stgB[s]], w=[wB_])
                            else:
                                k.op("pool", lambda e: e.tensor_copy(out=win[:, kc, c0:c0 + cw], in_=stg[s][:, 0:cw]), r=[stgB[s]], w=[wB_])
                        s = n % 3
                        n += 1
                        k.dma(stg[s][:], wout_d[:, kc, :], w=[stgB[s]])
                        k.op("dve", lambda e: e.tensor_tensor(out=wout[:, kc, :], in0=stg[s][:], in1=gbc[:], op=ALU.mult),
                             r=[stgB[s], gbcB], w=[wB_])
                    k.barrier()
                mixer_tiles(b, win, wout, wB_, sb2)
                k.barrier()

        def mixer_tiles(b, win, wout, wB_, sb2):
            cosT = sb2("cosT", [128, NT, 32])
            sinT = sb2("sinT", [128, NT, 32])
            csB = Bf()
            with ExitStack() as es4:
                sb4 = lambda n, s, d=F32: _alloc(es4, n, s, d)
                posi = sb4("posi", [128, NT], I32)
                posf = sb4("posf", [128, NT])
                ang = sb4("ang", [128, NT, 32])
                angi = sb4("angi", [128, NT, 32], I32)
                angf = sb4("angf", [128, NT, 32])
                angm = sb4("angm", [128, NT, 32])
                k.dma(posi[:], pos_d[:, b, :], w=[csB])
                k.op("dve", lambda e: e.tensor_copy(out=posf[:], in_=posi[:]), r=[csB], w=[csB])
                k.op("dve", lambda e: e.tensor_tensor(out=ang[:], in0=posf[:].unsqueeze(2).to_broadcast([128, NT, 32]),
                                                      in1=cst[:, C_INVF:C_INVF + 32].unsqueeze(1).to_broadcast([128, NT, 32]), op=ALU.mult),
                     r=[csB, cstB], w=[csB])
                TWO_PI = 2.0 * np.pi
                C1 = 6.28125
                C2 = float(TWO_PI - C1)

                def reduce_sin(dst, shift):
                    k.op("dve", lambda e: e.tensor_scalar(out=angf[:], in0=ang[:], scalar1=float(shift), scalar2=float(1.0 / TWO_PI),
                                                          op0=ALU.add, op1=ALU.mult), r=[csB], w=[csB])
                    k.op("dve", lambda e: e.tensor_copy(out=angi[:], in_=angf[:]), r=[csB], w=[csB])
                    k.op("dve", lambda e: e.tensor_copy(out=angf[:], in_=angi[:]), r=[csB], w=[csB])
                    k.op("dve", lambda e: e.scalar_tensor_tensor(out=angm[:], in0=angf[:], scalar=-C1, in1=ang[:], op0=ALU.mult, op1=ALU.add),
                         r=[csB], w=[csB])
                    k.op("dve", lambda e: e.scalar_tensor_tensor(out=angm[:], in0=angf[:], scalar=-C2, in1=angm[:], op0=ALU.mult, op1=ALU.add),
                         r=[csB], w=[csB])
                    if shift != 0.0:
                        k.op("dve", lambda e: e.tensor_scalar(out=angm[:], in0=angm[:], scalar1=float(shift), scalar2=None, op0=ALU.add),
                             r=[csB], w=[csB])
                    k.op("dve", lambda e: e.tensor_scalar(out=angf[:], in0=angm[:], scalar1=float(np.pi), scalar2=-TWO_PI,
                                                          op0=ALU.is_gt, op1=ALU.mult), r=[csB], w=[csB])
                    k.op("dve", lambda e: e.tensor_tensor(out=angm[:], in0=angm[:], in1=angf[:], op=ALU.add), r=[csB], w=[csB])
                    k.op("dve", lambda e: e.tensor_scalar(out=angf[:], in0=angm[:], scalar1=float(-np.pi), scalar2=TWO_PI,
                                                          op0=ALU.is_lt, op1=ALU.mult), r=[csB], w=[csB])
                    k.op("dve", lambda e: e.tensor_tensor(out=angm[:], in0=angm[:], in1=angf[:], op=ALU.add), r=[csB], w=[csB])
                    k.op("dve", lambda e: e.tensor_scalar(out=angm[:], in0=angm[:], scalar1=float(-np.pi), scalar2=float(np.pi),
                                                          op0=ALU.max, op1=ALU.min), r=[csB], w=[csB])
                    k.op("act", lambda e: e.activation(out=dst[:], in_=angm[:], func=AF.Sin), r=[csB], w=[csB])

                reduce_sin(sinT, 0.0)
                reduce_sin(cosT, float(np.pi / 2))
                k.barrier()

            lnbc = sb2("mlnbc", [128, 2, D])
            lnB = Bf()
            kT = sb2("kT", [64, SEQ], BF16)
            kiT = sb2("kiT", [64, SEQ], BF16)
            vaug = sb2("vaug", [128, NT, 65], BF16)
            kvBs = Bs(NT)
            xts = [sb2("xt%d" % i, [128, D]) for i in range(2)]
            xtB = Bs(2)
            uTt = sb2("uTt", [128, 8, 128], BF16)
            uB = Bf()
            roped = sb2("roped", [128, 18, 64], BF16)
            ropB = Bf()
            qTs = [sb2("qT%d" % i, [64, 8, 128], BF16) for i in range(2)]
            qBs = Bs(2)
            qiT = sb2("qiT", [64, 8, 128], BF16)
            qiB = Bf()
            absw = sb2("absw", [128, 8])
            sgn = sb2("sgn", [128, 8])
            awB = Bf()
            score = sb2("score", [128, SEQ])
            scB = Bf()
            work = sb2("work", [128, SEQ])
            wkB = Bf()
            tok = work[:, 0:NTM]
            tokB = wkB
            mbias = sb2("mbias", [128, SEQ], BF16)
            mbB = Bf()
            bs = sb2("bs", [128, 8])
            wkt = sb2("wkt", [128, 24])
            m8B = Bf()
            rel = [sb2("rel%d" % i, [128, 512]) for i in range(3)]
            relB = Bs(3)
            PT = [sb2("PT%d" % i, [128, 512], BF16) for i in range(3)]
            PTB = Bs(3)
            rec = sb2("rec", [128, 8])
            attn = sb2("attn", [128, 8, 64], BF16)
            atB = Bf()
            catT = sb2("catT", [128, 8, 128], BF16)
            catB = Bf()
            st6 = sb2("mst6", [128, 2, 6])
            mv = sb2("mmv", [128, 2])
            tmpc = sb2("mtmpc", [128, 2])
            smB = Bf()
            xc = sb2("xc", [128, 12, 131])
            xcB = Bf()
            tm = sb2("tm", [128, 1536])
            tmB = Bf()
            junk = sb2("junk", [128, 128])
            jkB = Bf()
            ss = sb2("ss", [128, 8])
            rs = sb2("rs", [128, 8])
            sc4 = sb2("sc4", [128, 16, 4])
            s4B = Bf()
            gg = sb2("gg", [128, 16])
            gdnp = sb2("gdnp", [128, 8])
            negA = sb2("negA", [128, 4])
            dng = sb2("dng", [128, 128])
            convw = sb2("convw", [128, 12, 4])
            gpB = Bf()
            hd = [sb2("hd%d" % h, [128, 6, 128]) for h in range(4)]
            hdB = [Bs(6) for _ in range(4)]
            ycv = lambda cc: hd[2 + cc // 6][:, cc % 6, :]
            ycB = lambda cc: hdB[2 + cc // 6][cc % 6]
            rA = tm[:, 0:576].rearrange("p (h d) -> p h d", d=32)
            rBt = tm[:, 576:1152].rearrange("p (h d) -> p h d", d=32)
            rpB = [tmB]
            kd = [sb2("kd%d" % h, [128, 128], BF16) for h in range(4)]
            kdB = Bs(4)
            T3 = [sb2("T3%d" % h, [128, 3, 128], BF16) for h in range(4)]
            T3B = Bs(4)
            AN = [[sb2("AN%d_%d" % (h, i), [128, 2, 128]) for i in range(2)] for h in range(4)]
            ANB = [Bs(2) for _ in range(4)]
            Mm = [[sb2("Mm%d_%d" % (h, i), [128, 128]) for i in range(2)] for h in range(4)]
            MB = [Bs(2) for _ in range(4)]
            aqk = [sb2("aqk%d" % h, [128, 128], BF16) for h in range(4)]
            aqB = Bs(4)
            U = [sb2("U%d" % h, [128, 128]) for h in range(4)]
            UB = Bs(4)
            WT = [sb2("WT%d" % h, [128, 128], BF16) for h in range(4)]
            WTB = Bs(4)
            dl = [sb2("dl%d" % h, [128, 128], BF16) for h in range(4)]
            dlB = Bs(4)
            S = sb2("S", [128, 4, 128])
            Sb = sb2("Sb", [128, 4, 128], BF16)
            SB_ = Bs(4)
            SbB = Bs(4)
            otm = sb2("otm", [128, 4, 128])
            otB = Bs(4)
            sz = sb2("sz", [128, 512])
            szB = Bf()
            dn = sb2("dn", [128, 512], BF16)
            dnB = Bf()

            dbank = [0]

            def dbl():
                i = dbank[0] % 2
                dbank[0] += 1
                b0 = (0, 2)[i]
                return b0, [psB[b0], psB[b0 + 1]]

            for i in range(2):
                k.dma(lnbc[:, i, :], lnp_d[2 + i, :].partition_broadcast(128), w=[lnB])
            k.dma(gdnp[:], gdnp_d.partition_broadcast(128), w=[gpB])
            k.dma(dng[:], dng_d.partition_broadcast(128), w=[gpB])
            k.dma(convw[:], conv_d, w=[gpB])
            k.op("act", lambda e: e.activation(out=negA[:], in_=gdnp[:, 0:4], func=AF.Exp), r=[gpB], w=[gpB])
            k.op("dve", lambda e: e.tensor_scalar(out=negA[:], in0=negA[:], scalar1=-1.0, scalar2=None, op0=ALU.mult), r=[gpB], w=[gpB])
            k.op("pool", lambda e: e.memset(S[:], 0.0), w=SB_)
            k.op("pool", lambda e: e.memset(Sb[:], 0.0), w=SbB)
            k.op("pool", lambda e: e.memset(xc[:], 0.0), w=[xcB])
            k.op("pool", lambda e: e.memset(vaug[:], 1.0), w=kvBs)
            WC = float(8 ** -0.5 * 64 ** -0.5)

            def p1(t):
                xa, xB1 = xts[t % 2], xtB[t % 2]
                k.dma(xa[:], xs_d[t * 128:(t + 1) * 128, :], r=[xsB[t]], w=[xB1])
                b0, dB = 4, [psB[4], psB[5]]
                for kc in range(8):
                    bank = b0 + kc // 4
                    k.op("pe", lambda e: e.transpose(ps[:, bank, (kc % 4) * 128:(kc % 4 + 1) * 128], xa[:, kc * 128:(kc + 1) * 128], ident),
                         r=[xB1, cstB], w=dB)
                for kc in range(8):
                    bank = b0 + kc // 4
                    k.op("act", lambda e: e.activation(out=uTt[:, kc, :], in_=ps[:, bank, (kc % 4) * 128:(kc % 4 + 1) * 128],
                                                       func=AF.Identity, scale=modT[:, 32 + kc, b:b + 1], bias=modT[:, 24 + kc, b:b + 1]),
                         r=dB + [modB], w=[uB])
                yield
                for (c0, c1) in ((0, 1024), (1024, NTM)):
                    b0, dB = 4, [psB[4], psB[5]]
                    for s0 in range(c0, c1, 512):
                        s1 = min(s0 + 512, c1)
                        bank = b0 + (s0 - c0) // 512
                        for kc in range(8):
                            k.op("pe", lambda e: e.matmul(ps[:, bank, 0:s1 - s0], lhsT=uTt[:, kc, :], rhs=win[:, kc, s0:s1],
                                                          start=(kc == 0), stop=(kc == 7)), r=[uB, wB_], w=dB)
                        k.op("act", lambda e: e.activation(out=tok[:, s0:s1], in_=ps[:, bank, 0:s1 - s0], func=AF.Identity),
                             r=dB, w=[tokB])
                    yield
                tk = tok[:, 0:1152].rearrange("p (h d) -> p h d", d=64)
                cb = cosT[:, t, :].unsqueeze(1).to_broadcast([128, 18, 32])
                sbb = sinT[:, t, :].unsqueeze(1).to_broadcast([128, 18, 32])
                k.op("dve", lambda e: e.tensor_tensor(out=rA, in0=tk[:, :, 0:32], in1=cb, op=ALU.mult), r=[tokB, csB], w=rpB)
                k.op("dve", lambda e: e.tensor_tensor(out=rBt, in0=tk[:, :, 32:64], in1=sbb, op=ALU.mult), r=[tokB, csB], w=rpB)
                k.op("dve", lambda e: e.tensor_tensor(out=roped[:, :, 0:32], in0=rA, in1=rBt, op=ALU.subtract), r=rpB, w=[ropB])
                k.op("dve", lambda e: e.tensor_tensor(out=rA, in0=tk[:, :, 32:64], in1=cb, op=ALU.mult), r=[tokB, csB, ropB], w=rpB)
                k.op("dve", lambda e: e.tensor_tensor(out=rBt, in0=tk[:, :, 0:32], in1=sbb, op=ALU.mult), r=[tokB, csB], w=rpB)
                k.op("dve", lambda e: e.tensor_tensor(out=roped[:, :, 32:64], in0=rA, in1=rBt, op=ALU.add), r=rpB, w=[ropB])
                yield
                k.op("act", lambda e: e.activation(out=vaug[:, t, 0:64], in_=tok[:, 1152:1216], func=AF.Identity), r=[tokB], w=[kvBs[t]])
                k.op("act", lambda e: e.activation(out=sgn[:], in_=tok[:, 1216:1224], func=AF.Sign), r=[tokB], w=[awB])
                k.op("dve", lambda e: e.scalar_tensor_tensor(out=absw[:], in0=tok[:, 1216:1224], scalar=WC, in1=sgn[:],
                                                             op0=ALU.mult, op1=ALU.mult), r=[tokB, awB], w=[awB])
                b0, dB = 4, [psB[4], psB[5]]
                pbf = ps[:, b0:b0 + 2, :].bitcast(BF16)
                for h in range(16):
                    k.op("pe", lambda e: e.transpose(pbf[0:64, h // 8, (h % 8) * 128:(h % 8 + 1) * 128], roped[:, h, :], idb[:]),
                         r=[ropB, cstB], w=dB)
                k.op("act", lambda e: e.activation(out=qTs[t % 2][:].rearrange("p h t -> p (h t)"), in_=pbf[0:64, 0, :], func=AF.Identity),
                     r=dB, w=[qBs[t % 2]])
                k.op("act", lambda e: e.activation(out=qiT[:].rearrange("p h t -> p (h t)"), in_=pbf[0:64, 1, :], func=AF.Identity),
                     r=dB, w=[qiB])
                yield
                b0, dB = 4, [psB[4], psB[5]]
                pbf2 = ps[:, b0, :].bitcast(BF16)
                for h in range(2):
                    k.op("pe", lambda e: e.transpose(pbf2[0:64, h * 128:(h + 1) * 128], roped[:, 16 + h, :], idb[:]),
                         r=[ropB, cstB], w=dB)
                k.op("act", lambda e: e.activation(out=kT[:, t * 128:(t + 1) * 128], in_=pbf2[0:64, 0:128], func=AF.Identity), r=dB, w=[kvBs[t]])
                k.op("act", lambda e: e.activation(out=kiT[:, t * 128:(t + 1) * 128], in_=pbf2[0:64, 128:256], func=AF.Identity), r=dB, w=[kvBs[t]])

            def gdn_pro(t):
                k.op("dve", lambda e: e.tensor_tensor(out=sc4[:, 0, :], in0=tok[:, 1224:1228], in1=gdnp[:, 4:8], op=ALU.add),
                     r=[tokB, gpB], w=[s4B])
                k.op("act", lambda e: e.activation(out=sc4[:, 0, :], in_=sc4[:, 0, :], func=AF.Exp), r=[s4B], w=[s4B])
                k.op("act", lambda e: e.activation(out=sc4[:, 0, :], in_=sc4[:, 0, :], func=AF.Ln, bias=1.0), r=[s4B], w=[s4B])
                k.op("dve", lambda e: e.tensor_tensor(out=sc4[:, 0, :], in0=sc4[:, 0, :], in1=negA[:], op=ALU.mult), r=[s4B, gpB], w=[s4B])
                k.op("act", lambda e: e.activation(out=sc4[:, 1, :], in_=tok[:, 1228:1232], func=AF.Sigmoid), r=[tokB], w=[s4B])
                k.op("act", lambda e: e.activation(out=sz[:], in_=tok[:, 1232:1744], func=AF.Silu), r=[tokB], w=[szB])
                b0, dB = dbl()
                for i, co in enumerate((C_TRI, C_OBD, C_SEL0, C_SEL1)):
                    k.op("pe", lambda e: e.matmul(ps[:, b0, i * 4:i * 4 + 4], lhsT=cst[:, co:co + 128], rhs=sc4[:, 0, :], start=True, stop=True),
                         r=[s4B, cstB], w=dB)
                k.op("dve", lambda e: e.tensor_copy(out=gg[:], in_=ps[:, b0, 0:16]), r=dB, w=[s4B])
                k.op("act", lambda e: e.activation(out=sc4[:, 2, :], in_=gg[:, 0:4], func=AF.Exp), r=[s4B], w=[s4B])
                k.op("dve", lambda e: e.tensor_tensor(out=sc4[:, 8, :], in0=gg[:, 4:8], in1=gg[:, 0:4], op=ALU.subtract), r=[s4B], w=[s4B])
                k.op("act", lambda e: e.activation(out=sc4[:, 3, :], in_=sc4[:, 8, :], func=AF.Exp), r=[s4B], w=[s4B])
                k.op("act", lambda e: e.activation(out=sc4[:, 6, :], in_=gg[:, 8:12], func=AF.Exp), r=[s4B], w=[s4B])
                k.op("act", lambda e: e.activation(out=sc4[:, 7, :], in_=gg[:, 12:16], func=AF.Exp), r=[s4B], w=[s4B])
                k.op("dve", lambda e: e.tensor_tensor(out=sc4[:, 4, :], in0=sc4[:, 1, :], in1=sc4[:, 2, :], op=ALU.mult), r=[s4B], w=[s4B])
                k.op("dve", lambda e: e.tensor_scalar(out=sc4[:, 5, :], in0=sc4[:, 1, :], scalar1=-1.0, scalar2=None, op0=ALU.mult), r=[s4B], w=[s4B])
                yield
                for grp in range(3):
                    b0, dB = dbl()
                    for q4 in range(4):
                        cc = grp * 4 + q4
                        for kc in range(8):
                            k.op("pe", lambda e: e.matmul(ps[:, b0, q4 * 128:(q4 + 1) * 128], lhsT=win[:, kc, NTM + cc * 128:NTM + (cc + 1) * 128],
                                                          rhs=uTt[:, kc, :], start=(kc == 0), stop=(kc == 7)), r=[uB, wB_], w=dB)
                    k.op("act", lambda e: e.activation(out=xc[:, grp * 4:(grp + 1) * 4, 3:131],
                                                       in_=ps[:, b0, :].rearrange("p (a c) -> p a c", a=4), func=AF.Identity),
                         r=dB, w=[xcB])
                    yield
                tmv = tm[:, 0:768].rearrange("p (a c) -> p a c", a=6)
                for half in range(2):
                    yv = hd[2 + half][:]
                    yB = hdB[2 + half]
                    xs_ = lambda j: xc[:, 6 * half:6 * half + 6, j:j + 128]
                    wj_ = lambda j: convw[:, 6 * half:6 * half + 6, j:j + 1].to_broadcast([128, 6, 128])
                    k.op("dve", lambda e: e.tensor_tensor(out=yv, in0=xs_(3), in1=wj_(3), op=ALU.mult), r=[xcB, gpB], w=yB)
                    for j in range(3):
                        k.op("dve", lambda e: e.tensor_tensor(out=tmv, in0=xs_(j), in1=wj_(j), op=ALU.mult), r=[xcB, gpB], w=[tmB])
                        k.op("dve", lambda e: e.tensor_tensor(out=yv, in0=yv, in1=tmv, op=ALU.add), r=yB + [tmB], w=yB)
                    yield
                k.op("pool", lambda e: e.tensor_copy(out=xc[:, :, 0:3], in_=xc[:, :, 128:131]), r=[xcB], w=[xcB])
                for hh in (2, 3):
                    k.op("act", lambda e: e.activation(out=hd[hh][:], in_=hd[hh][:], func=AF.Silu), r=hdB[hh], w=hdB[hh])
                for grp in range(3):
                    b0, dB = dbl()
                    for q4 in range(4):
                        cc = grp * 4 + q4
                        k.op("pe", lambda e: e.transpose(ps[:, b0, q4 * 128:(q4 + 1) * 128], ycv(cc), ident), r=[ycB(cc), cstB], w=dB)
                    k.op("act", lambda e: e.activation(out=tm[:, grp * 512:(grp + 1) * 512], in_=ps[:, b0, :], func=AF.Identity), r=dB, w=[tmB])
                    yield
                for g8 in range(8):
                    k.op("act", lambda e: e.activation(out=junk[:], in_=tm[:, g8 * 128:(g8 + 1) * 128], func=AF.Square,
                                                       accum_out=ss[:, g8:g8 + 1]), r=[tmB], w=[jkB, s4B])
                k.op("dve", lambda e: e.tensor_scalar(out=rs[:], in0=ss[:], scalar1=1e-6, scalar2=None, op0=ALU.add), r=[s4B], w=[s4B])
                k.op("act", lambda e: e.activation(out=rs[:], in_=rs[:], func=AF.Sqrt), r=[s4B], w=[s4B])
                k.op("dve", lambda e: e.reciprocal(out=rs[:], in_=rs[:]), r=[s4B], w=[s4B])
                k.op("dve", lambda e: e.tensor_scalar(out=sc4[:, 9, :], in0=rs[:, 0:4], scalar1=float(128 ** -0.5), scalar2=None, op0=ALU.mult),
                     r=[s4B], w=[s4B])
                k.op("dve", lambda e: e.tensor_tensor(out=sc4[:, 10, :], in0=rs[:, 4:8], in1=sc4[:, 4, :], op=ALU.mult), r=[s4B], w=[s4B])
                k.op("dve", lambda e: e.tensor_tensor(out=sc4[:, 11, :], in0=rs[:, 4:8], in1=sc4[:, 3, :], op=ALU.mult), r=[s4B], w=[s4B])
                k.op("dve", lambda e: e.tensor_tensor(out=sc4[:, 12, :], in0=sc4[:, 9, :], in1=sc4[:, 2, :], op=ALU.mult), r=[s4B], w=[s4B])
                yield

            flags = {"idx": -1, "g1": -1}

            def attn_path(t):
                W = (t + 1) * 128
                if t < 2:
                    flags["idx"] = t
                if t >= 2:
                    nrel = 0
                    for s0 in range(0, W, 512):
                        s1 = min(s0 + 512, W)
                        sw = s1 - s0
                        for h in range(8):
                            ri = nrel % 3
                            rb = 4 + nrel % 4
                            nrel += 1
                            k.op("pe", lambda e: e.matmul(ps[:, rb, 0:sw], lhsT=qiT[:, h, :], rhs=kiT[:, s0:s1], start=True, stop=True),
                                 r=[qiB] + kvBs[s0 // 128:(s1 + 127) // 128], w=[psB[rb]])
                            k.op("act", lambda e: e.activation(out=rel[ri][:, 0:sw], in_=ps[:, rb, 0:sw], func=AF.Relu,
                                                               scale=absw[:, h:h + 1]), r=[psB[rb], awB], w=[relB[ri]])
                            if h == 0:
                                k.op("dve", lambda e: e.tensor_scalar(out=score[:, s0:s1], in0=rel[ri][:, 0:sw], scalar1=sgn[:, 0:1],
                                                                      scalar2=None, op0=ALU.mult), r=[relB[ri], awB], w=[scB])
                            else:
                                k.op("dve", lambda e: e.scalar_tensor_tensor(out=score[:, s0:s1], in0=rel[ri][:, 0:sw], scalar=sgn[:, h:h + 1],
                                                                             in1=score[:, s0:s1], op0=ALU.mult, op1=ALU.add),
                                     r=[relB[ri], awB, scB], w=[scB])
                            if h % 2 == 1:
                                yield
                    flags["idx"] = t
                    k.op("pool", lambda e: e.affine_select(out=score[:, t * 128:W], in_=score[:, t * 128:W], pattern=[[-1, 128]],
                                                           compare_op=ALU.is_ge, fill=NEGFILL, base=0, channel_multiplier=1),
                         r=[scB], w=[scB])
                    k.op("dve", lambda e: e.tensor_reduce(out=bs[:, 0:1], in_=score[:, 0:t * 128], axis=AX.X, op=ALU.min), r=[scB], w=[m8B])
                    k.op("dve", lambda e: e.tensor_reduce(out=bs[:, 1:2], in_=score[:, 0:W], axis=AX.X, op=ALU.max), r=[scB], w=[m8B])
                    k.op("dve", lambda e: e.tensor_tensor(out=bs[:, 2:3], in0=bs[:, 1:2], in1=bs[:, 0:1], op=ALU.subtract), r=[m8B], w=[m8B])
                    k.op("dve", lambda e: e.tensor_scalar(out=wkt[:], in0=cst[:, C_POW:C_POW + 24], scalar1=bs[:, 2:3], scalar2=None, op0=ALU.mult),
                         r=[m8B, cstB], w=[m8B])
                    k.op("dve", lambda e: e.tensor_tensor(out=bs[:, 3:4], in0=bs[:, 0:1], in1=wkt[:, 0:1], op=ALU.add), r=[m8B], w=[m8B])
                    yield
                    for kk in range(NBIS):
                        k.op("dve", lambda e: e.tensor_scalar(out=mbias[:, 0:W], in0=score[:, 0:W], scalar1=bs[:, 3:4], scalar2=0.0,
                                                              op0=ALU.is_ge, op1=ALU.add, accum_out=bs[:, 6:7]), r=[scB, m8B], w=[mbB, m8B])
                        k.op("dve", lambda e: e.scalar_tensor_tensor(out=bs[:, 4:5], in0=bs[:, 6:7], scalar=255.5, in1=wkt[:, kk:kk + 1],
                                                                     op0=ALU.is_ge, op1=ALU.mult), r=[m8B], w=[m8B])
                        k.op("dve", lambda e: e.scalar_tensor_tensor(out=bs[:, 3:4], in0=bs[:, 4:5], scalar=wkt[:, kk + 1:kk + 2], in1=bs[:, 3:4],
                                                                     op0=ALU.subtract, op1=ALU.add), r=[m8B], w=[m8B])
                        yield
                    k.op("dve", lambda e: e.tensor_tensor(out=bs[:, 5:6], in0=bs[:, 3:4], in1=wkt[:, NBIS:NBIS + 1], op=ALU.subtract), r=[m8B], w=[m8B])
                    k.op("dve", lambda e: e.tensor_scalar(out=mbias[:, 0:W], in0=score[:, 0:W], scalar1=bs[:, 5:6], scalar2=MASKV,
                                                          op0=ALU.is_lt, op1=ALU.mult), r=[scB, m8B], w=[mbB])
                else:
                    k.op("pool", lambda e: e.memset(mbias[:, 0:W], 0.0), w=[mbB])
                    k.op("pool", lambda e: e.affine_select(out=mbias[:, t * 128:W], in_=mbias[:, t * 128:W], pattern=[[-1, 128]],
                                                           compare_op=ALU.is_ge, fill=MASKV, base=0, channel_multiplier=1),
                         r=[mbB], w=[mbB])
                yield
                pvB = [psB[6], psB[7]]
                items = [(kb, hf) for kb in range(t + 1) for hf in range(2)]

                def emit_st(i):
                    kb, hf = items[i]
                    sbk = 4 + i % 2
                    pi = i % 3
                    k.op("pe", lambda e: e.matmul(ps[:, sbk, :], lhsT=kT[:, kb * 128:(kb + 1) * 128],
                                                  rhs=qTs[t % 2][:, hf * 4:(hf + 1) * 4, :].rearrange("p h t -> p (h t)"),
                                                  start=True, stop=False), r=[kvBs[kb], qBs[t % 2]], w=[psB[sbk]])
                    k.op("pe", lambda e: e.matmul(ps[:, sbk, :], lhsT=mbias[:, kb * 128:(kb + 1) * 128], rhs=i8b[:],
                                                  start=False, stop=True), r=[mbB, cstB], w=[psB[sbk]])
                    k.op("act", lambda e: e.activation(out=PT[pi][:], in_=ps[:, sbk, :], func=AF.Exp, scale=0.125),
                         r=[psB[sbk]], w=[PTB[pi]])

                def emit_pv(i):
                    kb, hf = items[i]
                    pi = i % 3
                    for hh in range(4):
                        k.op("pe", lambda e: e.matmul(ps[:, 6 + hf, hh * 128:hh * 128 + 65], lhsT=PT[pi][:, hh * 128:(hh + 1) * 128],
                                                      rhs=vaug[:, kb, :], start=(kb == 0 and hh == 0), stop=(kb == t),
                                                      skip_group_check=True), r=[PTB[pi], kvBs[kb]], w=[psB[6 + hf]])

                for i in range(len(items)):
                    emit_st(i)
                    if i >= 1:
                        emit_pv(i - 1)
                    if i % 2 == 1:
                        yield
                emit_pv(len(items) - 1)
                pv = ps[:, 6:8, :].rearrange("p a (h c) -> p (a h) c", c=128)
                k.op("dve", lambda e: e.reciprocal(out=rec[:], in_=pv[:, :, 64]), r=pvB, w=[atB])
                k.op("dve", lambda e: e.tensor_tensor(out=attn[:], in0=pv[:, :, 0:64], in1=rec[:].unsqueeze(2).to_broadcast([128, 8, 64]),
                                                      op=ALU.mult), r=pvB + [atB], w=[atB])
                pbf = ps[:, 4, :].bitcast(BF16)
                for j in range(4):
                    k.op("pe", lambda e: e.transpose(pbf[:, j * 128:(j + 1) * 128], attn[:, 2 * j:2 * j + 2, :].rearrange("p h d -> p (h d)"), idb[:]),
                         r=[atB, cstB], w=[psB[4]])
                k.op("act", lambda e: e.activation(out=catT[:, 0:4, :].rearrange("p a t -> p (a t)"), in_=pbf[:, 0:512], func=AF.Identity),
                     r=[psB[4]], w=[catB])
                yield

            KH, KBG, QS, QD, VB, DG = range(6)

            def gdn_path(t):
                H4 = range(4)
                col = lambda s, h: sc4[:, s, h:h + 1]
                bk = lambda h: [psB[h]]
                for _ in gdn_pro(t):
                    yield
                for h in H4:
                    ksl = tm[:, 512 + h * 128:512 + (h + 1) * 128]
                    qsl = tm[:, h * 128:(h + 1) * 128]
                    vsl = tm[:, 1024 + h * 128:1024 + (h + 1) * 128]
                    for dst, dB_, src, sc_ in ((hd[h][:, KH, :], hdB[h][KH], ksl, rs[:, 4 + h:5 + h]),
                                               (hd[h][:, QS, :], hdB[h][QS], qsl, col(9, h)),
                                               (hd[h][:, QD, :], hdB[h][QD], qsl, col(12, h)),
                                               (hd[h][:, KBG, :], hdB[h][KBG], ksl, col(10, h)),
                                               (kd[h][:], kdB[h], ksl, col(11, h)),
                                               (hd[h][:, VB, :], hdB[h][VB], vsl, col(1, h))):
                        k.op("act", lambda e: e.activation(out=dst, in_=src, func=AF.Identity, scale=sc_), r=[tmB, s4B], w=[dB_])
                    k.op("dve", lambda e: e.tensor_scalar(out=hd[h][:, DG, :], in0=ident, scalar1=gg[:, h:h + 1], scalar2=None, op0=ALU.mult),
                         r=[cstB, s4B], w=[hdB[h][DG]])
                    if h % 2 == 1:
                        yield
                flags["g1"] = t
                for h in H4:
                    for i, src in enumerate((KH, QS, QD)):
                        k.op("pe", lambda e: e.transpose(ps[:, h, i * 128:(i + 1) * 128], hd[h][:, src, :], ident), r=[hdB[h][src], cstB], w=bk(h))
                yield
                for h in H4:
                    k.op("act", lambda e: e.activation(out=T3[h][:].rearrange("p a t -> p (a t)"), in_=ps[:, h, 0:384], func=AF.Identity),
                         r=bk(h), w=[T3B[h]])
                yield
                for h in H4:
                    k.op("pe", lambda e: e.matmul(ps[:, h, 0:128], lhsT=T3[h][:, 0, :], rhs=T3[h][:, 0, :], start=True, stop=True), r=[T3B[h]], w=bk(h))
                    k.op("pe", lambda e: e.matmul(ps[:, h, 128:256], lhsT=T3[h][:, 0, :], rhs=T3[h][:, 1, :], start=True, stop=True), r=[T3B[h]], w=bk(h))
                    k.op("pe", lambda e: e.matmul(ps[:, h, 256:384], lhsT=cst[:, C_ONES:C_ONES + 128], rhs=hd[h][:, DG, :], start=True, stop=True),
                         r=[hdB[h][DG], cstB], w=bk(h))
                yield
                for h in H4:
                    tt1 = AN[h][1][:, 0, :]
                    tt2 = AN[h][1][:, 1, :]
                    k.op("dve", lambda e: e.scalar_tensor_tensor(out=tt1, in0=ps[:, h, 256:384], scalar=gg[:, h:h + 1],
                                                                 in1=cst[:, C_MB1:C_MB1 + 128], op0=ALU.subtract, op1=ALU.subtract),
                         r=bk(h) + [s4B, cstB], w=[ANB[h][1]])
                    k.op("dve", lambda e: e.scalar_tensor_tensor(out=tt2, in0=ps[:, h, 256:384], scalar=gg[:, h:h + 1],
                                                                 in1=cst[:, C_MB2:C_MB2 + 128], op0=ALU.subtract, op1=ALU.add),
                         r=bk(h) + [s4B, cstB], w=[ANB[h][1]])
                yield
                for h in H4:
                    k.op("act", lambda e: e.activation(out=AN[h][1][:, 0, :], in_=AN[h][1][:, 0, :], func=AF.Exp, scale=-1.0),
                         r=[ANB[h][1]], w=[ANB[h][1]])
                    k.op("act", lambda e: e.activation(out=AN[h][1][:, 1, :], in_=AN[h][1][:, 1, :], func=AF.Exp), r=[ANB[h][1]], w=[ANB[h][1]])
                yield
                for h in H4:
                    k.op("dve", lambda e: e.scalar_tensor_tensor(out=AN[h][0][:, 1, :], in0=ps[:, h, 0:128], scalar=col(5, h), in1=AN[h][1][:, 0, :],
                                                                 op0=ALU.mult, op1=ALU.mult), r=bk(h) + [s4B, ANB[h][1]], w=[ANB[h][0]])
                    k.op("dve", lambda e: e.tensor_tensor(out=aqk[h][:], in0=ps[:, h, 128:256], in1=AN[h][1][:, 1, :], op=ALU.mult),
                         r=bk(h) + [ANB[h][1]], w=[aqB[h]])
                yield
                for h in H4:
                    k.op("pe", lambda e: e.transpose(ps[:, h, 0:128], AN[h][0][:, 1, :], ident), r=[ANB[h][0], cstB], w=bk(h))
                yield
                for h in H4:
                    k.op("act", lambda e: e.activation(out=AN[h][0][:, 0, :], in_=ps[:, h, 0:128], func=AF.Identity), r=bk(h), w=[ANB[h][0]])
                yield
                for h in H4:
                    k.op("dve", lambda e: e.tensor_tensor(out=Mm[h][0][:], in0=AN[h][0][:, 0, :], in1=ident, op=ALU.add),
                         r=[ANB[h][0], cstB], w=[MB[h][0]])

                def sq(h, cur):
                    k.op("pe", lambda e: e.matmul(ps[:, h, 0:128], lhsT=AN[h][cur][:, 1, :], rhs=AN[h][cur][:, 0, :], start=True, stop=True),
                         r=[ANB[h][cur]], w=bk(h))
                    k.op("pe", lambda e: e.matmul(ps[:, h, 128:256], lhsT=AN[h][cur][:, 0, :], rhs=AN[h][cur][:, 1, :], start=True, stop=True),
                         r=[ANB[h][cur]], w=bk(h))

                def ev(h, nxt):
                    k.op("act", lambda e: e.activation(out=AN[h][nxt][:].rearrange("p a c -> p (a c)"), in_=ps[:, h, 0:256], func=AF.Identity),
                         r=bk(h), w=[ANB[h][nxt]])

                def pr(h, an, mc):
                    k.op("pe", lambda e: e.matmul(ps[:, h, 256:384], lhsT=AN[h][an][:, 1, :], rhs=Mm[h][mc][:], start=True, stop=True),
                         r=[ANB[h][an], MB[h][mc]], w=bk(h))

                def ad(h, mc):
                    k.op("dve", lambda e: e.tensor_tensor(out=Mm[h][1 - mc][:], in0=ps[:, h, 256:384], in1=Mm[h][mc][:], op=ALU.add),
                         r=bk(h) + [MB[h][mc]], w=[MB[h][1 - mc]])

                for h in H4:
                    sq(h, 0)
                yield
                for h in H4:
                    ev(h, 1)
                yield
                for it in range(1, 6):
                    an = it % 2
                    for h in H4:
                        if it < 5:
                            sq(h, an)
                        pr(h, an, (it - 1) % 2)
                    yield
                    for h in H4:
                        if it < 5:
                            ev(h, 1 - an)
                        ad(h, (it - 1) % 2)
                    yield
                mf = 5 % 2
                for h in H4:
                    k.op("pe", lambda e: e.matmul(ps[:, h, 0:128], lhsT=Mm[h][mf][:], rhs=hd[h][:, VB, :], start=True, stop=True),
                         r=[MB[h][mf], hdB[h][VB]], w=bk(h))
                    k.op("pe", lambda e: e.matmul(ps[:, h, 128:256], lhsT=hd[h][:, KBG, :], rhs=Mm[h][mf][:], start=True, stop=True),
                         r=[MB[h][mf], hdB[h][KBG]], w=bk(h))
                yield
                for h in H4:
                    k.op("act", lambda e: e.activation(out=U[h][:], in_=ps[:, h, 0:128], func=AF.Identity), r=bk(h), w=[UB[h]])
                    k.op("act", lambda e: e.activation(out=WT[h][:], in_=ps[:, h, 128:256], func=AF.Identity), r=bk(h), w=[WTB[h]])
                yield
                for c in range(2):
                    r0, r1 = 64 * c, 64 * c + 64
                    for h in H4:
                        k.op("pe", lambda e: e.matmul(ps[:, h, 0:128], lhsT=WT[h][:], rhs=Sb[:, h, :], start=True, stop=True),
                             r=[WTB[h], SbB[h]], w=bk(h))
                    yield
                    for h in H4:
                        k.op("dve", lambda e: e.tensor_tensor(out=dl[h][r0:r1, :], in0=U[h][r0:r1, :], in1=ps[r0:r1, h, 0:128], op=ALU.subtract),
                             r=bk(h) + [UB[h]], w=[dlB[h]])
                    yield
                    for h in H4:
                        k.op("pe", lambda e: e.matmul(ps[:, h, 256:384], lhsT=kd[h][r0:r1, :], rhs=dl[h][r0:r1, :], start=True, stop=True),
                             r=[kdB[h], dlB[h]], w=bk(h))
                        k.op("pe", lambda e: e.matmul(ps[:, h, 128:256], lhsT=T3[h][:, 2, :], rhs=Sb[:, h, :], start=True, stop=False),
                             r=[T3B[h], SbB[h]], w=bk(h))
                        k.op("pe", lambda e: e.matmul(ps[:, h, 128:256], lhsT=aqk[h][r0:r1, :], rhs=dl[h][r0:r1, :], start=False, stop=True),
                             r=[aqB[h], dlB[h]], w=bk(h))
                    yield
                    for h in H4:
                        k.op("dve", lambda e: e.scalar_tensor_tensor(out=Sb[:, h, :], in0=S[:, h, :], scalar=sc4[:, 6 + c, h:h + 1],
                                                                     in1=ps[:, h, 256:384], op0=ALU.mult, op1=ALU.add),
                             r=bk(h) + [s4B, SB_[h]], w=[SbB[h]])
                        k.op("dve", lambda e: e.scalar_tensor_tensor(out=S[:, h, :], in0=S[:, h, :], scalar=sc4[:, 6 + c, h:h + 1],
                                                                     in1=ps[:, h, 256:384], op0=ALU.mult, op1=ALU.add),
                             r=bk(h) + [s4B, SB_[h]], w=[SB_[h]])
                        k.op("act", lambda e: e.activation(out=otm[r0:r1, h, :], in_=ps[r0:r1, h, 128:256], func=AF.Identity), r=bk(h), w=[otB[h]])
                    yield
                for h in H4:
                    k.op("act", lambda e: e.activation(out=junk[:], in_=otm[:, h, :], func=AF.Square, accum_out=ss[:, h:h + 1]),
                         r=[otB[h]], w=[jkB, s4B])
                k.op("dve", lambda e: e.tensor_scalar(out=rs[:, 0:4], in0=ss[:, 0:4], scalar1=float(1.0 / 128), scalar2=1e-6,
                                                      op0=ALU.mult, op1=ALU.add), r=[s4B], w=[s4B])
                k.op("act", lambda e: e.activation(out=rs[:, 0:4], in_=rs[:, 0:4], func=AF.Sqrt), r=[s4B], w=[s4B])
                k.op("dve", lambda e: e.reciprocal(out=rs[:, 0:4], in_=rs[:, 0:4]), r=[s4B], w=[s4B])
                yield
                for h in H4:
                    k.op("dve", lambda e: e.scalar_tensor_tensor(out=otm[:, h, :], in0=otm[:, h, :], scalar=rs[:, h:h + 1], in1=dng[:],
                                                                 op0=ALU.mult, op1=ALU.mult), r=[otB[h], s4B, gpB], w=[otB[h]])
                k.op("dve", lambda e: e.tensor_tensor(out=dn[:], in0=otm[:].rearrange("p h d -> p (h d)"), in1=sz[:], op=ALU.mult),
                     r=otB + [szB], w=[dnB])
                yield
                pbf = ps[:, 0, :].bitcast(BF16)
                for j in range(4):
                    k.op("pe", lambda e: e.transpose(pbf[:, j * 128:(j + 1) * 128], dn[:, j * 128:(j + 1) * 128], idb[:]), r=[dnB, cstB], w=[psB[0]])
                k.op("act", lambda e: e.activation(out=catT[:, 4:8, :].rearrange("p a t -> p (a t)"), in_=pbf[:, 0:512], func=AF.Identity),
                     r=[psB[0]], w=[catB])
                yield

            def epilogue(t):
                xa, xB1 = xts[t % 2], xtB[t % 2]
                b0, dB = 4, [psB[4], psB[5]]
                for hf in range(2):
                    for fc in range(8):
                        k.op("pe", lambda e: e.matmul(ps[:, b0 + hf, :], lhsT=catT[:, fc, :], rhs=wout[:, fc, hf * 512:(hf + 1) * 512],
                                                      start=(fc == 0), stop=(fc == 7)), r=[catB, wB_], w=dB)
                k.op("dve", lambda e: e.scalar_tensor_tensor(out=xa[:].rearrange("p (a c) -> p a c", a=2),
                                                             in0=xa[:].rearrange("p (a c) -> p a c", a=2), scalar=ALPHA,
                                                             in1=ps[:, b0:b0 + 2, :], op0=ALU.mult, op1=ALU.add), r=dB + [xB1], w=[xB1])
                yield
                layer_norm_tile(xa[:], xB1, lnbc, lnB, st6, mv, tmpc, smB)
                k.dma(xs_d[t * 128:(t + 1) * 128, :], xa[:], r=[xB1], w=[xsB[t]])
                yield

            for _ in p1(0):
                pass
            pend_epi = None
            for t in range(NT):
                ga = attn_path(t)
                gg_ = gdn_path(t)
                gp = p1(t + 1) if t + 1 < NT else None
                npf = 0
                alive = [ga, gg_]
                if pend_epi is not None:
                    alive.append(pend_epi)
                    pend_epi = None
                while alive:
                    for g in list(alive):
                        try:
                            next(g)
                        except StopIteration:
                            alive.remove(g)
                    if gp is not None and not NOINTER and flags["idx"] == t and flags["g1"] == t and npf < PFMAX:
                        npf += 1
                        try:
                            next(gp)
                        except StopIteration:
                            gp = None
                if gp is not None:
                    for _ in gp:
                        pass
                pend_epi = epilogue(t)
            for _ in pend_epi:
                pass

        for b in range(2):
            ffn_phase(b, 0)
            if stage >= 2:
                mixer_phase(b)
            if stage >= 3:
                ffn_phase(b, 1)
        if stage < 3:
            with nc.sbuf_tensor("dbgt", [128, D], F32) as dbg:
                dB_ = Bf()
                for t in range(NT):
                    k.dma(dbg[:], xs_d[t * 128:(t + 1) * 128, :], r=[xsB[t]], w=[dB_])
                    k.dma(out_d[1, t * 128:(t + 1) * 128, :], dbg[:], r=[dB_], w=[outB[NT + t]])
        k.barrier()
    return nc


_PERM = None


def _perm():
    a_q = np.arange(0, 512); a_k = np.arange(512, 576); a_v = np.arange(576, 640)
    i_q = np.arange(640, 1152); i_k = np.arange(1152, 1216); i_w = np.arange(1216, 1224)
    b_q = np.arange(1224, 1736); b_k = np.arange(1736, 2248); b_v = np.arange(2248, 2760)
    b_z = np.arange(2760, 3272); b_a = np.arange(3272, 3276); b_b = np.arange(3276, 3280)
    return np.concatenate([a_q, i_q, a_k, i_k, a_v, i_w, b_a, b_b, b_z, b_q, b_k, b_v])


def make_in_maps(inp):
    f = lambda a: np.ascontiguousarray(np.asarray(a))
    shared = {}
    shared["wada"] = f(np.asarray(inp["w_ada"])[0].reshape(8, 128, 72, 128).transpose(2, 1, 0, 3))
    shared["badaT"] = f(np.asarray(inp["b_ada"])[0].reshape(72, 128).T)
    ffn = ((1, inp["ffn1_w1"], inp["ffn1_w3"], inp["ffn1_w2"]), (2, inp["ffn2_w1"], inp["ffn2_w3"], inp["ffn2_w2"]))
    for i, a1, a3, a2 in ffn:
        for nm, w in (("w1", a1), ("w3", a3)):
            w = np.asarray(w)[0]
            shared["%s_%d" % (nm, i)] = f(w.reshape(8, 128, NCH, 128).transpose(2, 1, 0, 3).reshape(NCH, 128, 1024))
        shared["w2_%d" % i] = f(np.asarray(a2)[0].reshape(NCH, 128, 1024))
    win = np.asarray(inp["w_in"])[0][:, _perm()]
    shared["win"] = f(win.reshape(8, 128, NCOL).transpose(1, 0, 2))
    shared["wout"] = f(np.asarray(inp["w_out"])[0].reshape(8, 128, D).transpose(1, 0, 2))
    shared["lnp"] = f(np.stack([np.asarray(inp[n])[0] for n in ("ln1_g", "ln1_b", "ln2_g", "ln2_b", "ln3_g", "ln3_b")]))
    shared["convT"] = f(np.asarray(inp["conv_w"])[0].reshape(4, 12, 128).transpose(2, 1, 0))
    shared["gdnp"] = f(np.concatenate([np.asarray(inp["a_log"])[0], np.asarray(inp["dt_bias"])[0]]))
    shared["dng"] = f(np.asarray(inp["dn_norm_g"])[0])
    shared["consts"] = make_consts()
    x = np.asarray(inp["x"]); c = np.asarray(inp["c"]); pos = np.asarray(inp["positions"])
    maps = []
    for core in range(8):
        m = dict(shared)
        b0 = 2 * core
        m["x"] = f(x[b0:b0 + 2])
        m["cT"] = f(c[b0:b0 + 2].reshape(2, 8, 128).transpose(2, 1, 0))
        m["pos"] = f(pos[b0:b0 + 2].reshape(2, NT, 128).transpose(2, 0, 1).astype(np.int32))
        maps.append(m)
    return maps


def kernel(**inputs):
    nc = build(STAGE)
    maps = make_in_maps(inputs)
    res = run_bass_kernel_spmd(nc, maps, core_ids=list(range(8)))
    out = np.concatenate([np.asarray(r["out"]) for r in res.results], axis=0)
    return out.astype(np.float32)
```
